# Optimizing a Trainium2 kernel written in Bass

```python
import math
import jax
import jax.numpy as jnp
from jax import lax
import numpy as np

D_MODEL = 2048
BATCH = 16
SEQ = 2048
DEPTH = 4

CTX_LEN = 256
GRID_W = 64
N_MIXERS = 3
RMS_EPS = 1e-6
N_MOD = 6

ATT_HEADS = 32
ATT_KV_HEADS = 4
ATT_GQA = ATT_HEADS // ATT_KV_HEADS
ATT_HEAD_DIM = 64
ATT_WINDOW = 128
ATT_BLOCK = 128
ATT_Q_DIM = ATT_HEADS * ATT_HEAD_DIM
ATT_KV_DIM = ATT_KV_HEADS * ATT_HEAD_DIM
ATT_IN = 2 * ATT_KV_DIM + ATT_Q_DIM
ROPE_THETA = 10000.0
ROPE_PAIRS = ATT_HEAD_DIM // 4
NEG_INF = -1e30

SSD_D_INNER = 2 * D_MODEL
SSD_HEAD_DIM = 64
SSD_HEADS = SSD_D_INNER // SSD_HEAD_DIM
SSD_GROUPS = 8
SSD_STATE = 128
SSD_GN = SSD_GROUPS * SSD_STATE
SSD_CONV_W = 3
SSD_CHUNK = 64
SSD_CONV_DIM = SSD_D_INNER + 2 * SSD_GN
SSD_IN = SSD_CONV_DIM + 2 * SSD_HEADS + SSD_D_INNER

HG_EXPAND = 128
HG_HEADS = D_MODEL // HG_EXPAND
HG_KEY = HG_HEADS * HG_EXPAND
HG_HEAD_V = D_MODEL // HG_HEADS
HG_VAL = HG_HEADS * HG_HEAD_V
HG_CHUNK = 64
HG_IN = 2 * HG_KEY + HG_VAL + HG_KEY + HG_VAL

FFN_DIM = -(-8 * D_MODEL // (3 * 256)) * 256

kernel_name = 'hybrid_interleaved_dit_trunk'


def _rms(t):
    t32 = t.astype(jnp.float32)
    return (t32 * lax.rsqrt(jnp.mean(t32 * t32, axis=-1, keepdims=True) + RMS_EPS)).astype(t.dtype)


def rms_norm(t, g):
    return _rms(t) * g


def modulate(t, g, shift, scale):
    return rms_norm(t, g) * (1.0 + scale) + shift


def _flip(t):
    return jnp.flip(t, axis=1)


def swiglu(t, w_in, w_out):
    gu = t @ w_in
    return (jax.nn.silu(gu[..., :FFN_DIM]) * gu[..., FFN_DIM:]) @ w_out


def axial_rope_tables(n_tokens):
    rows = n_tokens // GRID_W
    row = jnp.broadcast_to(jnp.arange(rows)[:, None], (rows, GRID_W)).reshape(-1).astype(jnp.float32)
    col = jnp.broadcast_to(jnp.arange(GRID_W)[None, :], (rows, GRID_W)).reshape(-1).astype(jnp.float32)
    inv_freq = ROPE_THETA ** (-jnp.arange(ROPE_PAIRS, dtype=jnp.float32) / ROPE_PAIRS)
    ang = jnp.stack([row[:, None] * inv_freq, col[:, None] * inv_freq], axis=1)
    return jnp.cos(ang), jnp.sin(ang)


def apply_axial_rope(t, cos, sin):
    shp = t.shape
    tt = t.reshape(shp[:-1] + (2, 2, ROPE_PAIRS))
    bshape = (shp[1],) + (1,) * (len(shp) - 3) + (2, ROPE_PAIRS)
    cs = cos.reshape(bshape).astype(t.dtype)
    sn = sin.reshape(bshape).astype(t.dtype)
    t1, t2 = tt[..., 0, :], tt[..., 1, :]
    return jnp.stack([t1 * cs - t2 * sn, t2 * cs + t1 * sn], axis=-2).reshape(shp)


def sink_softmax(scores, sink):
    sink_col = jnp.broadcast_to(sink[:, :, None, None], scores.shape[:-1] + (1,))
    return jax.nn.softmax(jnp.concatenate([sink_col, scores], axis=-1), axis=-1)[..., 1:]


def key_band(t, nb):
    b = t.shape[0]
    tb = t.reshape((b, nb, ATT_BLOCK) + t.shape[2:])
    tp = jnp.pad(tb, ((0, 0), (1, 1), (0, 0), (0, 0), (0, 0)))
    return jnp.concatenate([tp[:, :-2], tp[:, 1:-1], tp[:, 2:]], axis=2)


def banded_window_attention(q, k, v, k_c, v_c, sink):
    b, s = q.shape[:2]
    nb = s // ATT_BLOCK
    scale = ATT_HEAD_DIM ** -0.5
    qb = q.reshape(b, nb, ATT_BLOCK, ATT_KV_HEADS, ATT_GQA, ATT_HEAD_DIM)
    kb, vb = key_band(k, nb), key_band(v, nb)
    s_loc = jnp.einsum('bnqkgd,bnjkd->bnkgqj', qb, kb).astype(jnp.float32) * scale
    s_ctx = jnp.einsum('bnqkgd,bmkd->bnkgqm', qb, k_c).astype(jnp.float32) * scale
    r = jnp.arange(ATT_BLOCK)
    cidx = jnp.arange(3 * ATT_BLOCK)
    blk = jnp.arange(nb)
    in_window = jnp.abs(r[:, None] + ATT_BLOCK - cidx[None, :]) <= ATT_WINDOW
    kpos = (blk[:, None] - 1) * ATT_BLOCK + cidx[None, :]
    valid = in_window[None] & ((kpos >= 0) & (kpos < s))[:, None, :]
    s_loc = jnp.where(valid[None, :, None, None], s_loc, NEG_INF)
    p = sink_softmax(jnp.concatenate([s_loc, s_ctx], axis=-1), sink).astype(v.dtype)
    n_loc = 3 * ATT_BLOCK
    o = (jnp.einsum('bnkgqj,bnjkd->bnqkgd', p[..., :n_loc], vb)
         + jnp.einsum('bnkgqm,bmkd->bnqkgd', p[..., n_loc:], v_c))
    return o.reshape(b, s, ATT_Q_DIM)


def context_attention(q_c, k_c, v_c, sink):
    b, l = q_c.shape[:2]
    scores = jnp.einsum('blkgd,bmkd->bkglm', q_c, k_c).astype(jnp.float32) * (ATT_HEAD_DIM ** -0.5)
    p = sink_softmax(scores, sink).astype(v_c.dtype)
    return jnp.einsum('bkglm,bmkd->blkgd', p, v_c).reshape(b, l, ATT_Q_DIM)


def attention_mixer(u, u_c, w_in, w_out, sink, rope_cos, rope_sin, need_ctx):
    b, s, _ = u.shape
    lc = u_c.shape[1]
    kvh, dh = ATT_KV_HEADS, ATT_HEAD_DIM
    proj = u @ w_in
    k = apply_axial_rope(proj[..., :ATT_KV_DIM].reshape(b, s, kvh, dh), rope_cos, rope_sin)
    v = proj[..., ATT_KV_DIM:2 * ATT_KV_DIM].reshape(b, s, kvh, dh)
    q = apply_axial_rope(proj[..., 2 * ATT_KV_DIM:].reshape(b, s, kvh, ATT_GQA, dh), rope_cos, rope_sin)
    proj_c = u_c @ (w_in if need_ctx else w_in[:, :2 * ATT_KV_DIM])
    k_c = proj_c[..., :ATT_KV_DIM].reshape(b, lc, kvh, dh)
    v_c = proj_c[..., ATT_KV_DIM:2 * ATT_KV_DIM].reshape(b, lc, kvh, dh)
    sink = sink.astype(jnp.float32).reshape(kvh, ATT_GQA)
    y = banded_window_attention(q, k, v, k_c, v_c, sink) @ w_out
    if not need_ctx:
        return y, None
    q_c = proj_c[..., 2 * ATT_KV_DIM:].reshape(b, lc, kvh, ATT_GQA, dh)
    return y, context_attention(q_c, k_c, v_c, sink) @ w_out


def centred_depthwise_conv(t, w, bias):
    pad = SSD_CONV_W // 2
    l = t.shape[1]
    tp = jnp.pad(t, ((0, 0), (pad, pad), (0, 0)))
    out = bias + tp[:, 0:l] * w[0]
    for tap in range(1, SSD_CONV_W):
        out = out + tp[:, tap:tap + l] * w[tap]
    return out


def ssd_chunk_scan(x, dt, a, bm, cm, s0, want_y):
    b, l, nh, p = x.shape
    g, n = bm.shape[-2:]
    e = nh // g
    nc = l // SSD_CHUNK
    q = SSD_CHUNK
    acum = jnp.cumsum((dt * a).reshape(b, nc, q, g, e), axis=2).transpose(0, 1, 3, 4, 2)
    xdt = (x * dt[..., None]).reshape(b, nc, q, g, e, p)
    bc = bm.reshape(b, nc, q, g, n)
    cc = cm.reshape(b, nc, q, g, n)
    decay_end = jnp.exp(acum[..., -1:] - acum)
    states = jnp.einsum('bcqgn,bcgeq,bcqgep->bcgepn', bc, decay_end, xdt)
    chunk_decay = jnp.exp(acum[..., -1])

    def step(s, inp):
        st, dec = inp
        return dec[..., None, None] * s + st, s

    s_fin, s_start = lax.scan(step, s0, (jnp.moveaxis(states, 1, 0), jnp.moveaxis(chunk_decay, 1, 0)))
    if not want_y:
        return None, s_fin
    tril = jnp.tril(jnp.ones((q, q), dtype=bool))
    seg = jnp.where(tril, acum[..., :, None] - acum[..., None, :], -jnp.inf)
    cb = jnp.einsum('bcign,bcjgn->bcgij', cc, bc)
    y_diag = jnp.einsum('bcgij,bcgeij,bcjgep->bcigep', cb, jnp.exp(seg), xdt)
    y_off = jnp.einsum('bcign,cbgepn,bcgei->bcigep', cc, s_start, jnp.exp(acum))
    return (y_diag + y_off).reshape(b, l, nh, p).astype(x.dtype), s_fin


def gated_group_rms_norm(y, z, w):
    b, l = y.shape[:2]
    t = y.reshape(b, l, SSD_D_INNER) * jax.nn.silu(z)
    t = _rms(t.reshape(b, l, SSD_GROUPS, SSD_D_INNER // SSD_GROUPS)).reshape(b, l, SSD_D_INNER)
    return t * w


def ssd_mixer(u, u_c, w_in, conv_w, conv_b, dt_bias, a_log, d_skip, norm_w, w_out, need_ctx):
    b = u.shape[0]
    n_state = SSD_CONV_DIM + 2 * SSD_HEADS
    a = -jnp.exp(a_log.astype(jnp.float32))

    def split_inputs(t, w):
        bt, lt, _ = t.shape
        proj = t @ w
        xbc = jax.nn.silu(centred_depthwise_conv(proj[..., :SSD_CONV_DIM], conv_w, conv_b))
        xs = xbc[..., :SSD_D_INNER].reshape(bt, lt, SSD_HEADS, SSD_HEAD_DIM)
        bm = xbc[..., SSD_D_INNER:SSD_D_INNER + SSD_GN].reshape(bt, lt, SSD_GROUPS, SSD_STATE)
        cm = xbc[..., SSD_D_INNER + SSD_GN:].reshape(bt, lt, SSD_GROUPS, SSD_STATE)
        dt = jax.nn.softplus(proj[..., SSD_CONV_DIM:n_state].astype(jnp.float32).reshape(bt, lt, 2, SSD_HEADS)
                             + dt_bias.astype(jnp.float32))
        return proj[..., n_state:], xs, bm, cm, dt

    def bidir(xs, bm, cm, dt, s_f, s_b, want_y):
        y_f, fin_f = ssd_chunk_scan(xs, dt[:, :, 0], a[0], bm, cm, s_f, want_y)
        y_b, fin_b = ssd_chunk_scan(_flip(xs), _flip(dt[:, :, 1]), a[1], _flip(bm), _flip(cm), s_b, want_y)
        y = (y_f + _flip(y_b) + xs * d_skip[:, None]) if want_y else None
        return y, fin_f, fin_b

    zero = jnp.zeros((b, SSD_GROUPS, SSD_HEADS // SSD_GROUPS, SSD_HEAD_DIM, SSD_STATE), jnp.float32)
    z_c, x_c, b_c, c_c, dt_c = split_inputs(u_c, w_in if need_ctx else w_in[:, :n_state])
    y_c, s_f, s_b = bidir(x_c, b_c, c_c, dt_c, zero, zero, need_ctx)
    z, xs, bm, cm, dt = split_inputs(u, w_in)
    y, _, _ = bidir(xs, bm, cm, dt, s_f, s_b, True)
    y = gated_group_rms_norm(y, z, norm_w) @ w_out
    if not need_ctx:
        return y, None
    return y, gated_group_rms_norm(y_c, z_c, norm_w) @ w_out


def hgrn2_lower_bounds(lb_raw):
    p = jax.nn.softmax(lb_raw.astype(jnp.float32), axis=1)
    return jnp.cumsum(p, axis=1) - p[:, :1]


def hgrn2_chunk_scan(k, v, log_g, s0, q=None):
    b, l, nh, dk = k.shape
    nc = l // HG_CHUNK

    def chunks(t):
        return t.reshape(b, nc, HG_CHUNK, nh, t.shape[-1]).transpose(1, 0, 3, 2, 4)

    tril = jnp.tril(jnp.ones((HG_CHUNK, HG_CHUNK), dtype=bool))

    def step(state, inp):
        kc, vc, gc = inp[0], inp[1], inp[2]
        gcum = jnp.cumsum(gc, axis=2)
        g_last = gcum[:, :, -1:]
        new_state = (jnp.exp(g_last[:, :, 0])[..., None] * state
                     + jnp.einsum('bhsd,bhsv->bhdv', kc * jnp.exp(g_last - gcum), vc))
        if q is None:
            return new_state, None
        qc = inp[3]
        rel = jnp.where(tril[:, :, None], gcum[:, :, :, None] - gcum[:, :, None], -jnp.inf)
        scores = jnp.einsum('bhtd,bhsd,bhtsd->bhts', qc, kc, jnp.exp(rel))
        o = (jnp.einsum('bhts,bhsv->bhtv', scores, vc)
             + jnp.einsum('bhtd,bhdv->bhtv', qc * jnp.exp(gcum), state))
        return new_state, o

    xs = (chunks(k), chunks(v), chunks(log_g)) + (() if q is None else (chunks(q),))
    s_fin, o = lax.scan(step, s0, xs)
    if q is None:
        return None, s_fin
    return o.transpose(1, 0, 3, 2, 4).reshape(b, l, nh, v.shape[-1]), s_fin


def hgrn2_mixer(u, u_c, w_in, lb, norm_w, w_out, need_ctx):
    b = u.shape[0]
    n_state = 2 * HG_KEY + HG_VAL

    def split_inputs(t, w):
        bt, lt, _ = t.shape
        proj = t @ w
        f = proj[..., :2 * HG_KEY].astype(jnp.float32).reshape(bt, lt, 2, HG_KEY)
        g = lb + (1.0 - lb) * jax.nn.sigmoid(f)
        log_g = jnp.log(g).reshape(bt, lt, 2, HG_HEADS, HG_EXPAND)
        k = (1.0 - g).astype(t.dtype).reshape(bt, lt, 2, HG_HEADS, HG_EXPAND)
        v = proj[..., 2 * HG_KEY:n_state].reshape(bt, lt, HG_HEADS, HG_HEAD_V)
        return proj[..., n_state:], k, log_g, v

    def query(rest):
        bt, lt = rest.shape[:2]
        return jax.nn.silu(rest[..., :HG_KEY]).reshape(bt, lt, HG_HEADS, HG_EXPAND)

    def bidir(q, k, log_g, v, s_f, s_b):
        o_f, fin_f = hgrn2_chunk_scan(k[:, :, 0], v, log_g[:, :, 0], s_f, q)
        o_b, fin_b = hgrn2_chunk_scan(_flip(k[:, :, 1]), _flip(v), _flip(log_g[:, :, 1]), s_b,
                                      None if q is None else _flip(q))
        o = None if q is None else o_f + _flip(o_b)
        return o, fin_f, fin_b

    def readout(o, rest):
        bt, lt = o.shape[:2]
        o = (_rms(o.astype(u.dtype)) * norm_w.reshape(HG_HEADS, HG_HEAD_V)).reshape(bt, lt, HG_VAL)
        return (o * jax.nn.silu(rest[..., HG_KEY:])) @ w_out

    zero = jnp.zeros((b, HG_HEADS, HG_EXPAND, HG_HEAD_V), jnp.float32)
    rest_c, k_c, lg_c, v_c = split_inputs(u_c, w_in if need_ctx else w_in[:, :n_state])
    o_c, s_f, s_b = bidir(query(rest_c) if need_ctx else None, k_c, lg_c, v_c, zero, zero)
    rest, k, lg, v = split_inputs(u, w_in)
    o, _, _ = bidir(query(rest), k, lg, v, s_f, s_b)
    y = readout(o, rest)
    if not need_ctx:
        return y, None
    return y, readout(o_c, rest_c)


def setup_inputs(seed: int = 0) -> dict:
    key = jax.random.key(seed)
    ks = iter(jax.random.split(key, 32))

    def nrm(shape, scale):
        return jax.random.normal(next(ks), shape, jnp.float32) * scale

    n_a, n_b, n_c = [len(range(m, DEPTH, N_MIXERS)) for m in range(N_MIXERS)]
    d = D_MODEL
    x = nrm((BATCH, SEQ, d), 1.0)
    c = nrm((BATCH, d), 1.0)
    ctx = nrm((BATCH, CTX_LEN, d), 1.0)
    c_ctx = nrm((d,), 1.0)
    ada_w = nrm((DEPTH, d, N_MOD * d), 0.5 * d ** -0.5)
    ada_b = nrm((DEPTH, N_MOD * d), 0.02)
    norm_g = 1.0 + nrm((DEPTH, 4, d), 0.02)
    ffn_w_in = nrm((DEPTH, d, 2 * FFN_DIM), d ** -0.5)
    ffn_w_out = nrm((DEPTH, FFN_DIM, d), FFN_DIM ** -0.5)
    attn_w_in = nrm((n_a, d, ATT_IN), d ** -0.5)
    attn_w_out = nrm((n_a, ATT_Q_DIM, d), ATT_Q_DIM ** -0.5)
    attn_sink = nrm((n_a, ATT_HEADS), 0.5)
    ssd_w_in = nrm((n_b, d, SSD_IN), d ** -0.5)
    ssd_conv_w = nrm((n_b, SSD_CONV_W, SSD_CONV_DIM), SSD_CONV_W ** -0.5)
    ssd_conv_b = nrm((n_b, SSD_CONV_DIM), 0.02)
    dt0 = jnp.exp(jax.random.uniform(next(ks), (n_b, 2, SSD_HEADS), jnp.float32,
                                     minval=math.log(1e-3), maxval=math.log(1e-1)))
    ssd_dt_bias = dt0 + jnp.log(-jnp.expm1(-dt0))
    ssd_a_log = jnp.log(jax.random.uniform(next(ks), (n_b, 2, SSD_HEADS), jnp.float32, minval=1.0, maxval=16.0))
    ssd_d = 1.0 + nrm((n_b, SSD_HEADS), 0.1)
    ssd_norm_w = 1.0 + nrm((n_b, SSD_D_INNER), 0.02)
    ssd_w_out = nrm((n_b, SSD_D_INNER, d), SSD_D_INNER ** -0.5)
    hgrn_w_in = nrm((n_c, d, HG_IN), d ** -0.5)
    hgrn_lb = nrm((2, DEPTH, HG_KEY), 0.1)
    hgrn_norm_w = 1.0 + nrm((n_c, HG_VAL), 0.02)
    hgrn_w_out = nrm((n_c, HG_VAL, d), HG_VAL ** -0.5)
    return {'x': x, 'c': c, 'ctx': ctx, 'c_ctx': c_ctx,
            'ada_w': ada_w, 'ada_b': ada_b, 'norm_g': norm_g,
            'ffn_w_in': ffn_w_in, 'ffn_w_out': ffn_w_out,
            'attn_w_in': attn_w_in, 'attn_w_out': attn_w_out, 'attn_sink': attn_sink,
            'ssd_w_in': ssd_w_in, 'ssd_conv_w': ssd_conv_w, 'ssd_conv_b': ssd_conv_b,
            'ssd_dt_bias': ssd_dt_bias, 'ssd_a_log': ssd_a_log, 'ssd_d': ssd_d,
            'ssd_norm_w': ssd_norm_w, 'ssd_w_out': ssd_w_out,
            'hgrn_w_in': hgrn_w_in, 'hgrn_lb': hgrn_lb, 'hgrn_norm_w': hgrn_norm_w, 'hgrn_w_out': hgrn_w_out}


def reference(x, c, ctx, c_ctx, ada_w, ada_b, norm_g, ffn_w_in, ffn_w_out,
              attn_w_in, attn_w_out, attn_sink,
              ssd_w_in, ssd_conv_w, ssd_conv_b, ssd_dt_bias, ssd_a_log, ssd_d, ssd_norm_w, ssd_w_out,
              hgrn_w_in, hgrn_lb, hgrn_norm_w, hgrn_w_out):
    b = x.shape[0]
    rope_cos, rope_sin = axial_rope_tables(x.shape[1])
    lower_bounds = hgrn2_lower_bounds(hgrn_lb)
    silu_c = jax.nn.silu(c)
    silu_cc = jax.nn.silu(c_ctx)
    h = ctx
    for i in range(DEPTH):
        kind, j = i % N_MIXERS, i // N_MIXERS
        need_ctx = i < DEPTH - 1
        mod = (silu_c @ ada_w[i] + ada_b[i]).reshape(b, N_MOD, 1, D_MODEL)
        mod_c = (silu_cc @ ada_w[i] + ada_b[i]).reshape(N_MOD, D_MODEL)
        u = modulate(x, norm_g[i, 0], mod[:, 0], mod[:, 1])
        u_c = modulate(h, norm_g[i, 0], mod_c[0], mod_c[1])
        if kind == 0:
            y, y_c = attention_mixer(u, u_c, attn_w_in[j], attn_w_out[j], attn_sink[j],
                                     rope_cos, rope_sin, need_ctx)
        elif kind == 1:
            y, y_c = ssd_mixer(u, u_c, ssd_w_in[j], ssd_conv_w[j], ssd_conv_b[j], ssd_dt_bias[j],
                               ssd_a_log[j], ssd_d[j], ssd_norm_w[j], ssd_w_out[j], need_ctx)
        else:
            y, y_c = hgrn2_mixer(u, u_c, hgrn_w_in[j], lower_bounds[:, i], hgrn_norm_w[j],
                                 hgrn_w_out[j], need_ctx)
        x = x + mod[:, 2] * rms_norm(y, norm_g[i, 1])
        f = swiglu(modulate(x, norm_g[i, 2], mod[:, 3], mod[:, 4]), ffn_w_in[i], ffn_w_out[i])
        x = x + mod[:, 5] * rms_norm(f, norm_g[i, 3])
        if need_ctx:
            h = h + mod_c[2] * rms_norm(y_c, norm_g[i, 1])
            f_c = swiglu(modulate(h, norm_g[i, 2], mod_c[3], mod_c[4]), ffn_w_in[i], ffn_w_out[i])
            h = h + mod_c[5] * rms_norm(f_c, norm_g[i, 3])
    return x
```

```python
import numpy as np
import ml_dtypes
from concourse.bass_utils import run_bass_kernel_spmd
import concourse.bass as bass
import concourse.mybir as mybir

F32 = mybir.dt.float32
BF16 = mybir.dt.bfloat16
ALU = mybir.AluOpType
AF = mybir.ActivationFunctionType
AX = mybir.AxisListType
ENGS = ['pe', 'act', 'dve', 'pool', 'sp']
EPOCH = 30000
DMA_R = 8
DMA_EPOCH = 1800


class Node:
    __slots__ = ('lw', 'rd', 'parent', 'kids')

    def __init__(self, parent=None):
        self.lw = None
        self.rd = []
        self.parent = parent
        self.kids = []


class Ins:
    __slots__ = ('eng', 'fn', 'idx', 'waits', 'mile', 'mnum', 'dma', 'dsem', 'dval')

    def __init__(self, eng, fn, idx, dma):
        self.eng = eng
        self.fn = fn
        self.idx = idx
        self.waits = []
        self.mile = False
        self.mnum = 0
        self.dma = dma
        self.dsem = None
        self.dval = 0


class Tile:
    def __init__(self, ap, parent_node=None):
        self.ap = ap
        self.node = Node(parent_node)
        if parent_node is not None:
            parent_node.kids.append(self.node)
        self.subs = {}

    def __getitem__(self, k):
        return self.ap[k]

    def sub(self, key):
        s = self.subs.get(key)
        if s is None:
            s = Tile(self.ap, self.node)
            self.subs[key] = s
        return s


class Prog:
    def __init__(self, nc, arena_bytes=204 * 1024):
        self.nc = nc
        self.streams = {e: [] for e in ENGS}
        self.seen = {e: {} for e in ENGS}
        self.dseen = {e: {} for e in ENGS}
        self.ndma = {e: 0 for e in ENGS}
        self.dma_hist = {e: [] for e in ENGS}
        self.arena_bytes = arena_bytes
        self.top = 0
        self.ghosts = []
        self.live = []
        self._arena_cm = nc.sbuf_tensor('arena', [128, arena_bytes // 2], BF16)
        self.arena = self._arena_cm.__enter__()
        self._psum_cm = nc.psum_tensor('psarena', [128, 8, 512], F32)
        self.psarena = self._psum_cm.__enter__()
        self.banks = [Tile(self.psarena[:, i, :]) for i in range(8)]
        self.dsems = {}
        self.final_waits = []

    def mark(self):
        return (self.top, len(self.live))

    def release(self, mark):
        top, nlive = mark
        for (s, e, t) in self.live[nlive:]:
            deps = []
            self._collect(t.node, deps)
            self.ghosts.append((s, e, deps))
        del self.live[nlive:]
        self.top = top

    def _collect(self, node, deps):
        if node.lw is not None:
            deps.append(node.lw)
        deps.extend(node.rd)
        for k in node.kids:
            self._collect(k, deps)

    def alloc(self, shape, dtype):
        esz = 4 if dtype == F32 else 2
        free = int(np.prod(shape[1:]))
        nbytes = (free * esz + 31) // 32 * 32
        s = self.top
        e = s + nbytes
        assert e <= self.arena_bytes, f"SBUF arena overflow {e} > {self.arena_bytes}"
        self.top = e
        ap = self.arena[0:shape[0], s // 2:(s + free * esz) // 2]
        if dtype != BF16:
            ap = ap.bitcast(dtype)
        if len(shape) == 3:
            ap = ap.rearrange('p (a b) -> p a b', b=shape[2])
        elif len(shape) == 4:
            ap = ap.rearrange('p (a b c) -> p a b c', b=shape[2], c=shape[3])
        t = Tile(ap)
        inherited = []
        ng = []
        for (gs, ge, deps) in self.ghosts:
            if gs < e and s < ge:
                inherited.extend(deps)
                if s <= gs and ge <= e:
                    continue
            ng.append((gs, ge, deps))
        self.ghosts = ng
        t.node.rd = list(dict.fromkeys(inherited))
        self.live.append((s, e, t))
        return t

    def _resolve(self, ins, deps, soft=()):
        e = ins.eng
        best = {}
        for hard, lst in ((True, deps), (False, soft)):
          for d in lst:
            if d is ins:
                continue
            if d.dma:
                key = ('d', d.dsem)
                if key not in best or best[key].dval < d.dval:
                    best[key] = d
            else:
                if d.eng == e and (e == 'pe' or not hard):
                    continue
                key = ('e', d.eng)
                if key not in best or best[key].idx < d.idx:
                    best[key] = d
        for key, d in best.items():
            if d.dma:
                if self.dseen[e].get(d.dsem, 0) >= d.dval:
                    continue
                self.dseen[e][d.dsem] = d.dval
                ins.waits.append(d)
            else:
                if self.seen[e].get(d.eng, -1) >= d.idx:
                    continue
                self.seen[e][d.eng] = d.idx
                d.mile = True
                ins.waits.append(d)

    def op(self, eng, fn, r=(), w=(), dma=False):
        st = self.streams[eng]
        ins = Ins(eng, fn, len(st), dma)
        deps = []
        if dma:
            i = self.ndma[eng]
            self.ndma[eng] = i + 1
            slot = i % DMA_R
            use = i // DMA_R
            ins.dsem = (eng, slot, use // DMA_EPOCH)
            ins.dval = 16 * (use % DMA_EPOCH + 1)
            hist = self.dma_hist[eng]
            if i >= DMA_R:
                deps.append(hist[i - DMA_R])
            hist.append(ins)
        for t in r:
            n = t.node
            if n.lw is not None:
                deps.append(n.lw)
            if n.parent is not None and n.parent.lw is not None:
                deps.append(n.parent.lw)
            for k in n.kids:
                if k.lw is not None:
                    deps.append(k.lw)
        soft = []
        for t in w:
            n = t.node
            nodes = [n] + n.kids + ([n.parent] if n.parent is not None else [])
            for m in nodes:
                if m.lw is not None:
                    deps.append(m.lw)
                soft.extend(m.rd)
        self._resolve(ins, deps, soft)
        for t in r:
            rd = t.node.rd
            if not dma:
                rd[:] = [x for x in rd if x.dma or x.eng != eng]
            rd.append(ins)
        for t in w:
            n = t.node
            n.lw = ins
            n.rd = []
            for k in n.kids:
                k.lw = ins
                k.rd = []
        st.append(ins)
        return ins

    def dma(self, eng, out, in_, r=(), w=(), **kw):
        return self.op(eng, lambda e: e.dma_start(out=out, in_=in_, **kw), r=r, w=w, dma=True)


    def act(self, out, in_, func, r=(), w=(), **kw):
        return self.op('act', lambda e: e.activation(out, in_, func, **kw), r=r, w=w)

    def tt(self, eng, out, a, b, op, r=(), w=()):
        return self.op(eng, lambda e: e.tensor_tensor(out, a, b, op), r=r, w=w)

    def ts(self, eng, out, a, s1, s2, op0, op1=None, r=(), w=()):
        if op1 is None:
            return self.op(eng, lambda e: e.tensor_scalar(out, a, s1, None, op0), r=r, w=w)
        return self.op(eng, lambda e: e.tensor_scalar(out, a, s1, s2, op0, op1), r=r, w=w)

    def stt(self, eng, out, in0, scalar, in1, op0, op1, r=(), w=()):
        return self.op('dve', lambda e: e.scalar_tensor_tensor(out, in0, scalar, in1, op0, op1), r=r, w=w)

    def copy(self, eng, out, in_, r=(), w=()):
        if eng == 'act':
            return self.op('act', lambda e: e.activation(out, in_, AF.Copy), r=r, w=w)
        return self.op(eng, lambda e: e.tensor_copy(out, in_), r=r, w=w)

    def mm(self, out, lhsT, rhs, start, stop, r=(), w=()):
        return self.op('pe', lambda e: e.matmul(out, lhsT, rhs, start=start, stop=stop), r=r, w=w)

    def memset(self, eng, ap, val, w=()):
        return self.op(eng, lambda e: e.memset(ap, val), w=w)

    def finish(self, final_tiles=()):
        nc = self.nc
        fin = Ins('sp', None, len(self.streams['sp']), False)
        fdeps = []
        for e in ENGS:
            fdeps.extend(self.dma_hist[e][-DMA_R:])
            if e != 'sp' and self.streams[e]:
                fdeps.append(self.streams[e][-1])
        self._resolve(fin, fdeps)
        self.streams['sp'].append(fin)
        nmile = {}
        for e in ENGS:
            m = 0
            for ins in self.streams[e]:
                if ins.mile:
                    m += 1
                    ins.mnum = m
            nmile[e] = m
        sem_cms = []

        def newsem(name):
            cm = nc.semaphore(name)
            sem_cms.append(cm)
            return cm.__enter__()

        esems = {e: [newsem(f's_{e}_{k}') for k in range((nmile[e] + EPOCH - 1) // EPOCH)] for e in ENGS}
        dkeys = set()
        for e in ENGS:
            for d in self.dma_hist[e]:
                dkeys.add(d.dsem)
        dsems = {k: newsem(f'd_{k[0]}_{k[1]}_{k[2]}') for k in sorted(dkeys)}
        self.n_sems = len(sem_cms)

        def emit(e, eh):
            for ins in self.streams[e]:
                for d in ins.waits:
                    if d.dma:
                        eh.wait_ge(dsems[d.dsem], d.dval)
                    else:
                        m = d.mnum - 1
                        eh.wait_ge(esems[d.eng][m // EPOCH], m % EPOCH + 1)
                if ins.fn is None:
                    continue
                bi = ins.fn(eh)
                if ins.dma:
                    bi.then_inc(dsems[ins.dsem], 16)
                elif ins.mile:
                    m = ins.mnum - 1
                    bi.then_inc(esems[e][m // EPOCH], 1)

        with nc.Block() as block:
            @block.tensor
            def _(eh):
                emit('pe', eh)

            @block.scalar
            def _(eh):
                emit('act', eh)

            @block.vector
            def _(eh):
                emit('dve', eh)

            @block.gpsimd
            def _(eh):
                emit('pool', eh)

            @block.sync
            def _(eh):
                emit('sp', eh)
        for cm in reversed(sem_cms):
            cm.__exit__(None, None, None)
        self._psum_cm.__exit__(None, None, None)
        self._arena_cm.__exit__(None, None, None)
        return {e: len(self.streams[e]) for e in ENGS}, nmile


D = 2048
KC = 16
FF = 5632
FC = 44
EPS = 1e-6
NMIX = 3


class Cfg:
    def __init__(self, SEQ=2048, CTX=256, NSEQ=2, layers=(0, 1, 2, 3), depth=4, fblk=512, lbdepth=4):
        self.lbdepth = lbdepth
        self.SEQ, self.CTX, self.NSEQ = SEQ, CTX, NSEQ
        self.T = SEQ + CTX
        self.NJ = NSEQ + 1
        self.layers = tuple(layers)
        self.depth = depth
        self.fblk = fblk
        self.NCH = self.T // 64
        self.NCB = CTX // 128
        self.NLB = SEQ // 128


def segs(t0, t1, CTX, maxn=512):
    out = []
    pieces = []
    if t0 < CTX:
        pieces.append((t0, min(t1, CTX), True))
    if t1 > CTX:
        pieces.append((max(t0, CTX), t1, False))
    for a, b, isctx in pieces:
        n = b - a
        k = -(-n // maxn)
        sz = -(-n // k)
        p = a
        while p < b:
            e = min(b, p + sz)
            out.append((p, e - p, isctx))
            p = e
    return out


class Builder:
    def __init__(self, cfg):
        self.cfg = cfg
        nc = bass.Bass("TRN2", target_bir_lowering=False)
        self.nc = nc
        self.P = Prog(nc)
        c = cfg
        T, NJ = c.T, c.NJ

        def inp(name, shape, dt=F32):
            return nc.dram_tensor(name, list(shape), dt, kind="ExternalInput").ap()

        self.xin = inp("xin", [c.NSEQ, D, T])
        self.cvec = inp("cvec", [128, KC, NJ])
        self.ada_w = inp("ada_w", [c.depth, D, 6 * D])
        self.ada_bT = inp("ada_bT", [c.depth, 128, 96])
        self.norm_gT = inp("norm_gT", [c.depth, 4, 128, KC])
        self.ffn_w_in = inp("ffn_w_in", [c.depth, D, 2 * FF])
        self.ffn_w_out = inp("ffn_w_out", [c.depth, FF, D])
        na = len(range(0, c.depth, NMIX))
        nb = len(range(1, c.depth, NMIX))
        ncx = len(range(2, c.depth, NMIX))
        self.attn_w_in = inp("attn_w_in", [na, D, 2560])
        self.attn_w_rot = inp("attn_w_rot", [na, D, 2304])
        self.attn_w_out = inp("attn_w_out", [na, D, D])
        self.attn_sink = inp("attn_sink", [na, 32])
        self.ssd_w_in = inp("ssd_w_in", [max(nb, 1), D, 10368])
        self.ssd_conv_wT = inp("ssd_conv_wT", [max(nb, 1), 128, 48, 3])
        self.ssd_conv_bT = inp("ssd_conv_bT", [max(nb, 1), 128, 48])
        self.ssd_dt_bias = inp("ssd_dt_bias", [max(nb, 1), 128])
        self.ssd_a_log = inp("ssd_a_log", [max(nb, 1), 128])
        self.ssd_d = inp("ssd_d", [max(nb, 1), 64])
        self.ssd_norm_w = inp("ssd_norm_w", [max(nb, 1), 4096])
        self.ssd_w_out = inp("ssd_w_out", [max(nb, 1), 4096, D])
        self.hgrn_w_in = inp("hgrn_w_in", [max(ncx, 1), D, 10240])
        self.hgrn_lb = inp("hgrn_lb", [2, c.lbdepth, D])
        self.hgrn_norm_w = inp("hgrn_norm_w", [max(ncx, 1), D])
        self.hgrn_w_out = inp("hgrn_w_out", [max(ncx, 1), D, D])
        self.ropeC = inp("ropeC", [128, T])
        self.ropeS = inp("ropeS", [128, T])
        self.c_ident = inp("c_ident", [128, 128])
        self.c_maskp = inp("c_maskp", [128, 128])
        self.c_maskn = inp("c_maskn", [128, 128])
        self.c_tri = inp("c_tri", [64, 4, 64])
        self.xout = nc.dram_tensor("xout", [c.NSEQ, D, T], F32, kind="ExternalOutput").ap()
        self.X = nc.dram_tensor("Xs", [c.NSEQ, D, T], F32).ap()
        self.Y = nc.dram_tensor("Ys", [c.NSEQ, D, T], F32).ap()
        self.QT = nc.dram_tensor("QTs", [D, T], BF16).ap()
        self.OT = nc.dram_tensor("OTs", [4096, T], BF16).ap()
        self.XC = nc.dram_tensor("XCs", [6144, T], BF16).ap()
        self.DT = nc.dram_tensor("DTs", [T, 128], F32).ap()
        self.ZS = nc.dram_tensor("ZSs", [T, 4096], BF16).ap()
        self.YB = nc.dram_tensor("YBs", [T, 512], F32).ap()
        self.tX = [Tile(self.X[s]) for s in range(c.NSEQ)]
        self.tY = [Tile(self.Y[s]) for s in range(c.NSEQ)]
        self.tXin = Tile(self.xin)
        self.tXout = Tile(self.xout)
        self.tQT, self.tOT, self.tXC = Tile(self.QT), Tile(self.OT), Tile(self.XC)
        self.tDT, self.tZS, self.tYB = Tile(self.DT), Tile(self.ZS), Tile(self.YB)
        self.wcount = 0
        self.dbg = {}
        self.dbg_on = False
        self.setup_consts()

    def setup_consts(self):
        P, c = self.P, self.cfg
        self.ident = P.alloc([128, 128], BF16)
        P.dma('pool', self.ident[:, :], self.c_ident, w=[self.ident])
        self.ones = P.alloc([128, 128], BF16)
        P.memset('dve', self.ones[:, :], 1.0, w=[self.ones])
        self.identf = P.alloc([128, 128], F32)
        P.dma('sp', self.identf[:, :], self.c_ident, w=[self.identf])
        self.onesf = P.alloc([128, 128], F32)
        P.memset('dve', self.onesf[:, :], 1.0, w=[self.onesf])
        self.trif = P.alloc([64, 4, 64], F32)
        P.dma('sp', self.trif[:, :, :], self.c_tri, w=[self.trif])
        self.trib = P.alloc([64, 4, 64], BF16)
        P.dma('pool', self.trib[:, :, :], self.c_tri, w=[self.trib])
        cv = P.alloc([128, KC, c.NJ], F32)
        P.dma('sp', cv[:, :, :], self.cvec, w=[cv])
        self.csil = P.alloc([128, KC, c.NJ], BF16)
        P.act(self.csil[:, :, :], cv[:, :, :], AF.Silu, r=[cv], w=[self.csil])
        self.modT = P.alloc([128, 6, KC, c.NJ], F32)
        self.gT = P.alloc([128, 4, KC], F32)
        self.A1 = P.alloc([128, KC, c.NJ], F32)
        self.G1 = P.alloc([128, KC, c.NJ], F32)
        self.A2 = P.alloc([128, KC, c.NJ], F32)
        self.G2 = P.alloc([128, KC, c.NJ], F32)
        self.adab = P.alloc([128, 96], F32)
        self.wring = []

    def dump(self, name, tile, ap, shape, dt=F32):
        if not self.dbg_on:
            return
        d = self.nc.dram_tensor("dbg_" + name, list(shape), dt, kind="ExternalOutput").ap()
        t = Tile(d)
        self.dbg[name] = t
        self.P.dma('pool' if dt != tile.ap.dtype else 'sp', d, ap, r=[tile], w=[t])

    def alloc_wring(self, n=3):
        self.wring = [self.P.alloc([128, 8192], BF16) for _ in range(n)]

    def wtile(self):
        t = self.wring[self.wcount % len(self.wring)]
        self.wcount += 1
        return t

    def load_w(self, Wd, kchunks, c0, ncols, wt=None):
        P = self.P
        if wt is None:
            wt = self.wtile()
        assert kchunks * ncols <= 8192
        v = wt[:, 0:kchunks * ncols].rearrange('p (k m) -> p k m', m=ncols)
        src = Wd.rearrange('(k p) m -> p k m', p=128)
        step = max(1, 2048 // ncols)
        for k0 in range(0, kchunks, step):
            k1 = min(kchunks, k0 + step)
            P.dma('pool', v[:, k0:k1, :], src[:, k0:k1, c0:c0 + ncols], w=[wt])
        return wt, v

    def adaln(self, li):
        P, c = self.P, self.cfg
        NJ = c.NJ
        mk = P.mark()
        self.alloc_wring(3)
        P.dma('sp', self.adab[:, :], self.ada_bT[li], w=[self.adab])
        P.dma('sp', self.gT[:, :, :], self.norm_gT[li].rearrange('j p k -> p j k'), w=[self.gT])
        W = self.ada_w[li]
        n = 0
        for g in range(6 * D // 512):
            wt, v = self.load_w(W, KC, g * 512, 512)
            bank = P.banks[4 + (n % 2)]
            n += 1
            for mc in range(4):
                for kc in range(KC):
                    P.mm(bank[:, mc * NJ:(mc + 1) * NJ], v[:, kc, mc * 128:(mc + 1) * 128], self.csil[:, kc, :],
                         kc == 0, kc == KC - 1, r=[wt, self.csil], w=[bank])
            ch0 = g * 4
            kind, chunk = ch0 // KC, ch0 % KC
            P.tt('dve', self.modT[:, kind, chunk:chunk + 4, :],
                 bank[:, 0:4 * NJ].rearrange('p (m j) -> p m j', j=NJ),
                 self.adab[:, ch0:ch0 + 4].unsqueeze(2).broadcast_to([128, 4, NJ]), ALU.add,
                 r=[bank, self.adab], w=[self.modT])
        m = self.modT

        def gb(j):
            return self.gT[:, j, :].unsqueeze(2).broadcast_to([128, KC, NJ])
        P.stt('dve', self.A1[:, :, :], m[:, 1, :, :], 1.0, gb(0), ALU.add, ALU.mult, r=[m, self.gT], w=[self.A1])
        P.tt('dve', self.G1[:, :, :], m[:, 2, :, :], gb(1), ALU.mult, r=[m, self.gT], w=[self.G1])
        P.stt('dve', self.A2[:, :, :], m[:, 4, :, :], 1.0, gb(2), ALU.add, ALU.mult, r=[m, self.gT], w=[self.A2])
        P.tt('dve', self.G2[:, :, :], m[:, 5, :, :], gb(3), ALU.mult, r=[m, self.gT], w=[self.G2])
        P.release(mk)

    def rstd_of(self, src, srct, n, out_rstd, sqring, bank, nfeat=D, kchunks=KC):
        P = self.P
        for kc in range(kchunks):
            sq = sqring[kc % len(sqring)]
            P.act(sq[:, 0:n], src[:, kc, :], AF.Square, r=[srct], w=[sq])
            P.mm(bank[:, 0:n], self.ones[:, :], sq[:, 0:n], kc == 0, kc == kchunks - 1, r=[sq, self.ones], w=[bank])
        P.act(out_rstd[:, 0:n], bank[:, 0:n], AF.Ln, r=[bank], w=[out_rstd], bias=EPS, scale=1.0 / nfeat)
        P.act(out_rstd[:, 0:n], out_rstd[:, 0:n], AF.Exp, r=[out_rstd], w=[out_rstd], scale=-0.5)

    def modulate(self, xt, xv, n, rstd, A, Bm, j, uout, uoutt, tmp):
        P = self.P
        P.tt('dve', tmp[:, :, 0:n], xv, rstd[:, 0:n].unsqueeze(1).broadcast_to([128, KC, n]), ALU.mult,
             r=[xt, rstd], w=[tmp])
        for kc in range(KC):
            if kc % 2 == 0:
                P.act(uout[:, kc, :], tmp[:, kc, 0:n], AF.Identity, r=[tmp, A, Bm], w=[uoutt],
                      bias=Bm[:, kc, j:j + 1], scale=A[:, kc, j:j + 1])
            else:
                P.ts('pool', uout[:, kc, :], tmp[:, kc, 0:n], A[:, kc, j:j + 1], Bm[:, kc, j:j + 1], ALU.mult, ALU.add,
                     r=[tmp, A, Bm], w=[uoutt])

    def prenorm_seq(self, s, first, uT):
        P, c = self.P, self.cfg
        mk = P.mark()
        xb = [P.alloc([128, KC, 256], F32) for _ in range(2)]
        sqr = [P.alloc([128, 512], BF16) for _ in range(3)]
        rstd = [P.alloc([128, 256], F32) for _ in range(2)]
        src = self.xin[s] if first else self.X[s]
        srct = self.tXin if first else self.tX[s]
        for i, (t0, n, isctx) in enumerate(segs(0, c.T, c.CTX, 256)):
            x = xb[i % 2]
            j = c.NSEQ if isctx else s
            P.dma('sp', x[:, :, 0:n], src.rearrange('(k p) t -> p k t', p=128)[:, :, t0:t0 + n], r=[srct], w=[x])
            if first:
                P.dma('sp', self.X[s].rearrange('(k p) t -> p k t', p=128)[:, :, t0:t0 + n], x[:, :, 0:n], r=[x], w=[self.tX[s]])
            r = rstd[i % 2]
            self.rstd_of(x[:, :, 0:n], x, n, r, sqr, P.banks[6 + (i % 2)])
            self.modulate(x, x[:, :, 0:n], n, r, self.A1, _Kind(self.modT, 0), j, uT[:, :, t0:t0 + n], uT, x)
        P.release(mk)

    def ffn_phase(self, li, s, last):
        P, c = self.P, self.cfg
        Xd = self.X[s].rearrange('(k p) t -> p k t', p=128)
        Yd = self.Y[s].rearrange('(k p) t -> p k t', p=128)
        Od = self.xout[s].rearrange('(k p) t -> p k t', p=128)
        Win = self.ffn_w_in[li]
        Wout = self.ffn_w_out[li]
        B2 = _Kind(self.modT, 3)
        mk0 = P.mark()
        self.alloc_wring(3)
        for t0 in range(0, c.T, c.fblk):
            t1 = min(c.T, t0 + c.fblk)
            nb = t1 - t0
            sg = segs(t0, t1, c.CTX, 512)
            sgm = segs(t0, t1, 0, 512)
            mk = P.mark()
            x = P.alloc([128, KC, nb], F32)
            y = P.alloc([128, KC, nb], BF16)
            u = P.alloc([128, KC, nb], BF16)
            sqr = [P.alloc([128, 512], BF16) for _ in range(3)]
            rs = [P.alloc([128, 512], F32) for _ in range(2)]
            mk2 = P.mark()
            ym = P.alloc([128, KC, nb], F32)
            for i, (p, n, isctx) in enumerate(sg):
                j = c.NSEQ if isctx else s
                o = p - t0
                P.dma('sp', x[:, :, o:o + n], Xd[:, :, p:p + n], r=[self.tX[s]], w=[x])
                P.dma('sp', ym[:, :, o:o + n], Yd[:, :, p:p + n], r=[self.tY[s]], w=[ym])
                r0 = rs[0]
                self.rstd_of(ym[:, :, o:o + n], ym, n, r0, sqr, P.banks[6])
                P.tt('dve', ym[:, :, o:o + n], ym[:, :, o:o + n], r0[:, 0:n].unsqueeze(1).broadcast_to([128, KC, n]), ALU.mult,
                     r=[ym, r0], w=[ym])
                for kc in range(KC):
                    P.stt('dve' if kc % 2 else 'pool', x[:, kc, o:o + n], ym[:, kc, o:o + n], self.G1[:, kc, j:j + 1], x[:, kc, o:o + n],
                          ALU.mult, ALU.add, r=[ym, self.G1, x], w=[x])
                P.dma('sp', Xd[:, :, p:p + n], x[:, :, o:o + n], r=[x], w=[self.tX[s]])
                r1 = rs[1]
                self.rstd_of(x[:, :, o:o + n], x, n, r1, sqr, P.banks[7])
                P.tt('dve', ym[:, :, o:o + n], x[:, :, o:o + n], r1[:, 0:n].unsqueeze(1).broadcast_to([128, KC, n]), ALU.mult,
                     r=[x, r1], w=[ym])
                for kc in range(KC):
                    if kc % 2 == 0:
                        P.act(u[:, kc, o:o + n], ym[:, kc, o:o + n], AF.Identity, r=[ym, self.A2, B2], w=[u],
                              bias=B2[:, kc, j:j + 1], scale=self.A2[:, kc, j:j + 1])
                    else:
                        P.ts('pool', u[:, kc, o:o + n], ym[:, kc, o:o + n], self.A2[:, kc, j:j + 1], B2[:, kc, j:j + 1],
                             ALU.mult, ALU.add, r=[ym, self.A2, B2], w=[u])
            P.release(mk2)
            h = P.alloc([128, FC, nb], BF16)
            sgt = [P.alloc([128, 512], F32) for _ in range(2)]
            nbk = 0
            for m0 in range(0, FC, 2):
                wt = self.wtile()
                v = wt[:, 0:KC * 512].rearrange('p (k m) -> p k m', m=512)
                srcw = Win.rearrange('(k p) m -> p k m', p=128)
                for k0 in range(0, KC, 4):
                    P.dma('pool', v[:, k0:k0 + 4, 0:256], srcw[:, k0:k0 + 4, m0 * 128:m0 * 128 + 256], w=[wt])
                    P.dma('pool', v[:, k0:k0 + 4, 256:512], srcw[:, k0:k0 + 4, FF + m0 * 128:FF + m0 * 128 + 256], w=[wt])
                for mm_ in range(2):
                    m = m0 + mm_
                    for (p, n, isctx) in sgm:
                        o = p - t0
                        bg = P.banks[(nbk % 2) * 2]
                        bu = P.banks[(nbk % 2) * 2 + 1]
                        nbk += 1
                        for kc in range(KC):
                            P.mm(bg[:, 0:n], v[:, kc, mm_ * 128:(mm_ + 1) * 128], u[:, kc, o:o + n], kc == 0, kc == KC - 1,
                                 r=[wt, u], w=[bg])
                        for kc in range(KC):
                            P.mm(bu[:, 0:n], v[:, kc, 256 + mm_ * 128:256 + (mm_ + 1) * 128], u[:, kc, o:o + n], kc == 0, kc == KC - 1,
                                 r=[wt, u], w=[bu])
                        st_ = sgt[nbk % 2]
                        P.act(st_[:, 0:n], bg[:, 0:n], AF.Silu, r=[bg], w=[st_])
                        P.tt('dve', h[:, m, o:o + n], st_[:, 0:n], bu[:, 0:n], ALU.mult, r=[st_, bu], w=[h])
            for mo in range(KC):
                wt, v = self.load_w(Wout, FC, mo * 128, 128)
                for (p, n, isctx) in sgm:
                    o = p - t0
                    bk = P.banks[nbk % 4]
                    nbk += 1
                    for kc in range(FC):
                        P.mm(bk[:, 0:n], v[:, kc, :], h[:, kc, o:o + n], kc == 0, kc == FC - 1, r=[wt, h], w=[bk])
                    P.copy('act', y[:, mo, o:o + n], bk[:, 0:n], r=[bk], w=[y])
            P.release(mk2)
            tmp = P.alloc([128, KC, 512], F32)
            for i, (p, n, isctx) in enumerate(sg):
                j = c.NSEQ if isctx else s
                o = p - t0
                r0 = rs[0]
                self.rstd_of(y[:, :, o:o + n], y, n, r0, sqr, P.banks[6])
                P.tt('dve', tmp[:, :, 0:n], y[:, :, o:o + n], r0[:, 0:n].unsqueeze(1).broadcast_to([128, KC, n]), ALU.mult,
                     r=[y, r0], w=[tmp])
                for kc in range(KC):
                    P.stt('dve' if kc % 2 else 'pool', x[:, kc, o:o + n], tmp[:, kc, 0:n], self.G2[:, kc, j:j + 1], x[:, kc, o:o + n],
                          ALU.mult, ALU.add, r=[tmp, self.G2, x], w=[x])
                if last:
                    P.dma('sp', Od[:, :, p:p + n], x[:, :, o:o + n], r=[x], w=[self.tXout])
                else:
                    P.dma('sp', Xd[:, :, p:p + n], x[:, :, o:o + n], r=[x], w=[self.tX[s]])
            P.release(mk)
        P.release(mk0)


    def outproj(self, Wd, kch, src, srct, col0, tok0, ntok, s):
        P = self.P
        gcols = min(512, (8192 // kch) // 128 * 128)
        mk = P.mark()
        ys = [P.alloc([128, 512], F32) for _ in range(3)]
        nb = 0
        sg = segs(0, ntok, 0, 512)
        for m0 in range(0, D, gcols):
            wt, v = self.load_w(Wd, kch, m0, gcols)
            for mc in range(gcols // 128):
                for (p, n, _) in sg:
                    bk = P.banks[nb % 4]
                    y = ys[nb % 3]
                    nb += 1
                    for kc in range(kch):
                        P.mm(bk[:, 0:n], v[:, kc, mc * 128:(mc + 1) * 128], src[:, kc, col0 + p:col0 + p + n], kc == 0, kc == kch - 1,
                             r=[wt, srct], w=[bk])
                    P.copy('act' if nb % 2 else 'dve', y[:, 0:n], bk[:, 0:n], r=[bk], w=[y])
                    P.dma('sp', self.Y[s][m0 + mc * 128:m0 + (mc + 1) * 128, tok0 + p:tok0 + p + n], y[:, 0:n], r=[y], w=[self.tY[s]])
        P.release(mk)

    def attn_phase(self, li, ja, s, first, _x):
        P, c = self.P, self.cfg
        T, NCB = c.T, c.NCB
        NTT = T // 128
        Win, Wrot, Wout = self.attn_w_in[ja], self.attn_w_rot[ja], self.attn_w_out[ja]
        mk = P.mark()
        kT = P.alloc([128, 4, T], BF16)
        vt = P.alloc([128, NTT, 4 * 65], BF16)
        es = P.alloc([128, 32], F32)
        P.dma('sp', es[:, :], self.attn_sink[ja:ja + 1, :].broadcast_to([128, 32]), w=[es])
        P.act(es[:, :], es[:, :], AF.Exp, r=[es], w=[es])
        P.memset('dve', vt[:, :, :], 1.0, w=[vt])
        mk1 = P.mark()
        self.alloc_wring(3)
        uT = P.alloc([128, KC, T], BF16)
        self.prenorm_seq(s, first, uT)
        rC = P.alloc([128, T], F32)
        rS = P.alloc([128, T], F32)
        P.dma('sp', rC[:, :], self.ropeC, w=[rC])
        P.dma('sp', rS[:, :], self.ropeS, w=[rS])
        sg = segs(0, T, 0, 512)
        t1r = [P.alloc([128, 512], F32) for _ in range(2)]
        t2r = [P.alloc([128, 512], F32) for _ in range(2)]
        qs = [P.alloc([128, T], BF16) for _ in range(2)]
        nb = 0
        srcW = Win.rearrange('(k p) m -> p k m', p=128)
        srcR = Wrot.rearrange('(k p) m -> p k m', p=128)

        def rope_proj(v, wt, dst_ap_fn, dstt):
            nonlocal nb
            for (p, n, _) in sg:
                ba = P.banks[(nb % 2) * 2]
                bb = P.banks[(nb % 2) * 2 + 1]
                t1 = t1r[nb % 2]
                t2 = t2r[nb % 2]
                nb += 1
                for kc in range(KC):
                    P.mm(ba[:, 0:n], v[:, kc, 0:128], uT[:, kc, p:p + n], kc == 0, kc == KC - 1, r=[wt, uT], w=[ba])
                for kc in range(KC):
                    P.mm(bb[:, 0:n], v[:, kc, 128:256], uT[:, kc, p:p + n], kc == 0, kc == KC - 1, r=[wt, uT], w=[bb])
                P.tt('dve', t1[:, 0:n], ba[:, 0:n], rC[:, p:p + n], ALU.mult, r=[ba, rC], w=[t1])
                P.tt('dve', t2[:, 0:n], bb[:, 0:n], rS[:, p:p + n], ALU.mult, r=[bb, rS], w=[t2])
                P.tt('pool', dst_ap_fn(p, n), t1[:, 0:n], t2[:, 0:n], ALU.add, r=[t1, t2], w=[dstt])

        for kv in range(4):
            wt = self.wtile()
            v = wt[:, 0:KC * 256].rearrange('p (k m) -> p k m', m=256)
            for k0 in range(0, KC, 8):
                for hh in range(2):
                    P.dma('pool', v[:, k0:k0 + 8, hh * 64:(hh + 1) * 64], srcW[:, k0:k0 + 8, kv * 64:(kv + 1) * 64], w=[wt])
                    P.dma('pool', v[:, k0:k0 + 8, 128 + hh * 64:128 + (hh + 1) * 64], srcR[:, k0:k0 + 8, kv * 64:(kv + 1) * 64], w=[wt])
            rope_proj(v, wt, lambda p, n, kv=kv: kT[:, kv, p:p + n], kT)
        for ch in range(KC):
            wt = self.wtile()
            v = wt[:, 0:KC * 256].rearrange('p (k m) -> p k m', m=256)
            for k0 in range(0, KC, 8):
                P.dma('pool', v[:, k0:k0 + 8, 0:128], srcW[:, k0:k0 + 8, 512 + ch * 128:512 + (ch + 1) * 128], w=[wt])
                P.dma('pool', v[:, k0:k0 + 8, 128:256], srcR[:, k0:k0 + 8, 256 + ch * 128:256 + (ch + 1) * 128], w=[wt])
            q = qs[ch % 2]
            rope_proj(v, wt, lambda p, n, q=q: q[:, p:p + n], q)
            P.dma('sp', self.QT[ch * 128:(ch + 1) * 128, :], q[:, :], r=[q], w=[self.tQT])
        wt, v = self.load_w(Win, KC, 256, 256)
        for tt in range(NTT):
            bk = P.banks[tt % 4]
            for kc in range(KC):
                P.mm(bk[:, 0:256], uT[:, kc, tt * 128:(tt + 1) * 128], v[:, kc, :], kc == 0, kc == KC - 1, r=[wt, uT], w=[bk])
            P.copy('act' if tt % 2 else 'dve', vt[:, tt, :].rearrange('p (a b) -> p a b', b=65)[:, :, 0:64],
                   bk[:, 0:256].rearrange('p (a b) -> p a b', b=64), r=[bk], w=[vt])
        self.dump('kT', kT, kT[:, :, :], [128, 4, T])
        self.dump('vt', vt, vt[:, :, :], [128, NTT, 260])
        self.dump('uT', uT, uT[:, :, :], [128, KC, T])
        P.release(mk1)
        oT = P.alloc([128, KC, T], BF16)
        mkb = P.mark()
        qg = [P.alloc([128, 4, T], BF16) for _ in range(2)]
        pring = [P.alloc([128, 512], BF16) for _ in range(12)]
        otm = [P.alloc([128, 4, 128], BF16) for _ in range(2)]
        den = [P.alloc([128, 4], F32) for _ in range(2)]
        mp = P.alloc([128, 128], BF16)
        mn = P.alloc([128, 128], BF16)
        P.dma('pool', mp[:, :], self.c_maskp, w=[mp])
        P.dma('pool', mn[:, :], self.c_maskn, w=[mn])
        npi = 0
        nsb = 0
        nob = 0
        for g in range(4):
            q = qg[g % 2]
            P.dma('sp', q[:, :, :], self.QT[g * 512:(g + 1) * 512, :].rearrange('(c p) t -> p c t', p=128), r=[self.tQT], w=[q])
            for i in range(NTT):
                if i < NCB:
                    keys = [(jt, None) for jt in range(NCB)]
                else:
                    keys = [(jt, None) for jt in range(NCB)]
                    if i - 1 >= NCB:
                        keys.append((i - 1, mp))
                    keys.append((i, None))
                    if i + 1 < NTT:
                        keys.append((i + 1, mn))
                ot = otm[nob % 2]
                for half in range(2):
                    hs = slice(half * 64, (half + 1) * 64)
                    pts = []
                    for (jt, msk) in keys:
                        bk = P.banks[nsb % 4]
                        nsb += 1
                        P.mm(bk[:, :], kT[hs, g, jt * 128:(jt + 1) * 128], q[hs, :, i * 128:(i + 1) * 128], True, msk is None,
                             r=[kT, q], w=[bk])
                        if msk is not None:
                            P.mm(bk[:, :], self.ident[:, :], msk[:, :].unsqueeze(1).broadcast_to([128, 4, 128]), False, True,
                                 r=[self.ident, msk], w=[bk])
                        pt = pring[npi % 12]
                        npi += 1
                        P.act(pt[:, :], bk[:, :], AF.Exp, r=[bk], w=[pt], scale=0.125)
                        pts.append((pt, jt))
                    bo = P.banks[4 + (nob * 2 + half) % 2]
                    for cc in range(4):
                        for k_i, (pt, jt) in enumerate(pts):
                            P.mm(bo[:, cc * 65:(cc + 1) * 65], pt[:, cc * 128:(cc + 1) * 128], vt[:, jt, g * 65:(g + 1) * 65],
                                 k_i == 0, k_i == len(pts) - 1, r=[pt, vt], w=[bo])
                    dn = den[half]
                    bov = bo[:, 0:260].rearrange('p (a b) -> p a b', b=65)
                    esv = es[:, g * 8:(g + 1) * 8].rearrange('p (a two) -> p a two', two=2)[:, :, half]
                    P.tt('dve', dn[:, :], bov[:, :, 64], esv, ALU.add, r=[bo, es], w=[dn])
                    P.op('dve', lambda e, dn=dn: e.reciprocal(dn[:, :], dn[:, :]), r=[dn], w=[dn])
                    P.tt('dve', ot[:, :, half * 64:(half + 1) * 64], bov[:, :, 0:64], dn[:, :].unsqueeze(2).broadcast_to([128, 4, 64]),
                         ALU.mult, r=[bo, dn], w=[ot])
                bt = P.banks[6 + nob % 2]
                nob += 1
                for cc in range(4):
                    P.mm(bt[:, cc * 128:(cc + 1) * 128], ot[:, cc, :], self.ident[:, :], True, True, r=[ot, self.ident], w=[bt])
                P.copy('act', oT[:, g * 4:(g + 1) * 4, i * 128:(i + 1) * 128], bt[:, :].rearrange('p (a b) -> p a b', b=128),
                       r=[bt], w=[oT])
        self.dump('oT', oT, oT[:, :, :], [128, KC, T])
        P.release(mkb)
        self.alloc_wring(3)
        self.outproj(Wout, KC, oT, oT, 0, 0, T, s)
        P.release(mk)

    def ssd_phase(self, li, jb, s, first, _x):
        P, c = self.P, self.cfg
        T, CTX, SEQ, NCH = c.T, c.CTX, c.SEQ, c.NCH
        NCC = CTX // 64
        Win, Wout = self.ssd_w_in[jb], self.ssd_w_out[jb]
        mk = P.mark()
        self.alloc_wring(3)
        dt_all = P.alloc([64, NCH, 128], F32)
        Abc = P.alloc([64, 128], F32)
        Dbc = P.alloc([64, 64], F32)
        dtb = P.alloc([64, 128], F32)
        P.dma('sp', Abc[:, :], self.ssd_a_log[jb:jb + 1, :].broadcast_to([64, 128]), w=[Abc])
        P.act(Abc[:, :], Abc[:, :], AF.Exp, r=[Abc], w=[Abc])
        P.ts('dve', Abc[:, :], Abc[:, :], -1.0, None, ALU.mult, r=[Abc], w=[Abc])
        P.dma('sp', Dbc[:, :], self.ssd_d[jb:jb + 1, :].broadcast_to([64, 64]), w=[Dbc])
        P.dma('sp', dtb[:, :], self.ssd_dt_bias[jb:jb + 1, :].broadcast_to([64, 128]), w=[dtb])
        mk1 = P.mark()
        uT = P.alloc([128, KC, T], BF16)
        self.prenorm_seq(s, first, uT)
        cw = P.alloc([128, 48, 3], F32)
        cb = P.alloc([128, 48], F32)
        P.dma('sp', cw[:, :, :], self.ssd_conv_wT[jb], w=[cw])
        P.dma('sp', cb[:, :], self.ssd_conv_bT[jb], w=[cb])
        pre = [P.alloc([128, T + 4], F32) for _ in range(2)]
        acc = P.alloc([128, T], F32)
        xcs = [P.alloc([128, T], BF16) for _ in range(2)]
        for pz in pre:
            P.memset('dve', pz[:, :], 0.0, w=[pz])
        sgc = segs(0, T, CTX, 512)
        nbk = 0
        for g4 in range(12):
            wt, v = self.load_w(Win, KC, g4 * 512, 512)
            for mc in range(4):
                ch = g4 * 4 + mc
                pz = pre[ch % 2]
                xc = xcs[ch % 2]
                for (p, n, isctx) in sgc:
                    bk = P.banks[nbk % 4]
                    nbk += 1
                    for kc in range(KC):
                        P.mm(bk[:, 0:n], v[:, kc, mc * 128:(mc + 1) * 128], uT[:, kc, p:p + n], kc == 0, kc == KC - 1, r=[wt, uT], w=[bk])
                    off = 1 + p if isctx else 3 + p
                    P.copy('act', pz[:, off:off + n], bk[:, 0:n], r=[bk], w=[pz])
                for (off, t0, n) in ((1, 0, CTX), (CTX + 3, CTX, SEQ)):
                    P.ts('dve', acc[:, t0:t0 + n], pz[:, off:off + n], cw[:, ch, 1:2], cb[:, ch:ch + 1], ALU.mult, ALU.add, r=[pz, cw, cb], w=[acc])
                    P.stt('dve', acc[:, t0:t0 + n], pz[:, off - 1:off - 1 + n], cw[:, ch, 0:1], acc[:, t0:t0 + n], ALU.mult, ALU.add, r=[pz, cw, acc], w=[acc])
                    P.stt('dve', acc[:, t0:t0 + n], pz[:, off + 1:off + 1 + n], cw[:, ch, 2:3], acc[:, t0:t0 + n], ALU.mult, ALU.add, r=[pz, cw, acc], w=[acc])
                P.act(xc[:, :], acc[:, :], AF.Silu, r=[acc], w=[xc])
                P.dma('sp', self.XC[ch * 128:(ch + 1) * 128, :], xc[:, :], r=[xc], w=[self.tXC.sub(ch)])
        wt, v = self.load_w(Win, KC, 6144, 128)
        et = P.alloc([64, 512], F32)
        for c0 in range(0, NCH, 4):
            nn = min(4, NCH - c0)
            bk = P.banks[nbk % 4]
            nbk += 1
            for q in range(nn):
                cc = c0 + q
                for kc in range(KC):
                    P.mm(bk[0:64, q * 128:(q + 1) * 128], uT[:, kc, cc * 64:(cc + 1) * 64], v[:, kc, :], kc == 0, kc == KC - 1, r=[wt, uT], w=[bk])
            P.tt('dve', et[:, 0:nn * 128].rearrange('p (a b) -> p a b', b=128), bk[0:64, 0:nn * 128].rearrange('p (a b) -> p a b', b=128),
                 dtb[:, :].unsqueeze(1).broadcast_to([64, nn, 128]), ALU.add, r=[bk, dtb], w=[et])
            P.act(et[:, 0:nn * 128], et[:, 0:nn * 128], AF.Exp, r=[et], w=[et])
            P.act(dt_all[:, c0:c0 + nn, :], et[:, 0:nn * 128].rearrange('p (a b) -> p a b', b=128), AF.Ln, r=[et], w=[dt_all], bias=1.0, scale=1.0)
        zst = [P.alloc([128, 512], BF16) for _ in range(3)]
        nz = 0
        for g8 in range(8):
            wt, v = self.load_w(Win, KC, 6272 + g8 * 512, 512)
            for tt in range(T // 128):
                bk = P.banks[nbk % 4]
                nbk += 1
                for kc in range(KC):
                    P.mm(bk[:, :], uT[:, kc, tt * 128:(tt + 1) * 128], v[:, kc, :], kc == 0, kc == KC - 1, r=[wt, uT], w=[bk])
                z = zst[nz % 3]
                nz += 1
                P.act(z[:, :], bk[:, :], AF.Silu, r=[bk], w=[z])
                P.dma('sp', self.ZS[tt * 128:(tt + 1) * 128, g8 * 512:(g8 + 1) * 512], z[:, :], r=[z], w=[self.tZS.sub(g8)])
        P.release(mk1)
        nw = P.alloc([64, 4096], F32)
        P.dma('sp', nw[:, :], self.ssd_norm_w[jb:jb + 1, :].broadcast_to([64, 4096]), w=[nw])
        xcg = P.alloc([128, 6, T], BF16)
        ygT = P.alloc([128, 4, T], BF16)
        a_g = P.alloc([64, NCH, 16], F32)
        dtg = P.alloc([64, NCH, 16], F32)
        eacum = P.alloc([64, NCH, 16], F32)
        dend = P.alloc([64, NCH, 16], F32)
        cdg = P.alloc([128, NCH, 16], F32)
        S = P.alloc([128, 512], F32)
        Sb = P.alloc([128, 512], BF16)
        R2 = lambda shape, dt: [P.alloc(shape, dt) for _ in range(2)]
        x_tm, B_tm, cbm = R2([64, 512], BF16), R2([64, 128], BF16), R2([64, 64], BF16)
        segr, LT, GT = R2([64, 8, 64], F32), R2([64, 8, 64], BF16), R2([64, 8, 64], BF16)
        xdt, xdtd = R2([64, 8, 64], BF16), R2([64, 8, 64], BF16)
        yo, yb, zs, tn = R2([64, 8, 64], F32), R2([64, 512], F32), R2([64, 512], BF16), R2([64, 512], BF16)
        ss, junk = R2([64, 1], F32), R2([64, 512], F32)
        trif = self.trif
        XCv = self.XC
        it = 0
        for g in range(8):
            P.dma('sp', xcg[:, 0:4, :], XCv[g * 512:(g + 1) * 512, :].rearrange('(q p) t -> p q t', p=128),
                  r=[self.tXC.sub(g * 4 + q) for q in range(4)], w=[xcg])
            P.dma('sp', xcg[:, 4, :], XCv[4096 + g * 128:4096 + (g + 1) * 128, :], r=[self.tXC.sub(32 + g)], w=[xcg])
            P.dma('sp', xcg[:, 5, :], XCv[5120 + g * 128:5120 + (g + 1) * 128, :], r=[self.tXC.sub(40 + g)], w=[xcg])
            for d in range(2):
                hsl = slice(d * 64 + g * 8, d * 64 + g * 8 + 8)
                P.tt('dve', a_g[:, :, d * 8:(d + 1) * 8], dt_all[:, :, hsl], Abc[:, hsl].unsqueeze(1).broadcast_to([64, NCH, 8]), ALU.mult,
                     r=[dt_all, Abc], w=[a_g])
                P.copy('dve', dtg[:, :, d * 8:(d + 1) * 8], dt_all[:, :, hsl], r=[dt_all], w=[dtg])
            for d in range(2):
                bk = P.banks[7]
                P.mm(bk[0:64, 0:NCH * 8], trif[:, d, :], a_g[:, :, d * 8:(d + 1) * 8], True, True, r=[trif, a_g], w=[bk])
                P.act(eacum[:, :, d * 8:(d + 1) * 8], bk[0:64, 0:NCH * 8].rearrange('p (a b) -> p a b', b=8), AF.Exp, r=[bk], w=[eacum])
                P.mm(bk[0:64, 0:NCH * 8], trif[:, 2 + d, :], a_g[:, :, d * 8:(d + 1) * 8], True, True, r=[trif, a_g], w=[bk])
                P.act(dend[:, :, d * 8:(d + 1) * 8], bk[0:64, 0:NCH * 8].rearrange('p (a b) -> p a b', b=8), AF.Exp, r=[bk], w=[dend])
            hc = (NCH + 1) // 2
            for c0 in (0, hc):
                nn = min(hc, NCH - c0)
                bk = P.banks[7]
                P.mm(bk[:, 0:nn * 16], self.onesf[0:64, :], a_g[:, c0:c0 + nn, :], True, True, r=[self.onesf, a_g], w=[bk])
                P.act(cdg[:, c0:c0 + nn, :], bk[:, 0:nn * 16].rearrange('p (a b) -> p a b', b=16), AF.Exp, r=[bk], w=[cdg])
            for d in (1, 0):
                order = list(range(NCH)) if d == 0 else (list(range(NCC - 1, -1, -1)) + list(range(NCH - 1, NCC - 1, -1)))
                P.memset('dve', S[:, :], 0.0, w=[S])
                P.memset('pool', Sb[:, :], 0.0, w=[Sb])
                dsl = slice(d * 8, (d + 1) * 8)
                for cix in order:
                    k2 = it % 2
                    it += 1
                    tok = slice(cix * 64, (cix + 1) * 64)
                    bx, bB, bseg, by, boff, bst, btr = (P.banks[i_] for i_ in range(7))
                    for q in range(4):
                        P.mm(bx[0:64, q * 128:(q + 1) * 128], xcg[:, q, tok], self.ident[:, :], True, True, r=[xcg, self.ident], w=[bx])
                    P.mm(bB[0:64, 0:128], xcg[:, 4, tok], self.ident[:, :], True, True, r=[xcg, self.ident], w=[bB])
                    P.mm(bB[0:64, 128:192], xcg[:, 4, tok], xcg[:, 5, tok], True, True, r=[xcg], w=[bB])
                    P.copy('act', x_tm[k2][:, :], bx[0:64, :], r=[bx], w=[x_tm[k2]])
                    P.copy('act', B_tm[k2][:, :], bB[0:64, 0:128], r=[bB], w=[B_tm[k2]])
                    P.tt('dve', cbm[k2][:, :], bB[0:64, 128:192], trif[:, d, :], ALU.mult, r=[bB, trif], w=[cbm[k2]])
                    P.tt('pool', segr[k2][:, :, :], a_g[:, cix, dsl].unsqueeze(2).broadcast_to([64, 8, 64]),
                         trif[:, d, :].unsqueeze(1).broadcast_to([64, 8, 64]), ALU.mult, r=[a_g, trif], w=[segr[k2]])
                    P.mm(bseg[0:64, :], trif[:, 2 + d, :], segr[k2][:, :, :], True, True, r=[trif, segr[k2]], w=[bseg])
                    P.act(LT[k2][:, :, :], bseg[0:64, :].rearrange('p (a b) -> p a b', b=64), AF.Exp, r=[bseg], w=[LT[k2]])
                    P.tt('dve', GT[k2][:, :, :], LT[k2][:, :, :], cbm[k2][:, :].unsqueeze(1).broadcast_to([64, 8, 64]), ALU.mult,
                         r=[LT[k2], cbm[k2]], w=[GT[k2]])
                    xv = x_tm[k2][:, :].rearrange('p (a b) -> p a b', b=64)
                    P.tt('pool', xdt[k2][:, :, :], xv, dtg[:, cix, dsl].unsqueeze(2).broadcast_to([64, 8, 64]), ALU.mult,
                         r=[x_tm[k2], dtg], w=[xdt[k2]])
                    P.tt('pool', xdtd[k2][:, :, :], xdt[k2][:, :, :], dend[:, cix, dsl].unsqueeze(2).broadcast_to([64, 8, 64]), ALU.mult,
                         r=[xdt[k2], dend], w=[xdtd[k2]])
                    for h in range(8):
                        P.mm(by[0:64, h * 64:(h + 1) * 64], GT[k2][:, h, :], xdt[k2][:, h, :], True, True, r=[GT[k2], xdt[k2]], w=[by])
                    P.mm(boff[0:64, :], xcg[:, 5, tok], Sb[:, :], True, True, r=[xcg, Sb], w=[boff])
                    P.tt('dve', yo[k2][:, :, :], boff[0:64, :].rearrange('p (a b) -> p a b', b=64),
                         eacum[:, cix, dsl].unsqueeze(2).broadcast_to([64, 8, 64]), ALU.mult, r=[boff, eacum], w=[yo[k2]])
                    yv = yo[k2][:, :, :].rearrange('p a b -> p (a b)')
                    P.tt('dve', yv, yv, by[0:64, :], ALU.add, r=[yo[k2], by], w=[yo[k2]])
                    P.mm(bst[:, :], B_tm[k2][:, :], xdtd[k2][:, :, :], True, True, r=[B_tm[k2], xdtd[k2]], w=[bst])
                    Sv = S[:, :].rearrange('p (a b) -> p a b', b=64)
                    P.tt('dve', Sv, Sv, cdg[:, cix, dsl].unsqueeze(2).broadcast_to([128, 8, 64]), ALU.mult, r=[S, cdg], w=[S])
                    P.tt('dve', S[:, :], S[:, :], bst[:, :], ALU.add, r=[S, bst], w=[S])
                    P.copy('act', Sb[:, :], S[:, :], r=[S], w=[Sb])
                    ybt = self.tYB.sub(cix)
                    if d == 1:
                        P.dma('sp', self.YB[tok, :], yv, r=[yo[k2]], w=[ybt])
                    else:
                        P.dma('sp', yb[k2][:, :], self.YB[tok, :], r=[ybt], w=[yb[k2]])
                        P.dma('sp', zs[k2][:, :], self.ZS[tok, g * 512:(g + 1) * 512], r=[self.tZS.sub(g)], w=[zs[k2]])
                        P.tt('dve', yv, yv, yb[k2][:, :], ALU.add, r=[yo[k2], yb[k2]], w=[yo[k2]])
                        xd = junk[k2]
                        P.tt('pool', xd[:, :].rearrange('p (a b) -> p a b', b=64), xv,
                             Dbc[:, g * 8:(g + 1) * 8].unsqueeze(2).broadcast_to([64, 8, 64]), ALU.mult, r=[x_tm[k2], Dbc], w=[xd])
                        P.tt('dve', yv, yv, xd[:, :], ALU.add, r=[yo[k2], xd], w=[yo[k2]])
                        P.tt('dve', yv, yv, zs[k2][:, :], ALU.mult, r=[yo[k2], zs[k2]], w=[yo[k2]])
                        P.memset('dve', ss[k2][:, :], 0.0, w=[ss[k2]])
                        P.act(xd[:, :], yv, AF.Square, r=[yo[k2]], w=[xd, ss[k2]], accum_out=ss[k2][:, 0:1])
                        P.act(ss[k2][:, :], ss[k2][:, :], AF.Ln, r=[ss[k2]], w=[ss[k2]], bias=EPS, scale=1.0 / 512)
                        P.act(ss[k2][:, :], ss[k2][:, :], AF.Exp, r=[ss[k2]], w=[ss[k2]], scale=-0.5)
                        P.stt('dve', tn[k2][:, :], yv, ss[k2][:, 0:1], nw[:, g * 512:(g + 1) * 512], ALU.mult, ALU.mult,
                              r=[yo[k2], ss[k2], nw], w=[tn[k2]])
                        for q in range(4):
                            P.mm(btr[:, q * 64:(q + 1) * 64], tn[k2][:, q * 128:(q + 1) * 128], self.ident[0:64, 0:64], True, True,
                                 r=[tn[k2], self.ident], w=[btr])
                        P.copy('act', ygT[:, :, tok], btr[:, 0:256].rearrange('p (a b) -> p a b', b=64), r=[btr], w=[ygT])
            P.dma('sp', self.OT[g * 512:(g + 1) * 512, :].rearrange('(q p) t -> p q t', p=128), ygT[:, :, :], r=[ygT], w=[self.tOT.sub(g)])
        P.release(mk)
        mk = P.mark()
        self.alloc_wring(3)
        nblk = 1152 if T > 1152 else T
        src = P.alloc([128, 32, nblk], BF16)
        for t0 in range(0, T, nblk):
            n = min(nblk, T - t0)
            P.dma('sp', src[:, :, 0:n], self.OT[:, t0:t0 + n].rearrange('(q p) t -> p q t', p=128), r=[self.tOT], w=[src])
            self.outproj(Wout, 32, src, src, 0, t0, n, s)
        P.release(mk)

    def hgrn_phase(self, li, jc, s, first, lbi):
        P, c = self.P, self.cfg
        T, CTX, NCH = c.T, c.CTX, c.NCH
        NCC = CTX // 64
        Win, Wout = self.hgrn_w_in[jc], self.hgrn_w_out[jc]
        depth = c.lbdepth
        mk = P.mark()
        self.alloc_wring(2)
        nwb = P.alloc([64, D], F32)
        P.dma('sp', nwb[:, :], self.hgrn_norm_w[jc:jc + 1, :].broadcast_to([64, D]), w=[nwb])
        uT = P.alloc([128, KC, T], BF16)
        self.prenorm_seq(s, first, uT)
        lg_tm = P.alloc([64, NCH, 128], F32)
        k_tm = P.alloc([64, NCH, 128], BF16)
        v_tm = P.alloc([64, NCH, 128], BF16)
        gs_tm = P.alloc([64, NCH, 128], BF16)
        qT = P.alloc([128, T], BF16)
        oTh = P.alloc([128, T], BF16)
        lbraw = P.alloc([64, 2, depth, 128], F32)
        lbt = P.alloc([64, 2, 128], F32)
        omlt = P.alloc([64, 2, 128], F32)
        lden = P.alloc([64, 2, 128], F32)
        state = P.alloc([128, 128], F32)
        state_bf = P.alloc([128, 128], BF16)
        R2 = lambda shape, dt: [P.alloc(shape, dt) for _ in range(2)]
        sig, gg = R2([64, 128], F32), R2([64, 128], F32)
        egT, engT, qgT, kgT = R2([128, 64], F32), R2([128, 64], F32), R2([128, 64], BF16), R2([128, 64], BF16)
        ket, kend, sT = R2([64, 128], F32), R2([64, 128], BF16), R2([64, 64], BF16)
        osb, obl, on, on2 = R2([64, 128], F32), R2([64, 128], F32), R2([64, 128], F32), R2([64, 128], BF16)
        ss, junk = R2([64, 1], F32), R2([64, 128], F32)
        trif = self.trif
        srcW = Win.rearrange('(k p) m -> p k m', p=128)
        sgq = segs(0, T, 0, 512)
        it = 0
        npb = 0
        for h in range(16):
            for d in range(2):
                P.dma('sp', lbraw[:, d, :, :], self.hgrn_lb[d:d + 1, :, h * 128:(h + 1) * 128].broadcast_to([64, depth, 128]), w=[lbraw])
            P.act(lbraw[:, :, :, :], lbraw[:, :, :, :], AF.Exp, r=[lbraw], w=[lbraw])
            P.copy('dve', lden[:, :, :], lbraw[:, :, 0, :], r=[lbraw], w=[lden])
            for j in range(1, depth):
                P.tt('dve', lden[:, :, :], lden[:, :, :], lbraw[:, :, j, :], ALU.add, r=[lden, lbraw], w=[lden])
            P.op('dve', lambda e: e.reciprocal(omlt[:, :, :], lden[:, :, :]), r=[lden], w=[omlt])
            P.memset('dve', lbt[:, :, :], 0.0, w=[lbt])
            for j in range(1, lbi + 1):
                P.tt('dve', lbt[:, :, :], lbt[:, :, :], lbraw[:, :, j, :], ALU.add, r=[lbt, lbraw], w=[lbt])
            P.tt('dve', lbt[:, :, :], lbt[:, :, :], omlt[:, :, :], ALU.mult, r=[lbt, omlt], w=[lbt])
            P.ts('dve', omlt[:, :, :], lbt[:, :, :], -1.0, 1.0, ALU.mult, ALU.add, r=[lbt], w=[omlt])
            wt = self.wtile()
            v = wt[:, 0:KC * 512].rearrange('p (k m) -> p k m', m=512)
            for qi, c0 in enumerate((2048 + h * 128, 4096 + h * 128, 8192 + h * 128, h * 128)):
                for k0 in range(0, KC, 8):
                    P.dma('pool', v[:, k0:k0 + 8, qi * 128:(qi + 1) * 128], srcW[:, k0:k0 + 8, c0:c0 + 128], w=[wt])
            wq, vq = self.load_w(Win, KC, 6144 + h * 128, 128)
            for (p, n, _) in sgq:
                bk = P.banks[6 + npb % 2]
                npb += 1
                for kc in range(KC):
                    P.mm(bk[:, 0:n], vq[:, kc, :], uT[:, kc, p:p + n], kc == 0, kc == KC - 1, r=[wq, uT], w=[bk])
                P.act(qT[:, p:p + n], bk[:, 0:n], AF.Silu, r=[bk], w=[qT])
            for d in (1, 0):
                c_lo, c_hi = (0, 384) if d == 1 else (384, 512)
                for cc in range(NCH):
                    tok = slice(cc * 64, (cc + 1) * 64)
                    bk = P.banks[6 + npb % 2]
                    k2 = npb % 2
                    npb += 1
                    ncol = c_hi - c_lo
                    for kc in range(KC):
                        P.mm(bk[0:64, 0:ncol], uT[:, kc, tok], v[:, kc, c_lo:c_hi], kc == 0, kc == KC - 1, r=[wt, uT], w=[bk])
                    P.act(sig[k2][:, :], bk[0:64, 0:128], AF.Sigmoid, r=[bk], w=[sig[k2]])
                    P.tt('dve', gg[k2][:, :], sig[k2][:, :], omlt[:, d, :], ALU.mult, r=[sig[k2], omlt], w=[gg[k2]])
                    P.tt('dve', gg[k2][:, :], gg[k2][:, :], lbt[:, d, :], ALU.add, r=[gg[k2], lbt], w=[gg[k2]])
                    P.act(lg_tm[:, cc, :], gg[k2][:, :], AF.Ln, r=[gg[k2]], w=[lg_tm])
                    P.ts('dve', k_tm[:, cc, :], gg[k2][:, :], -1.0, 1.0, ALU.mult, ALU.add, r=[gg[k2]], w=[k_tm])
                    if d == 1:
                        P.copy('dve', v_tm[:, cc, :], bk[0:64, 128:256], r=[bk], w=[v_tm])
                        P.act(gs_tm[:, cc, :], bk[0:64, 256:384], AF.Silu, r=[bk], w=[gs_tm])
                order = list(range(NCH)) if d == 0 else (list(range(NCC - 1, -1, -1)) + list(range(NCH - 1, NCC - 1, -1)))
                P.memset('dve', state[:, :], 0.0, w=[state])
                P.memset('pool', state_bf[:, :], 0.0, w=[state_bf])
                for cix in order:
                    k2 = it % 2
                    it += 1
                    tok = slice(cix * 64, (cix + 1) * 64)
                    bA, bB, bC, bD, bE, bF = (P.banks[i_] for i_ in range(6))
                    P.mm(bA[:, 0:64], lg_tm[:, cix, :], trif[:, d, :], True, True, r=[lg_tm, trif], w=[bA])
                    P.mm(bA[:, 64:128], k_tm[:, cix, :], self.ident[0:64, 0:64], True, True, r=[k_tm, self.ident], w=[bA])
                    P.mm(bB[0:64, 0:128], trif[:, 2 + d, :], lg_tm[:, cix, :], True, True, r=[trif, lg_tm], w=[bB])
                    P.act(egT[k2][:, :], bA[:, 0:64], AF.Exp, r=[bA], w=[egT[k2]])
                    P.act(engT[k2][:, :], bA[:, 0:64], AF.Exp, r=[bA], w=[engT[k2]], scale=-1.0)
                    P.tt('dve', qgT[k2][:, :], qT[:, tok], egT[k2][:, :], ALU.mult, r=[qT, egT[k2]], w=[qgT[k2]])
                    P.tt('dve', kgT[k2][:, :], bA[:, 64:128], engT[k2][:, :], ALU.mult, r=[bA, engT[k2]], w=[kgT[k2]])
                    P.act(ket[k2][:, :], bB[0:64, 0:128], AF.Exp, r=[bB], w=[ket[k2]])
                    P.tt('pool', kend[k2][:, :], ket[k2][:, :], k_tm[:, cix, :], ALU.mult, r=[ket[k2], k_tm], w=[kend[k2]])
                    P.mm(bC[0:64, 0:64], kgT[k2][:, :], qgT[k2][:, :], True, True, r=[kgT[k2], qgT[k2]], w=[bC])
                    P.tt('dve', sT[k2][:, :], bC[0:64, 0:64], trif[:, d, :], ALU.mult, r=[bC, trif], w=[sT[k2]])
                    P.mm(bD[0:64, 0:128], sT[k2][:, :], v_tm[:, cix, :], True, False, r=[sT[k2], v_tm], w=[bD])
                    P.mm(bD[0:64, 0:128], qgT[k2][:, :], state_bf[:, :], False, True, r=[qgT[k2], state_bf], w=[bD])
                    P.mm(bE[:, 0:128], kend[k2][:, :], v_tm[:, cix, :], True, True, r=[kend[k2], v_tm], w=[bE])
                    ecol = 63 if d == 0 else 0
                    P.stt('dve', state[:, :], state[:, :], egT[k2][:, ecol:ecol + 1], bE[:, 0:128], ALU.mult, ALU.add,
                          r=[state, egT[k2], bE], w=[state])
                    P.copy('act', state_bf[:, :], state[:, :], r=[state], w=[state_bf])
                    ybt = self.tYB.sub(cix)
                    if d == 1:
                        P.copy('act', osb[k2][:, :], bD[0:64, 0:128], r=[bD], w=[osb[k2]])
                        P.dma('sp', self.YB[tok, 0:128], osb[k2][:, :], r=[osb[k2]], w=[ybt])
                    else:
                        P.dma('sp', obl[k2][:, :], self.YB[tok, 0:128], r=[ybt], w=[obl[k2]])
                        P.tt('dve', osb[k2][:, :], bD[0:64, 0:128], obl[k2][:, :], ALU.add, r=[bD, obl[k2]], w=[osb[k2]])
                        P.memset('dve', ss[k2][:, :], 0.0, w=[ss[k2]])
                        P.act(junk[k2][:, :], osb[k2][:, :], AF.Square, r=[osb[k2]], w=[junk[k2], ss[k2]], accum_out=ss[k2][:, 0:1])
                        P.act(ss[k2][:, :], ss[k2][:, :], AF.Ln, r=[ss[k2]], w=[ss[k2]], bias=EPS, scale=1.0 / 128)
                        P.act(ss[k2][:, :], ss[k2][:, :], AF.Exp, r=[ss[k2]], w=[ss[k2]], scale=-0.5)
                        P.stt('dve', on[k2][:, :], osb[k2][:, :], ss[k2][:, 0:1], nwb[:, h * 128:(h + 1) * 128], ALU.mult, ALU.mult,
                              r=[osb[k2], ss[k2], nwb], w=[on[k2]])
                        P.tt('dve', on2[k2][:, :], on[k2][:, :], gs_tm[:, cix, :], ALU.mult, r=[on[k2], gs_tm], w=[on2[k2]])
                        P.mm(bF[:, 0:64], on2[k2][:, :], self.ident[0:64, 0:64], True, True, r=[on2[k2], self.ident], w=[bF])
                        P.copy('act', oTh[:, tok], bF[:, 0:64], r=[bF], w=[oTh])
            P.dma('sp', self.OT[h * 128:(h + 1) * 128, :], oTh[:, :], r=[oTh], w=[self.tOT.sub(h)])
        P.release(mk)
        mk = P.mark()
        self.alloc_wring(3)
        src = P.alloc([128, KC, T], BF16)
        P.dma('sp', src[:, :, :], self.OT[0:D, :].rearrange('(q p) t -> p q t', p=128), r=[self.tOT], w=[src])
        self.outproj(Wout, KC, src, src, 0, 0, T, s)
        P.release(mk)

class _Kind:
    def __init__(self, modT, kind):
        self.node = modT.node
        self.ap = modT.ap[:, kind, :, :]

    def __getitem__(self, k):
        return self.ap[k]


def rope_tables(cfg):
    SEQ, CTX, T = cfg.SEQ, cfg.CTX, cfg.T
    pos = np.arange(SEQ)
    row = (pos // 64).astype(np.float32)
    col = (pos % 64).astype(np.float32)
    inv = (10000.0 ** (-np.arange(16, dtype=np.float32) / 16)).astype(np.float32)
    ang = np.stack([row[:, None] * inv, col[:, None] * inv], axis=1)
    cs, sn = np.cos(ang).astype(np.float32), np.sin(ang).astype(np.float32)
    C = np.ones((128, T), np.float32)
    S = np.zeros((128, T), np.float32)
    for dd in range(64):
        axis, half, pair = dd // 32, (dd % 32) // 16, dd % 16
        for rep in range(2):
            C[rep * 64 + dd, CTX:] = cs[:, axis, pair]
            S[rep * 64 + dd, CTX:] = sn[:, axis, pair] * (-1.0 if half == 0 else 1.0)
    return C, S


def const_tables():
    ident = np.eye(128, dtype=np.float32)
    j = np.arange(128)[:, None]
    q = np.arange(128)[None, :]
    maskp = np.where(j >= q, 0.0, -30000.0).astype(np.float32)
    maskn = np.where(j <= q, 0.0, -30000.0).astype(np.float32)
    t = np.arange(64)[:, None]
    i = np.arange(64)[None, :]
    tri = np.stack([(t <= i), (t >= i), (t > i), (t < i)], axis=1).astype(np.float32)
    return ident, maskp, maskn, tri


def host_inputs(cfg, inp, b0, nseq):
    d = D
    f32 = np.float32
    sl = slice(b0, b0 + nseq)
    xcat = np.concatenate([inp['ctx'][sl], inp['x'][sl]], axis=1)
    xin = np.ascontiguousarray(xcat.transpose(0, 2, 1))
    cv = np.concatenate([inp['c'][sl], inp['c_ctx'][None, :]], axis=0)
    cvec = np.ascontiguousarray(cv.reshape(nseq + 1, KC, 128).transpose(2, 1, 0))
    depth = inp['ada_w'].shape[0]
    ada_bT = np.ascontiguousarray(inp['ada_b'].reshape(depth, 96, 128).transpose(0, 2, 1))
    norm_gT = np.ascontiguousarray(inp['norm_g'].reshape(depth, 4, KC, 128).transpose(0, 1, 3, 2))
    w = inp['attn_w_in']
    perm = np.arange(64).reshape(2, 2, 16)[:, ::-1, :].reshape(64)
    kperm = (np.arange(4)[:, None] * 64 + perm[None, :]).reshape(-1)
    qperm = 512 + (np.arange(32)[:, None] * 64 + perm[None, :]).reshape(-1)
    w_rot = np.ascontiguousarray(w[:, :, np.concatenate([kperm, qperm])])
    C, S = rope_tables(cfg)
    ident, maskp, maskn, tri = const_tables()
    nb = inp['ssd_conv_w'].shape[0]
    conv_wT = np.ascontiguousarray(inp['ssd_conv_w'].reshape(nb, 3, 48, 128).transpose(0, 3, 2, 1))
    conv_bT = np.ascontiguousarray(inp['ssd_conv_b'].reshape(nb, 48, 128).transpose(0, 2, 1))
    m = dict(
        xin=xin, cvec=cvec, ada_w=inp['ada_w'], ada_bT=ada_bT, norm_gT=norm_gT,
        ffn_w_in=inp['ffn_w_in'], ffn_w_out=inp['ffn_w_out'],
        attn_w_in=w, attn_w_rot=w_rot, attn_w_out=inp['attn_w_out'], attn_sink=inp['attn_sink'],
        ssd_w_in=inp['ssd_w_in'], ssd_conv_wT=conv_wT, ssd_conv_bT=conv_bT,
        ssd_dt_bias=np.ascontiguousarray(inp['ssd_dt_bias'].reshape(nb, 128)),
        ssd_a_log=np.ascontiguousarray(inp['ssd_a_log'].reshape(nb, 128)),
        ssd_d=inp['ssd_d'], ssd_norm_w=inp['ssd_norm_w'], ssd_w_out=inp['ssd_w_out'],
        hgrn_w_in=inp['hgrn_w_in'], hgrn_lb=inp['hgrn_lb'], hgrn_norm_w=inp['hgrn_norm_w'], hgrn_w_out=inp['hgrn_w_out'],
        ropeC=C, ropeS=S, c_ident=ident, c_maskp=maskp, c_maskn=maskn, c_tri=tri,
    )
    return {k: np.ascontiguousarray(np.asarray(v, dtype=f32)) for k, v in m.items()}


def build_program(cfg):
    b = Builder(cfg)
    P = b.P
    with b.nc.allow_low_precision("bf16 matmul operands, fp32 accumulation"):
        nl = len(cfg.layers)
        cnt = {0: 0, 1: 0, 2: 0}
        for n_, li in enumerate(cfg.layers):
            kind = li % NMIX
            j = cnt[kind]
            cnt[kind] += 1
            b.adaln(n_)
            for s in range(cfg.NSEQ):
                first = (n_ == 0)
                if kind == 0:
                    b.attn_phase(n_, j, s, first, 0)
                elif kind == 1:
                    b.ssd_phase(n_, j, s, first, 0)
                else:
                    b.hgrn_phase(n_, j, s, first, li)
                b.ffn_phase(n_, s, n_ == nl - 1)
        stats = P.finish()
    return b, stats


_CACHE = {}
GROUPS = ((0,), (1,), (2,), (3,))


def _slice_inputs(inp, layers):
    ls = list(layers)
    out = dict(inp)
    for k in ('ada_w', 'ada_b', 'norm_g', 'ffn_w_in', 'ffn_w_out'):
        out[k] = inp[k][ls]
    for kind, keys in ((0, ('attn_w_in', 'attn_w_out', 'attn_sink')),
                       (1, ('ssd_w_in', 'ssd_conv_w', 'ssd_conv_b', 'ssd_dt_bias', 'ssd_a_log', 'ssd_d', 'ssd_norm_w', 'ssd_w_out')),
                       (2, ('hgrn_w_in', 'hgrn_norm_w', 'hgrn_w_out'))):
        js = [li // NMIX for li in ls if li % NMIX == kind]
        for k in keys:
            out[k] = inp[k][js] if js else inp[k][:1]
    return out


def kernel(**inputs):
    n_cores = 8
    inp = {k: np.asarray(v) for k, v in inputs.items()}
    B, SEQ = inp['x'].shape[0], inp['x'].shape[1]
    CTX = inp['ctx'].shape[1]
    nseq = B // n_cores
    xs = None
    for grp in GROUPS:
        cfg = Cfg(SEQ=SEQ, CTX=CTX, NSEQ=nseq, layers=grp, depth=len(grp), lbdepth=inp['hgrn_lb'].shape[1])
        key = tuple(li % NMIX if (li % NMIX) != 2 else ('h', li) for li in grp)
        if key not in _CACHE:
            _CACHE[key] = build_program(cfg)
        b, stats = _CACHE[key]
        sl = _slice_inputs(inp, grp)
        in_maps = []
        for core in range(n_cores):
            m = host_inputs(cfg, sl, core * nseq, nseq)
            if xs is not None:
                m['xin'] = xs[core]
            in_maps.append(m)
        res = run_bass_kernel_spmd(b.nc, in_maps, core_ids=list(range(n_cores)))
        xs = [np.ascontiguousarray(res.results[core]["xout"]) for core in range(n_cores)]
    outs = [np.ascontiguousarray(xs[core][:, :, CTX:].transpose(0, 2, 1)) for core in range(n_cores)]
    return np.concatenate(outs, axis=0).astype(np.float32)
```

```python
import numpy as np
import ml_dtypes
from concourse.bass_utils import run_bass_kernel_spmd
import concourse.bass as bass
import concourse.mybir as mybir

F32 = mybir.dt.float32
BF16 = mybir.dt.bfloat16
ALU = mybir.AluOpType
AF = mybir.ActivationFunctionType
AX = mybir.AxisListType
ENGS = ['pe', 'act', 'dve', 'pool', 'sp']
EPOCH = 30000
DMA_R = 8
DMA_EPOCH = 1800
CUT_EVERY = 10 ** 9


class Node:
    __slots__ = ('lw', 'rd', 'parent', 'kids')

    def __init__(self, parent=None):
        self.lw = None
        self.rd = []
        self.parent = parent
        self.kids = []


class Ins:
    __slots__ = ('eng', 'fn', 'idx', 'waits', 'mile', 'mnum', 'dma', 'dsem', 'dval', 'blk')

    def __init__(self, eng, fn, idx, dma):
        self.eng = eng
        self.fn = fn
        self.idx = idx
        self.waits = []
        self.mile = False
        self.mnum = 0
        self.dma = dma
        self.dsem = None
        self.dval = 0
        self.blk = 0


class Tile:
    def __init__(self, ap, parent_node=None):
        self.ap = ap
        self.node = Node(parent_node)
        if parent_node is not None:
            parent_node.kids.append(self.node)
        self.subs = {}

    def __getitem__(self, k):
        return self.ap[k]

    def sub(self, key):
        s = self.subs.get(key)
        if s is None:
            s = Tile(self.ap, self.node)
            self.subs[key] = s
        return s


class Prog:
    def __init__(self, nc, arena_bytes=204 * 1024):
        self.nc = nc
        self.streams = {e: [] for e in ENGS}
        self.seen = {e: {} for e in ENGS}
        self.dseen = {e: {} for e in ENGS}
        self.ndma = {e: 0 for e in ENGS}
        self.dma_hist = {e: [] for e in ENGS}
        self.arena_bytes = arena_bytes
        self.top = 0
        self.ghosts = []
        self.live = []
        self._arena_cm = nc.sbuf_tensor('arena', [128, arena_bytes // 2], BF16)
        self.arena = self._arena_cm.__enter__()
        self._psum_cms = [nc.psum_tensor(f'psb{i}', [128, 512], F32) for i in range(8)]
        self.banks = [Tile(cm.__enter__()[:, :]) for cm in self._psum_cms]
        self.dsems = {}
        self.final_waits = []
        self.blk = 0
        self.since_cut = 0
        self.cut_every = CUT_EVERY

    def mark(self):
        return (self.top, len(self.live))

    def release(self, mark):
        top, nlive = mark
        for (s, e, t) in self.live[nlive:]:
            deps = []
            self._collect(t.node, deps)
            self.ghosts.append((s, e, deps))
        del self.live[nlive:]
        self.top = top

    def _collect(self, node, deps):
        if node.lw is not None:
            deps.append(node.lw)
        deps.extend(node.rd)
        for k in node.kids:
            self._collect(k, deps)

    def alloc(self, shape, dtype):
        esz = 4 if dtype == F32 else 2
        free = int(np.prod(shape[1:]))
        nbytes = (free * esz + 31) // 32 * 32
        s = self.top
        e = s + nbytes
        assert e <= self.arena_bytes, f"SBUF arena overflow {e} > {self.arena_bytes}"
        self.top = e
        ap = self.arena[0:shape[0], s // 2:(s + free * esz) // 2]
        if dtype != BF16:
            ap = ap.bitcast(dtype)
        if len(shape) == 3:
            ap = ap.rearrange('p (a b) -> p a b', b=shape[2])
        elif len(shape) == 4:
            ap = ap.rearrange('p (a b c) -> p a b c', b=shape[2], c=shape[3])
        t = Tile(ap)
        inherited = []
        ng = []
        for (gs, ge, deps) in self.ghosts:
            if gs < e and s < ge:
                inherited.extend(deps)
                if s <= gs and ge <= e:
                    continue
            ng.append((gs, ge, deps))
        self.ghosts = ng
        t.node.rd = list(dict.fromkeys(inherited))
        self.live.append((s, e, t))
        return t

    def _resolve(self, ins, deps, soft=()):
        e = ins.eng
        best = {}
        for hard, lst in ((True, deps), (False, soft)):
          for d in lst:
            if d is ins:
                continue
            if d.dma:
                key = ('d', d.dsem)
                if key not in best or best[key].dval < d.dval:
                    best[key] = d
            else:
                if d.eng == e and (e == 'pe' or not hard):
                    continue
                key = ('e', d.eng)
                if key not in best or best[key].idx < d.idx:
                    best[key] = d
        for key, d in best.items():
            if d.dma:
                if self.dseen[e].get(d.dsem, 0) >= d.dval:
                    continue
                self.dseen[e][d.dsem] = d.dval
                ins.waits.append(d)
            else:
                if self.seen[e].get(d.eng, -1) >= d.idx:
                    continue
                self.seen[e][d.eng] = d.idx
                d.mile = True
                ins.waits.append(d)

    def op(self, eng, fn, r=(), w=(), dma=False):
        st = self.streams[eng]
        ins = Ins(eng, fn, len(st), dma)
        self.since_cut += 1
        if self.since_cut >= self.cut_every:
            self.since_cut = 0
            self.blk += 1
        ins.blk = self.blk
        deps = []
        if dma:
            i = self.ndma[eng]
            self.ndma[eng] = i + 1
            slot = i % DMA_R
            use = i // DMA_R
            ins.dsem = (eng, slot, use // DMA_EPOCH)
            ins.dval = 16 * (use % DMA_EPOCH + 1)
            hist = self.dma_hist[eng]
            if i >= DMA_R:
                deps.append(hist[i - DMA_R])
            hist.append(ins)
        for t in r:
            n = t.node
            if n.lw is not None:
                deps.append(n.lw)
            if n.parent is not None and n.parent.lw is not None:
                deps.append(n.parent.lw)
            for k in n.kids:
                if k.lw is not None:
                    deps.append(k.lw)
        soft = []
        for t in w:
            n = t.node
            nodes = [n] + n.kids + ([n.parent] if n.parent is not None else [])
            for m in nodes:
                if m.lw is not None:
                    deps.append(m.lw)
                soft.extend(m.rd)
        self._resolve(ins, deps, soft)
        for t in r:
            rd = t.node.rd
            if not dma:
                rd[:] = [x for x in rd if x.dma or x.eng != eng]
            rd.append(ins)
        for t in w:
            n = t.node
            n.lw = ins
            n.rd = []
            for k in n.kids:
                k.lw = ins
                k.rd = []
        st.append(ins)
        return ins

    def dma(self, eng, out, in_, r=(), w=(), **kw):
        return self.op(eng, lambda e: e.dma_start(out=out, in_=in_, **kw), r=r, w=w, dma=True)


    def act(self, out, in_, func, r=(), w=(), **kw):
        return self.op('act', lambda e: e.activation(out, in_, func, **kw), r=r, w=w)

    def tt(self, eng, out, a, b, op, r=(), w=()):
        return self.op(eng, lambda e: e.tensor_tensor(out, a, b, op), r=r, w=w)

    def ts(self, eng, out, a, s1, s2, op0, op1=None, r=(), w=()):
        if op1 is None:
            return self.op(eng, lambda e: e.tensor_scalar(out, a, s1, None, op0), r=r, w=w)
        return self.op(eng, lambda e: e.tensor_scalar(out, a, s1, s2, op0, op1), r=r, w=w)

    def stt(self, eng, out, in0, scalar, in1, op0, op1, r=(), w=()):
        return self.op('dve', lambda e: e.scalar_tensor_tensor(out, in0, scalar, in1, op0, op1), r=r, w=w)

    def copy(self, eng, out, in_, r=(), w=()):
        if eng == 'act':
            return self.op('act', lambda e: e.activation(out, in_, AF.Copy), r=r, w=w)
        return self.op(eng, lambda e: e.tensor_copy(out, in_), r=r, w=w)

    def mm(self, out, lhsT, rhs, start, stop, r=(), w=()):
        return self.op('pe', lambda e: e.matmul(out, lhsT, rhs, start=start, stop=stop), r=r, w=w)

    def memset(self, eng, ap, val, w=()):
        return self.op(eng, lambda e: e.memset(ap, val), w=w)

    def finish(self, final_tiles=()):
        nc = self.nc
        fin = Ins('sp', None, len(self.streams['sp']), False)
        fdeps = []
        for e in ENGS:
            fdeps.extend(self.dma_hist[e][-DMA_R:])
            if e != 'sp' and self.streams[e]:
                fdeps.append(self.streams[e][-1])
        self._resolve(fin, fdeps)
        self.streams['sp'].append(fin)
        nmile = {}
        for e in ENGS:
            m = 0
            for ins in self.streams[e]:
                if ins.mile:
                    m += 1
                    ins.mnum = m
            nmile[e] = m
        sem_cms = []

        def newsem(name):
            cm = nc.semaphore(name)
            sem_cms.append(cm)
            return cm.__enter__()

        esems = {e: [newsem(f's_{e}_{k}') for k in range((nmile[e] + EPOCH - 1) // EPOCH)] for e in ENGS}
        dkeys = set()
        for e in ENGS:
            for d in self.dma_hist[e]:
                dkeys.add(d.dsem)
        dsems = {k: newsem(f'd_{k[0]}_{k[1]}_{k[2]}') for k in sorted(dkeys)}
        self.n_sems = len(sem_cms)

        fin.blk = self.blk
        pos = {e: 0 for e in ENGS}

        def emit(e, eh, blk):
            st = self.streams[e]
            i = pos[e]
            while i < len(st) and st[i].blk == blk:
                ins = st[i]
                i += 1
                for d in ins.waits:
                    if d.dma:
                        eh.wait_ge(dsems[d.dsem], d.dval)
                    else:
                        m = d.mnum - 1
                        eh.wait_ge(esems[d.eng][m // EPOCH], m % EPOCH + 1)
                if ins.fn is None:
                    continue
                bi = ins.fn(eh)
                if ins.dma:
                    bi.then_inc(dsems[ins.dsem], 16)
                elif ins.mile:
                    m = ins.mnum - 1
                    bi.then_inc(esems[e][m // EPOCH], 1)
            pos[e] = i

        for blk in range(self.blk + 1):
            with nc.Block() as block:
                @block.tensor
                def _(eh):
                    emit('pe', eh, blk)

                @block.scalar
                def _(eh):
                    emit('act', eh, blk)

                @block.vector
                def _(eh):
                    emit('dve', eh, blk)

                @block.gpsimd
                def _(eh):
                    emit('pool', eh, blk)

                @block.sync
                def _(eh):
                    emit('sp', eh, blk)
        for cm in reversed(sem_cms):
            cm.__exit__(None, None, None)
        for cm in reversed(self._psum_cms):
            cm.__exit__(None, None, None)
        self._arena_cm.__exit__(None, None, None)
        return {e: len(self.streams[e]) for e in ENGS}, nmile


D = 2048
KC = 16
FF = 5632
FC = 44
EPS = 1e-6
NMIX = 3


class Cfg:
    def __init__(self, SEQ=2048, CTX=256, NSEQ=2, layers=(0, 1, 2, 3), depth=4, fblk=512, lbdepth=4):
        self.lbdepth = lbdepth
        self.SEQ, self.CTX, self.NSEQ = SEQ, CTX, NSEQ
        self.T = SEQ + CTX
        self.NJ = NSEQ + 1
        self.layers = tuple(layers)
        self.depth = depth
        self.fblk = fblk
        self.NCH = self.T // 64
        self.NCB = CTX // 128
        self.NLB = SEQ // 128


def segs(t0, t1, CTX, maxn=512):
    out = []
    pieces = []
    if t0 < CTX:
        pieces.append((t0, min(t1, CTX), True))
    if t1 > CTX:
        pieces.append((max(t0, CTX), t1, False))
    for a, b, isctx in pieces:
        n = b - a
        k = -(-n // maxn)
        sz = -(-n // k)
        p = a
        while p < b:
            e = min(b, p + sz)
            out.append((p, e - p, isctx))
            p = e
    return out


class Builder:
    def __init__(self, cfg):
        self.cfg = cfg
        nc = bass.Bass("TRN2", target_bir_lowering=False)
        self.nc = nc
        self.P = Prog(nc)
        c = cfg
        T, NJ = c.T, c.NJ

        def inp(name, shape, dt=F32):
            return nc.dram_tensor(name, list(shape), dt, kind="ExternalInput").ap()

        self.xin = inp("xin", [c.NSEQ, D, T])
        self.cvec = inp("cvec", [128, KC, NJ])
        self.ada_w = inp("ada_w", [c.depth, D, 6 * D])
        self.ada_bT = inp("ada_bT", [c.depth, 128, 96])
        self.norm_gT = inp("norm_gT", [c.depth, 4, 128, KC])
        self.ffn_w_in = inp("ffn_w_in", [c.depth, D, 2 * FF])
        self.ffn_w_out = inp("ffn_w_out", [c.depth, FF, D])
        na = len(range(0, c.depth, NMIX))
        nb = len(range(1, c.depth, NMIX))
        ncx = len(range(2, c.depth, NMIX))
        self.attn_w_in = inp("attn_w_in", [na, D, 2560])
        self.attn_w_rot = inp("attn_w_rot", [na, D, 2304])
        self.attn_w_out = inp("attn_w_out", [na, D, D])
        self.attn_sink = inp("attn_sink", [na, 32])
        self.ssd_w_in = inp("ssd_w_in", [max(nb, 1), D, 10368])
        self.ssd_conv_wT = inp("ssd_conv_wT", [max(nb, 1), 128, 48, 3])
        self.ssd_conv_bT = inp("ssd_conv_bT", [max(nb, 1), 128, 48])
        self.ssd_dt_bias = inp("ssd_dt_bias", [max(nb, 1), 128])
        self.ssd_a_log = inp("ssd_a_log", [max(nb, 1), 128])
        self.ssd_d = inp("ssd_d", [max(nb, 1), 64])
        self.ssd_norm_w = inp("ssd_norm_w", [max(nb, 1), 4096])
        self.ssd_w_out = inp("ssd_w_out", [max(nb, 1), 4096, D])
        self.hgrn_w_in = inp("hgrn_w_in", [max(ncx, 1), D, 10240])
        self.hgrn_lb = inp("hgrn_lb", [2, c.lbdepth, D])
        self.hgrn_norm_w = inp("hgrn_norm_w", [max(ncx, 1), D])
        self.hgrn_w_out = inp("hgrn_w_out", [max(ncx, 1), D, D])
        self.ropeC = inp("ropeC", [128, T])
        self.ropeS = inp("ropeS", [128, T])
        self.c_ident = inp("c_ident", [128, 128])
        self.c_maskp = inp("c_maskp", [128, 128])
        self.c_maskn = inp("c_maskn", [128, 128])
        self.c_tri = inp("c_tri", [64, 4, 64])
        self.xout = nc.dram_tensor("xout", [c.NSEQ, D, T], F32, kind="ExternalOutput").ap()
        self.X = nc.dram_tensor("Xs", [c.NSEQ, D, T], F32).ap()
        self.Y = nc.dram_tensor("Ys", [c.NSEQ, D, T], F32).ap()
        self.QT = nc.dram_tensor("QTs", [D, T], BF16).ap()
        self.OT = nc.dram_tensor("OTs", [4096, T], BF16).ap()
        self.XC = nc.dram_tensor("XCs", [6144, T], BF16).ap()
        self.DT = nc.dram_tensor("DTs", [T, 128], F32).ap()
        self.ZS = nc.dram_tensor("ZSs", [T, 4096], BF16).ap()
        self.YB = nc.dram_tensor("YBs", [T, 512], F32).ap()
        self.tX = [Tile(self.X[s]) for s in range(c.NSEQ)]
        self.tY = [Tile(self.Y[s]) for s in range(c.NSEQ)]
        self.tXin = Tile(self.xin)
        self.tXout = Tile(self.xout)
        self.tQT, self.tOT, self.tXC = Tile(self.QT), Tile(self.OT), Tile(self.XC)
        self.tDT, self.tZS, self.tYB = Tile(self.DT), Tile(self.ZS), Tile(self.YB)
        self.Wbin = nc.dram_tensor("Wbin_s", [D, 2 * FF], BF16).ap()
        self.Wbout = nc.dram_tensor("Wbout_s", [FF, D], BF16).ap()
        self.tWbin, self.tWbout = Tile(self.Wbin), Tile(self.Wbout)
        self.wcount = 0
        self.dbg = {}
        self.dbg_on = False
        self.setup_consts()

    def setup_consts(self):
        P, c = self.P, self.cfg
        self.ident = P.alloc([128, 128], BF16)
        P.dma('pool', self.ident[:, :], self.c_ident, w=[self.ident])
        self.ones = P.alloc([128, 128], BF16)
        P.memset('dve', self.ones[:, :], 1.0, w=[self.ones])
        self.identf = P.alloc([128, 128], F32)
        P.dma('sp', self.identf[:, :], self.c_ident, w=[self.identf])
        self.onesf = P.alloc([128, 128], F32)
        P.memset('dve', self.onesf[:, :], 1.0, w=[self.onesf])
        self.trif = P.alloc([64, 4, 64], F32)
        P.dma('sp', self.trif[:, :, :], self.c_tri, w=[self.trif])
        self.trib = P.alloc([64, 4, 64], BF16)
        P.dma('pool', self.trib[:, :, :], self.c_tri, w=[self.trib])
        cv = P.alloc([128, KC, c.NJ], F32)
        P.dma('sp', cv[:, :, :], self.cvec, w=[cv])
        self.csil = P.alloc([128, KC, c.NJ], BF16)
        P.act(self.csil[:, :, :], cv[:, :, :], AF.Silu, r=[cv], w=[self.csil])
        self.modT = P.alloc([128, 6, KC, c.NJ], F32)
        self.gT = P.alloc([128, 4, KC], F32)
        self.A1 = P.alloc([128, KC, c.NJ], F32)
        self.G1 = P.alloc([128, KC, c.NJ], F32)
        self.A2 = P.alloc([128, KC, c.NJ], F32)
        self.G2 = P.alloc([128, KC, c.NJ], F32)
        self.adab = P.alloc([128, 96], F32)
        self.wring = []

    def dump(self, name, tile, ap, shape, dt=F32):
        if not self.dbg_on:
            return
        d = self.nc.dram_tensor("dbg_" + name, list(shape), dt, kind="ExternalOutput").ap()
        t = Tile(d)
        self.dbg[name] = t
        self.P.dma('pool' if dt != tile.ap.dtype else 'sp', d, ap, r=[tile], w=[t])

    def alloc_wring(self, n=3):
        self.wring = [self.P.alloc([128, 8192], BF16) for _ in range(n)]

    def wtile(self):
        t = self.wring[self.wcount % len(self.wring)]
        self.wcount += 1
        return t

    def ffn_prepare(self, li):
        P = self.P
        src = self.ffn_w_in[li].rearrange('(a p) m -> p a m', p=128)
        dst = self.Wbin.rearrange('(a p) m -> p a m', p=128)
        for a in range(KC):
            for hh in range(2):
                P.dma('pool', dst[:, a, hh * FF:(hh + 1) * FF], src[:, a, hh * FF:(hh + 1) * FF], w=[self.tWbin.sub(a * 2 + hh)])
        src = self.ffn_w_out[li].rearrange('(a p) m -> p a m', p=128)
        dst = self.Wbout.rearrange('(a p) m -> p a m', p=128)
        for a in range(FC):
            P.dma('pool', dst[:, a, :], src[:, a, :], w=[self.tWbout.sub(a)])

    def load_w(self, Wd, kchunks, c0, ncols, wt=None, eng='pool', rt=()):
        P = self.P
        if wt is None:
            wt = self.wtile()
        assert kchunks * ncols <= 8192
        v = wt[:, 0:kchunks * ncols].rearrange('p (k m) -> p k m', m=ncols)
        src = Wd.rearrange('(k p) m -> p k m', p=128)
        step = max(1, 2048 // ncols)
        for k0 in range(0, kchunks, step):
            k1 = min(kchunks, k0 + step)
            P.dma(eng, v[:, k0:k1, :], src[:, k0:k1, c0:c0 + ncols], r=list(rt), w=[wt])
        return wt, v

    def adaln(self, li):
        P, c = self.P, self.cfg
        NJ = c.NJ
        mk = P.mark()
        self.alloc_wring(3)
        P.dma('sp', self.adab[:, :], self.ada_bT[li], w=[self.adab])
        P.dma('sp', self.gT[:, :, :], self.norm_gT[li].rearrange('j p k -> p j k'), w=[self.gT])
        W = self.ada_w[li]
        n = 0
        for g in range(6 * D // 512):
            wt, v = self.load_w(W, KC, g * 512, 512)
            bank = P.banks[4 + (n % 2)]
            n += 1
            for mc in range(4):
                for kc in range(KC):
                    P.mm(bank[:, mc * NJ:(mc + 1) * NJ], v[:, kc, mc * 128:(mc + 1) * 128], self.csil[:, kc, :],
                         kc == 0, kc == KC - 1, r=[wt, self.csil], w=[bank])
            ch0 = g * 4
            kind, chunk = ch0 // KC, ch0 % KC
            P.tt('dve', self.modT[:, kind, chunk:chunk + 4, :],
                 bank[:, 0:4 * NJ].rearrange('p (m j) -> p m j', j=NJ),
                 self.adab[:, ch0:ch0 + 4].unsqueeze(2).broadcast_to([128, 4, NJ]), ALU.add,
                 r=[bank, self.adab], w=[self.modT])
        m = self.modT

        def gb(j):
            return self.gT[:, j, :].unsqueeze(2).broadcast_to([128, KC, NJ])
        P.stt('dve', self.A1[:, :, :], m[:, 1, :, :], 1.0, gb(0), ALU.add, ALU.mult, r=[m, self.gT], w=[self.A1])
        P.tt('dve', self.G1[:, :, :], m[:, 2, :, :], gb(1), ALU.mult, r=[m, self.gT], w=[self.G1])
        P.stt('dve', self.A2[:, :, :], m[:, 4, :, :], 1.0, gb(2), ALU.add, ALU.mult, r=[m, self.gT], w=[self.A2])
        P.tt('dve', self.G2[:, :, :], m[:, 5, :, :], gb(3), ALU.mult, r=[m, self.gT], w=[self.G2])
        P.release(mk)

    def rstd_of(self, src, srct, n, out_rstd, sqring, bank, nfeat=D, kchunks=KC):
        P = self.P
        for kc in range(kchunks):
            sq = sqring[kc % len(sqring)]
            P.act(sq[:, 0:n], src[:, kc, :], AF.Square, r=[srct], w=[sq])
            P.mm(bank[:, 0:n], self.ones[:, :], sq[:, 0:n], kc == 0, kc == kchunks - 1, r=[sq, self.ones], w=[bank])
        P.act(out_rstd[:, 0:n], bank[:, 0:n], AF.Ln, r=[bank], w=[out_rstd], bias=EPS, scale=1.0 / nfeat)
        P.act(out_rstd[:, 0:n], out_rstd[:, 0:n], AF.Exp, r=[out_rstd], w=[out_rstd], scale=-0.5)

    def modulate(self, xt, xv, n, rstd, A, Bm, j, uout, uoutt, tmp):
        P = self.P
        P.tt('dve', tmp[:, :, 0:n], xv, rstd[:, 0:n].unsqueeze(1).broadcast_to([128, KC, n]), ALU.mult,
             r=[xt, rstd], w=[tmp])
        for kc in range(KC):
            if kc % 2 == 0:
                P.act(uout[:, kc, :], tmp[:, kc, 0:n], AF.Identity, r=[tmp, A, Bm], w=[uoutt],
                      bias=Bm[:, kc, j:j + 1], scale=A[:, kc, j:j + 1])
            else:
                P.ts('pool', uout[:, kc, :], tmp[:, kc, 0:n], A[:, kc, j:j + 1], Bm[:, kc, j:j + 1], ALU.mult, ALU.add,
                     r=[tmp, A, Bm], w=[uoutt])

    def prenorm_seq(self, s, first, uT):
        P, c = self.P, self.cfg
        mk = P.mark()
        xb = [P.alloc([128, KC, 256], F32) for _ in range(2)]
        sqr = [P.alloc([128, 512], BF16) for _ in range(3)]
        rstd = [P.alloc([128, 256], F32) for _ in range(2)]
        src = self.xin[s] if first else self.X[s]
        srct = self.tXin if first else self.tX[s]
        for i, (t0, n, isctx) in enumerate(segs(0, c.T, c.CTX, 256)):
            x = xb[i % 2]
            j = c.NSEQ if isctx else s
            P.dma('sp', x[:, :, 0:n], src.rearrange('(k p) t -> p k t', p=128)[:, :, t0:t0 + n], r=[srct], w=[x])
            if first:
                P.dma('sp', self.X[s].rearrange('(k p) t -> p k t', p=128)[:, :, t0:t0 + n], x[:, :, 0:n], r=[x], w=[self.tX[s]])
            r = rstd[i % 2]
            self.rstd_of(x[:, :, 0:n], x, n, r, sqr, P.banks[6 + (i % 2)])
            self.modulate(x, x[:, :, 0:n], n, r, self.A1, _Kind(self.modT, 0), j, uT[:, :, t0:t0 + n], uT, x)
        P.release(mk)

    def ffn_phase(self, li, s, last):
        P, c = self.P, self.cfg
        Xd = self.X[s].rearrange('(k p) t -> p k t', p=128)
        Yd = self.Y[s].rearrange('(k p) t -> p k t', p=128)
        Od = self.xout[s].rearrange('(k p) t -> p k t', p=128)
        Win = self.ffn_w_in[li]
        Wout = self.ffn_w_out[li]
        B2 = _Kind(self.modT, 3)
        mk0 = P.mark()
        self.alloc_wring(3)
        for t0 in range(0, c.T, c.fblk):
            t1 = min(c.T, t0 + c.fblk)
            nb = t1 - t0
            sg = segs(t0, t1, c.CTX, 512)
            sgm = segs(t0, t1, 0, 512)
            mk = P.mark()
            x = P.alloc([128, KC, nb], F32)
            y = P.alloc([128, KC, nb], BF16)
            u = P.alloc([128, KC, nb], BF16)
            sqr = [P.alloc([128, 512], BF16) for _ in range(3)]
            rs = [P.alloc([128, 512], F32) for _ in range(2)]
            mk2 = P.mark()
            ym = P.alloc([128, KC, nb], F32)
            for i, (p, n, isctx) in enumerate(sg):
                j = c.NSEQ if isctx else s
                o = p - t0
                P.dma('sp', x[:, :, o:o + n], Xd[:, :, p:p + n], r=[self.tX[s]], w=[x])
                P.dma('sp', ym[:, :, o:o + n], Yd[:, :, p:p + n], r=[self.tY[s]], w=[ym])
                r0 = rs[0]
                self.rstd_of(ym[:, :, o:o + n], ym, n, r0, sqr, P.banks[6])
                P.tt('dve', ym[:, :, o:o + n], ym[:, :, o:o + n], r0[:, 0:n].unsqueeze(1).broadcast_to([128, KC, n]), ALU.mult,
                     r=[ym, r0], w=[ym])
                for kc in range(KC):
                    P.stt('dve' if kc % 2 else 'pool', x[:, kc, o:o + n], ym[:, kc, o:o + n], self.G1[:, kc, j:j + 1], x[:, kc, o:o + n],
                          ALU.mult, ALU.add, r=[ym, self.G1, x], w=[x])
                P.dma('sp', Xd[:, :, p:p + n], x[:, :, o:o + n], r=[x], w=[self.tX[s]])
                r1 = rs[1]
                self.rstd_of(x[:, :, o:o + n], x, n, r1, sqr, P.banks[7])
                P.tt('dve', ym[:, :, o:o + n], x[:, :, o:o + n], r1[:, 0:n].unsqueeze(1).broadcast_to([128, KC, n]), ALU.mult,
                     r=[x, r1], w=[ym])
                for kc in range(KC):
                    if kc % 2 == 0:
                        P.act(u[:, kc, o:o + n], ym[:, kc, o:o + n], AF.Identity, r=[ym, self.A2, B2], w=[u],
                              bias=B2[:, kc, j:j + 1], scale=self.A2[:, kc, j:j + 1])
                    else:
                        P.ts('pool', u[:, kc, o:o + n], ym[:, kc, o:o + n], self.A2[:, kc, j:j + 1], B2[:, kc, j:j + 1],
                             ALU.mult, ALU.add, r=[ym, self.A2, B2], w=[u])
            P.release(mk2)
            h = P.alloc([128, FC, nb], BF16)
            sgt = [P.alloc([128, 512], F32) for _ in range(2)]
            nbk = 0
            for m0 in range(0, FC, 2):
                wt = self.wtile()
                v = wt[:, 0:KC * 512].rearrange('p (k m) -> p k m', m=512)
                srcw = self.Wbin.rearrange('(k p) m -> p k m', p=128)
                for k0 in range(0, KC, 8):
                    P.dma('sp', v[:, k0:k0 + 8, 0:256], srcw[:, k0:k0 + 8, m0 * 128:m0 * 128 + 256], r=[self.tWbin], w=[wt])
                    P.dma('sp', v[:, k0:k0 + 8, 256:512], srcw[:, k0:k0 + 8, FF + m0 * 128:FF + m0 * 128 + 256], r=[self.tWbin], w=[wt])
                for mm_ in range(2):
                    m = m0 + mm_
                    for (p, n, isctx) in sgm:
                        o = p - t0
                        bg = P.banks[(nbk % 2) * 2]
                        bu = P.banks[(nbk % 2) * 2 + 1]
                        nbk += 1
                        for kc in range(KC):
                            P.mm(bg[:, 0:n], v[:, kc, mm_ * 128:(mm_ + 1) * 128], u[:, kc, o:o + n], kc == 0, kc == KC - 1,
                                 r=[wt, u], w=[bg])
                        for kc in range(KC):
                            P.mm(bu[:, 0:n], v[:, kc, 256 + mm_ * 128:256 + (mm_ + 1) * 128], u[:, kc, o:o + n], kc == 0, kc == KC - 1,
                                 r=[wt, u], w=[bu])
                        st_ = sgt[nbk % 2]
                        P.act(st_[:, 0:n], bg[:, 0:n], AF.Silu, r=[bg], w=[st_])
                        P.tt('dve', h[:, m, o:o + n], st_[:, 0:n], bu[:, 0:n], ALU.mult, r=[st_, bu], w=[h])
            for mo in range(KC):
                wt, v = self.load_w(self.Wbout, FC, mo * 128, 128, eng='sp', rt=[self.tWbout])
                for (p, n, isctx) in sgm:
                    o = p - t0
                    bk = P.banks[nbk % 4]
                    nbk += 1
                    for kc in range(FC):
                        P.mm(bk[:, 0:n], v[:, kc, :], h[:, kc, o:o + n], kc == 0, kc == FC - 1, r=[wt, h], w=[bk])
                    P.copy('act', y[:, mo, o:o + n], bk[:, 0:n], r=[bk], w=[y])
            P.release(mk2)
            tmp = P.alloc([128, KC, 512], F32)
            for i, (p, n, isctx) in enumerate(sg):
                j = c.NSEQ if isctx else s
                o = p - t0
                r0 = rs[0]
                self.rstd_of(y[:, :, o:o + n], y, n, r0, sqr, P.banks[6])
                P.tt('dve', tmp[:, :, 0:n], y[:, :, o:o + n], r0[:, 0:n].unsqueeze(1).broadcast_to([128, KC, n]), ALU.mult,
                     r=[y, r0], w=[tmp])
                for kc in range(KC):
                    P.stt('dve' if kc % 2 else 'pool', x[:, kc, o:o + n], tmp[:, kc, 0:n], self.G2[:, kc, j:j + 1], x[:, kc, o:o + n],
                          ALU.mult, ALU.add, r=[tmp, self.G2, x], w=[x])
                if last:
                    P.dma('sp', Od[:, :, p:p + n], x[:, :, o:o + n], r=[x], w=[self.tXout])
                else:
                    P.dma('sp', Xd[:, :, p:p + n], x[:, :, o:o + n], r=[x], w=[self.tX[s]])
            P.release(mk)
        P.release(mk0)


    def outproj(self, Wd, kch, src, srct, col0, tok0, ntok, s):
        P = self.P
        gcols = min(512, (8192 // kch) // 128 * 128)
        mk = P.mark()
        ys = [P.alloc([128, 512], F32) for _ in range(3)]
        nb = 0
        sg = segs(0, ntok, 0, 512)
        for m0 in range(0, D, gcols):
            wt, v = self.load_w(Wd, kch, m0, gcols)
            for mc in range(gcols // 128):
                for (p, n, _) in sg:
                    bk = P.banks[nb % 4]
                    y = ys[nb % 3]
                    nb += 1
                    for kc in range(kch):
                        P.mm(bk[:, 0:n], v[:, kc, mc * 128:(mc + 1) * 128], src[:, kc, col0 + p:col0 + p + n], kc == 0, kc == kch - 1,
                             r=[wt, srct], w=[bk])
                    P.copy('act' if nb % 2 else 'dve', y[:, 0:n], bk[:, 0:n], r=[bk], w=[y])
                    P.dma('sp', self.Y[s][m0 + mc * 128:m0 + (mc + 1) * 128, tok0 + p:tok0 + p + n], y[:, 0:n], r=[y], w=[self.tY[s]])
        P.release(mk)

    def attn_phase(self, li, ja, s, first, _x):
        P, c = self.P, self.cfg
        T, NCB = c.T, c.NCB
        NTT = T // 128
        Win, Wrot, Wout = self.attn_w_in[ja], self.attn_w_rot[ja], self.attn_w_out[ja]
        mk = P.mark()
        kT = P.alloc([128, 4, T], BF16)
        vt = P.alloc([128, NTT, 4 * 65], BF16)
        es = P.alloc([128, 32], F32)
        P.dma('sp', es[:, :], self.attn_sink[ja:ja + 1, :].broadcast_to([128, 32]), w=[es])
        P.act(es[:, :], es[:, :], AF.Exp, r=[es], w=[es])
        P.memset('dve', vt[:, :, :], 1.0, w=[vt])
        mk1 = P.mark()
        self.alloc_wring(3)
        uT = P.alloc([128, KC, T], BF16)
        self.prenorm_seq(s, first, uT)
        rC = P.alloc([128, T], F32)
        rS = P.alloc([128, T], F32)
        P.dma('sp', rC[:, :], self.ropeC, w=[rC])
        P.dma('sp', rS[:, :], self.ropeS, w=[rS])
        sg = segs(0, T, 0, 512)
        t1r = [P.alloc([128, 512], F32) for _ in range(2)]
        t2r = [P.alloc([128, 512], F32) for _ in range(2)]
        qs = [P.alloc([128, T], BF16) for _ in range(2)]
        nb = 0
        srcW = Win.rearrange('(k p) m -> p k m', p=128)
        srcR = Wrot.rearrange('(k p) m -> p k m', p=128)

        def rope_proj(v, wt, dst_ap_fn, dstt):
            nonlocal nb
            for (p, n, _) in sg:
                ba = P.banks[(nb % 2) * 2]
                bb = P.banks[(nb % 2) * 2 + 1]
                t1 = t1r[nb % 2]
                t2 = t2r[nb % 2]
                nb += 1
                for kc in range(KC):
                    P.mm(ba[:, 0:n], v[:, kc, 0:128], uT[:, kc, p:p + n], kc == 0, kc == KC - 1, r=[wt, uT], w=[ba])
                for kc in range(KC):
                    P.mm(bb[:, 0:n], v[:, kc, 128:256], uT[:, kc, p:p + n], kc == 0, kc == KC - 1, r=[wt, uT], w=[bb])
                P.tt('dve', t1[:, 0:n], ba[:, 0:n], rC[:, p:p + n], ALU.mult, r=[ba, rC], w=[t1])
                P.tt('dve', t2[:, 0:n], bb[:, 0:n], rS[:, p:p + n], ALU.mult, r=[bb, rS], w=[t2])
                P.tt('pool', dst_ap_fn(p, n), t1[:, 0:n], t2[:, 0:n], ALU.add, r=[t1, t2], w=[dstt])

        for kv in range(4):
            wt = self.wtile()
            v = wt[:, 0:KC * 256].rearrange('p (k m) -> p k m', m=256)
            for k0 in range(0, KC, 8):
                for hh in range(2):
                    P.dma('pool', v[:, k0:k0 + 8, hh * 64:(hh + 1) * 64], srcW[:, k0:k0 + 8, kv * 64:(kv + 1) * 64], w=[wt])
                    P.dma('pool', v[:, k0:k0 + 8, 128 + hh * 64:128 + (hh + 1) * 64], srcR[:, k0:k0 + 8, kv * 64:(kv + 1) * 64], w=[wt])
            rope_proj(v, wt, lambda p, n, kv=kv: kT[:, kv, p:p + n], kT)
        for ch in range(KC):
            wt = self.wtile()
            v = wt[:, 0:KC * 256].rearrange('p (k m) -> p k m', m=256)
            for k0 in range(0, KC, 8):
                P.dma('pool', v[:, k0:k0 + 8, 0:128], srcW[:, k0:k0 + 8, 512 + ch * 128:512 + (ch + 1) * 128], w=[wt])
                P.dma('pool', v[:, k0:k0 + 8, 128:256], srcR[:, k0:k0 + 8, 256 + ch * 128:256 + (ch + 1) * 128], w=[wt])
            q = qs[ch % 2]
            rope_proj(v, wt, lambda p, n, q=q: q[:, p:p + n], q)
            P.dma('sp', self.QT[ch * 128:(ch + 1) * 128, :], q[:, :], r=[q], w=[self.tQT])
        wt, v = self.load_w(Win, KC, 256, 256)
        for tt in range(NTT):
            bk = P.banks[tt % 4]
            for kc in range(KC):
                P.mm(bk[:, 0:256], uT[:, kc, tt * 128:(tt + 1) * 128], v[:, kc, :], kc == 0, kc == KC - 1, r=[wt, uT], w=[bk])
            P.copy('act' if tt % 2 else 'dve', vt[:, tt, :].rearrange('p (a b) -> p a b', b=65)[:, :, 0:64],
                   bk[:, 0:256].rearrange('p (a b) -> p a b', b=64), r=[bk], w=[vt])
        self.dump('kT', kT, kT[:, :, :], [128, 4, T])
        self.dump('vt', vt, vt[:, :, :], [128, NTT, 260])
        self.dump('uT', uT, uT[:, :, :], [128, KC, T])
        P.release(mk1)
        oT = P.alloc([128, KC, T], BF16)
        mkb = P.mark()
        qg = [P.alloc([128, 4, T], BF16) for _ in range(2)]
        pring = [P.alloc([128, 512], BF16) for _ in range(12)]
        otm = [P.alloc([128, 4, 128], BF16) for _ in range(2)]
        den = [P.alloc([128, 4], F32) for _ in range(2)]
        mp = P.alloc([128, 128], BF16)
        mn = P.alloc([128, 128], BF16)
        P.dma('pool', mp[:, :], self.c_maskp, w=[mp])
        P.dma('pool', mn[:, :], self.c_maskn, w=[mn])
        npi = 0
        nsb = 0
        nob = 0
        for g in range(4):
            q = qg[g % 2]
            P.dma('sp', q[:, :, :], self.QT[g * 512:(g + 1) * 512, :].rearrange('(c p) t -> p c t', p=128), r=[self.tQT], w=[q])
            for i in range(NTT):
                if i < NCB:
                    keys = [(jt, None) for jt in range(NCB)]
                else:
                    keys = [(jt, None) for jt in range(NCB)]
                    if i - 1 >= NCB:
                        keys.append((i - 1, mp))
                    keys.append((i, None))
                    if i + 1 < NTT:
                        keys.append((i + 1, mn))
                ot = otm[nob % 2]
                for half in range(2):
                    hs = slice(half * 64, (half + 1) * 64)
                    pts = []
                    for (jt, msk) in keys:
                        bk = P.banks[nsb % 4]
                        nsb += 1
                        P.mm(bk[:, :], kT[hs, g, jt * 128:(jt + 1) * 128], q[hs, :, i * 128:(i + 1) * 128], True, msk is None,
                             r=[kT, q], w=[bk])
                        if msk is not None:
                            P.mm(bk[:, :], self.ident[:, :], msk[:, :].unsqueeze(1).broadcast_to([128, 4, 128]), False, True,
                                 r=[self.ident, msk], w=[bk])
                        pt = pring[npi % 12]
                        npi += 1
                        P.act(pt[:, :], bk[:, :], AF.Exp, r=[bk], w=[pt], scale=0.125)
                        pts.append((pt, jt))
                    bo = P.banks[4 + (nob * 2 + half) % 2]
                    for cc in range(4):
                        for k_i, (pt, jt) in enumerate(pts):
                            P.mm(bo[:, cc * 65:(cc + 1) * 65], pt[:, cc * 128:(cc + 1) * 128], vt[:, jt, g * 65:(g + 1) * 65],
                                 k_i == 0, k_i == len(pts) - 1, r=[pt, vt], w=[bo])
                    dn = den[half]
                    bov = bo[:, 0:260].rearrange('p (a b) -> p a b', b=65)
                    esv = es[:, g * 8:(g + 1) * 8].rearrange('p (a two) -> p a two', two=2)[:, :, half]
                    P.tt('dve', dn[:, :], bov[:, :, 64], esv, ALU.add, r=[bo, es], w=[dn])
                    P.op('dve', lambda e, dn=dn: e.reciprocal(dn[:, :], dn[:, :]), r=[dn], w=[dn])
                    P.tt('dve', ot[:, :, half * 64:(half + 1) * 64], bov[:, :, 0:64], dn[:, :].unsqueeze(2).broadcast_to([128, 4, 64]),
                         ALU.mult, r=[bo, dn], w=[ot])
                bt = P.banks[6 + nob % 2]
                nob += 1
                for cc in range(4):
                    P.mm(bt[:, cc * 128:(cc + 1) * 128], ot[:, cc, :], self.ident[:, :], True, True, r=[ot, self.ident], w=[bt])
                P.copy('act', oT[:, g * 4:(g + 1) * 4, i * 128:(i + 1) * 128], bt[:, :].rearrange('p (a b) -> p a b', b=128),
                       r=[bt], w=[oT])
        self.dump('oT', oT, oT[:, :, :], [128, KC, T])
        P.release(mkb)
        self.alloc_wring(3)
        self.outproj(Wout, KC, oT, oT, 0, 0, T, s)
        P.release(mk)

    def ssd_phase(self, li, jb, s, first, _x):
        P, c = self.P, self.cfg
        T, CTX, SEQ, NCH = c.T, c.CTX, c.SEQ, c.NCH
        NCC = CTX // 64
        Win, Wout = self.ssd_w_in[jb], self.ssd_w_out[jb]
        mk = P.mark()
        self.alloc_wring(3)
        dt_all = P.alloc([64, NCH, 128], F32)
        Abc = P.alloc([64, 128], F32)
        Dbc = P.alloc([64, 64], F32)
        dtb = P.alloc([64, 128], F32)
        P.dma('sp', Abc[:, :], self.ssd_a_log[jb:jb + 1, :].broadcast_to([64, 128]), w=[Abc])
        P.act(Abc[:, :], Abc[:, :], AF.Exp, r=[Abc], w=[Abc])
        P.ts('dve', Abc[:, :], Abc[:, :], -1.0, None, ALU.mult, r=[Abc], w=[Abc])
        P.dma('sp', Dbc[:, :], self.ssd_d[jb:jb + 1, :].broadcast_to([64, 64]), w=[Dbc])
        P.dma('sp', dtb[:, :], self.ssd_dt_bias[jb:jb + 1, :].broadcast_to([64, 128]), w=[dtb])
        mk1 = P.mark()
        uT = P.alloc([128, KC, T], BF16)
        self.prenorm_seq(s, first, uT)
        cw = P.alloc([128, 48, 3], F32)
        cb = P.alloc([128, 48], F32)
        P.dma('sp', cw[:, :, :], self.ssd_conv_wT[jb], w=[cw])
        P.dma('sp', cb[:, :], self.ssd_conv_bT[jb], w=[cb])
        pre = [P.alloc([128, T + 4], F32) for _ in range(2)]
        acc = P.alloc([128, T], F32)
        xcs = [P.alloc([128, T], BF16) for _ in range(2)]
        for pz in pre:
            P.memset('dve', pz[:, :], 0.0, w=[pz])
        sgc = segs(0, T, CTX, 512)
        nbk = 0
        for g4 in range(12):
            wt, v = self.load_w(Win, KC, g4 * 512, 512)
            for mc in range(4):
                ch = g4 * 4 + mc
                pz = pre[ch % 2]
                xc = xcs[ch % 2]
                for (p, n, isctx) in sgc:
                    bk = P.banks[nbk % 4]
                    nbk += 1
                    for kc in range(KC):
                        P.mm(bk[:, 0:n], v[:, kc, mc * 128:(mc + 1) * 128], uT[:, kc, p:p + n], kc == 0, kc == KC - 1, r=[wt, uT], w=[bk])
                    off = 1 + p if isctx else 3 + p
                    P.copy('act', pz[:, off:off + n], bk[:, 0:n], r=[bk], w=[pz])
                for (off, t0, n) in ((1, 0, CTX), (CTX + 3, CTX, SEQ)):
                    P.ts('dve', acc[:, t0:t0 + n], pz[:, off:off + n], cw[:, ch, 1:2], cb[:, ch:ch + 1], ALU.mult, ALU.add, r=[pz, cw, cb], w=[acc])
                    P.stt('dve', acc[:, t0:t0 + n], pz[:, off - 1:off - 1 + n], cw[:, ch, 0:1], acc[:, t0:t0 + n], ALU.mult, ALU.add, r=[pz, cw, acc], w=[acc])
                    P.stt('dve', acc[:, t0:t0 + n], pz[:, off + 1:off + 1 + n], cw[:, ch, 2:3], acc[:, t0:t0 + n], ALU.mult, ALU.add, r=[pz, cw, acc], w=[acc])
                P.act(xc[:, :], acc[:, :], AF.Silu, r=[acc], w=[xc])
                P.dma('sp', self.XC[ch * 128:(ch + 1) * 128, :], xc[:, :], r=[xc], w=[self.tXC.sub(ch)])
        wt, v = self.load_w(Win, KC, 6144, 128)
        et = P.alloc([64, 512], F32)
        for c0 in range(0, NCH, 4):
            nn = min(4, NCH - c0)
            bk = P.banks[nbk % 4]
            nbk += 1
            for q in range(nn):
                cc = c0 + q
                for kc in range(KC):
                    P.mm(bk[0:64, q * 128:(q + 1) * 128], uT[:, kc, cc * 64:(cc + 1) * 64], v[:, kc, :], kc == 0, kc == KC - 1, r=[wt, uT], w=[bk])
            P.tt('dve', et[:, 0:nn * 128].rearrange('p (a b) -> p a b', b=128), bk[0:64, 0:nn * 128].rearrange('p (a b) -> p a b', b=128),
                 dtb[:, :].unsqueeze(1).broadcast_to([64, nn, 128]), ALU.add, r=[bk, dtb], w=[et])
            P.act(et[:, 0:nn * 128], et[:, 0:nn * 128], AF.Exp, r=[et], w=[et])
            P.act(dt_all[:, c0:c0 + nn, :], et[:, 0:nn * 128].rearrange('p (a b) -> p a b', b=128), AF.Ln, r=[et], w=[dt_all], bias=1.0, scale=1.0)
        zst = [P.alloc([128, 512], BF16) for _ in range(3)]
        nz = 0
        for g8 in range(8):
            wt, v = self.load_w(Win, KC, 6272 + g8 * 512, 512)
            for tt in range(T // 128):
                bk = P.banks[nbk % 4]
                nbk += 1
                for kc in range(KC):
                    P.mm(bk[:, :], uT[:, kc, tt * 128:(tt + 1) * 128], v[:, kc, :], kc == 0, kc == KC - 1, r=[wt, uT], w=[bk])
                z = zst[nz % 3]
                nz += 1
                P.act(z[:, :], bk[:, :], AF.Silu, r=[bk], w=[z])
                P.dma('sp', self.ZS[tt * 128:(tt + 1) * 128, g8 * 512:(g8 + 1) * 512], z[:, :], r=[z], w=[self.tZS.sub(g8)])
        P.release(mk1)
        nw = P.alloc([64, 4096], F32)
        P.dma('sp', nw[:, :], self.ssd_norm_w[jb:jb + 1, :].broadcast_to([64, 4096]), w=[nw])
        xcg = P.alloc([128, 6, T], BF16)
        ygT = P.alloc([128, 4, T], BF16)
        a_g = P.alloc([64, NCH, 16], F32)
        dtg = P.alloc([64, NCH, 16], F32)
        eacum = P.alloc([64, NCH, 16], F32)
        dend = P.alloc([64, NCH, 16], F32)
        cdg = P.alloc([128, NCH, 16], F32)
        S = P.alloc([128, 512], F32)
        Sb = P.alloc([128, 512], BF16)
        R2 = lambda shape, dt: [P.alloc(shape, dt) for _ in range(2)]
        x_tm, B_tm, cbm = R2([64, 512], BF16), R2([64, 128], BF16), R2([64, 64], BF16)
        segr, LT, GT = R2([64, 8, 64], F32), R2([64, 8, 64], BF16), R2([64, 8, 64], BF16)
        xdt, xdtd = R2([64, 8, 64], BF16), R2([64, 8, 64], BF16)
        yo, yb, zs, tn = R2([64, 8, 64], F32), R2([64, 512], F32), R2([64, 512], BF16), R2([64, 512], BF16)
        ss, junk = R2([64, 1], F32), R2([64, 512], F32)
        trif = self.trif
        XCv = self.XC
        it = 0
        for g in range(8):
            P.dma('sp', xcg[:, 0:4, :], XCv[g * 512:(g + 1) * 512, :].rearrange('(q p) t -> p q t', p=128),
                  r=[self.tXC.sub(g * 4 + q) for q in range(4)], w=[xcg])
            P.dma('sp', xcg[:, 4, :], XCv[4096 + g * 128:4096 + (g + 1) * 128, :], r=[self.tXC.sub(32 + g)], w=[xcg])
            P.dma('sp', xcg[:, 5, :], XCv[5120 + g * 128:5120 + (g + 1) * 128, :], r=[self.tXC.sub(40 + g)], w=[xcg])
            for d in range(2):
                hsl = slice(d * 64 + g * 8, d * 64 + g * 8 + 8)
                P.tt('dve', a_g[:, :, d * 8:(d + 1) * 8], dt_all[:, :, hsl], Abc[:, hsl].unsqueeze(1).broadcast_to([64, NCH, 8]), ALU.mult,
                     r=[dt_all, Abc], w=[a_g])
                P.copy('dve', dtg[:, :, d * 8:(d + 1) * 8], dt_all[:, :, hsl], r=[dt_all], w=[dtg])
            for d in range(2):
                bk = P.banks[7]
                P.mm(bk[0:64, 0:NCH * 8], trif[:, d, :], a_g[:, :, d * 8:(d + 1) * 8], True, True, r=[trif, a_g], w=[bk])
                P.act(eacum[:, :, d * 8:(d + 1) * 8], bk[0:64, 0:NCH * 8].rearrange('p (a b) -> p a b', b=8), AF.Exp, r=[bk], w=[eacum])
                P.mm(bk[0:64, 0:NCH * 8], trif[:, 2 + d, :], a_g[:, :, d * 8:(d + 1) * 8], True, True, r=[trif, a_g], w=[bk])
                P.act(dend[:, :, d * 8:(d + 1) * 8], bk[0:64, 0:NCH * 8].rearrange('p (a b) -> p a b', b=8), AF.Exp, r=[bk], w=[dend])
            hc = (NCH + 1) // 2
            for c0 in (0, hc):
                nn = min(hc, NCH - c0)
                bk = P.banks[7]
                P.mm(bk[:, 0:nn * 16], self.onesf[0:64, :], a_g[:, c0:c0 + nn, :], True, True, r=[self.onesf, a_g], w=[bk])
                P.act(cdg[:, c0:c0 + nn, :], bk[:, 0:nn * 16].rearrange('p (a b) -> p a b', b=16), AF.Exp, r=[bk], w=[cdg])
            for d in (1, 0):
                order = list(range(NCH)) if d == 0 else (list(range(NCC - 1, -1, -1)) + list(range(NCH - 1, NCC - 1, -1)))
                P.memset('dve', S[:, :], 0.0, w=[S])
                P.memset('pool', Sb[:, :], 0.0, w=[Sb])
                dsl = slice(d * 8, (d + 1) * 8)
                for cix in order:
                    k2 = it % 2
                    it += 1
                    tok = slice(cix * 64, (cix + 1) * 64)
                    bx, bB, bseg, by, boff, bst, btr = (P.banks[i_] for i_ in range(7))
                    for q in range(4):
                        P.mm(bx[0:64, q * 128:(q + 1) * 128], xcg[:, q, tok], self.ident[:, :], True, True, r=[xcg, self.ident], w=[bx])
                    P.mm(bB[0:64, 0:128], xcg[:, 4, tok], self.ident[:, :], True, True, r=[xcg, self.ident], w=[bB])
                    P.mm(bB[0:64, 128:192], xcg[:, 4, tok], xcg[:, 5, tok], True, True, r=[xcg], w=[bB])
                    P.copy('act', x_tm[k2][:, :], bx[0:64, :], r=[bx], w=[x_tm[k2]])
                    P.copy('act', B_tm[k2][:, :], bB[0:64, 0:128], r=[bB], w=[B_tm[k2]])
                    P.tt('dve', cbm[k2][:, :], bB[0:64, 128:192], trif[:, d, :], ALU.mult, r=[bB, trif], w=[cbm[k2]])
                    P.tt('pool', segr[k2][:, :, :], a_g[:, cix, dsl].unsqueeze(2).broadcast_to([64, 8, 64]),
                         trif[:, d, :].unsqueeze(1).broadcast_to([64, 8, 64]), ALU.mult, r=[a_g, trif], w=[segr[k2]])
                    P.mm(bseg[0:64, :], trif[:, 2 + d, :], segr[k2][:, :, :], True, True, r=[trif, segr[k2]], w=[bseg])
                    P.act(LT[k2][:, :, :], bseg[0:64, :].rearrange('p (a b) -> p a b', b=64), AF.Exp, r=[bseg], w=[LT[k2]])
                    P.tt('dve', GT[k2][:, :, :], LT[k2][:, :, :], cbm[k2][:, :].unsqueeze(1).broadcast_to([64, 8, 64]), ALU.mult,
                         r=[LT[k2], cbm[k2]], w=[GT[k2]])
                    xv = x_tm[k2][:, :].rearrange('p (a b) -> p a b', b=64)
                    P.tt('pool', xdt[k2][:, :, :], xv, dtg[:, cix, dsl].unsqueeze(2).broadcast_to([64, 8, 64]), ALU.mult,
                         r=[x_tm[k2], dtg], w=[xdt[k2]])
                    P.tt('pool', xdtd[k2][:, :, :], xdt[k2][:, :, :], dend[:, cix, dsl].unsqueeze(2).broadcast_to([64, 8, 64]), ALU.mult,
                         r=[xdt[k2], dend], w=[xdtd[k2]])
                    for h in range(8):
                        P.mm(by[0:64, h * 64:(h + 1) * 64], GT[k2][:, h, :], xdt[k2][:, h, :], True, True, r=[GT[k2], xdt[k2]], w=[by])
                    P.mm(boff[0:64, :], xcg[:, 5, tok], Sb[:, :], True, True, r=[xcg, Sb], w=[boff])
                    P.tt('dve', yo[k2][:, :, :], boff[0:64, :].rearrange('p (a b) -> p a b', b=64),
                         eacum[:, cix, dsl].unsqueeze(2).broadcast_to([64, 8, 64]), ALU.mult, r=[boff, eacum], w=[yo[k2]])
                    yv = yo[k2][:, :, :].rearrange('p a b -> p (a b)')
                    P.tt('dve', yv, yv, by[0:64, :], ALU.add, r=[yo[k2], by], w=[yo[k2]])
                    P.mm(bst[:, :], B_tm[k2][:, :], xdtd[k2][:, :, :], True, True, r=[B_tm[k2], xdtd[k2]], w=[bst])
                    Sv = S[:, :].rearrange('p (a b) -> p a b', b=64)
                    P.tt('dve', Sv, Sv, cdg[:, cix, dsl].unsqueeze(2).broadcast_to([128, 8, 64]), ALU.mult, r=[S, cdg], w=[S])
                    P.tt('dve', S[:, :], S[:, :], bst[:, :], ALU.add, r=[S, bst], w=[S])
                    P.copy('act', Sb[:, :], S[:, :], r=[S], w=[Sb])
                    ybt = self.tYB.sub(cix)
                    if d == 1:
                        P.dma('sp', self.YB[tok, :], yv, r=[yo[k2]], w=[ybt])
                    else:
                        P.dma('sp', yb[k2][:, :], self.YB[tok, :], r=[ybt], w=[yb[k2]])
                        P.dma('sp', zs[k2][:, :], self.ZS[tok, g * 512:(g + 1) * 512], r=[self.tZS.sub(g)], w=[zs[k2]])
                        P.tt('dve', yv, yv, yb[k2][:, :], ALU.add, r=[yo[k2], yb[k2]], w=[yo[k2]])
                        xd = junk[k2]
                        P.tt('pool', xd[:, :].rearrange('p (a b) -> p a b', b=64), xv,
                             Dbc[:, g * 8:(g + 1) * 8].unsqueeze(2).broadcast_to([64, 8, 64]), ALU.mult, r=[x_tm[k2], Dbc], w=[xd])
                        P.tt('dve', yv, yv, xd[:, :], ALU.add, r=[yo[k2], xd], w=[yo[k2]])
                        P.tt('dve', yv, yv, zs[k2][:, :], ALU.mult, r=[yo[k2], zs[k2]], w=[yo[k2]])
                        P.memset('dve', ss[k2][:, :], 0.0, w=[ss[k2]])
                        P.act(xd[:, :], yv, AF.Square, r=[yo[k2]], w=[xd, ss[k2]], accum_out=ss[k2][:, 0:1])
                        P.act(ss[k2][:, :], ss[k2][:, :], AF.Ln, r=[ss[k2]], w=[ss[k2]], bias=EPS, scale=1.0 / 512)
                        P.act(ss[k2][:, :], ss[k2][:, :], AF.Exp, r=[ss[k2]], w=[ss[k2]], scale=-0.5)
                        P.stt('dve', tn[k2][:, :], yv, ss[k2][:, 0:1], nw[:, g * 512:(g + 1) * 512], ALU.mult, ALU.mult,
                              r=[yo[k2], ss[k2], nw], w=[tn[k2]])
                        for q in range(4):
                            P.mm(btr[:, q * 64:(q + 1) * 64], tn[k2][:, q * 128:(q + 1) * 128], self.ident[0:64, 0:64], True, True,
                                 r=[tn[k2], self.ident], w=[btr])
                        P.copy('act', ygT[:, :, tok], btr[:, 0:256].rearrange('p (a b) -> p a b', b=64), r=[btr], w=[ygT])
            P.dma('sp', self.OT[g * 512:(g + 1) * 512, :].rearrange('(q p) t -> p q t', p=128), ygT[:, :, :], r=[ygT], w=[self.tOT.sub(g)])
        P.release(mk)
        mk = P.mark()
        self.alloc_wring(3)
        nblk = 1152 if T > 1152 else T
        src = P.alloc([128, 32, nblk], BF16)
        for t0 in range(0, T, nblk):
            n = min(nblk, T - t0)
            P.dma('sp', src[:, :, 0:n], self.OT[:, t0:t0 + n].rearrange('(q p) t -> p q t', p=128), r=[self.tOT], w=[src])
            self.outproj(Wout, 32, src, src, 0, t0, n, s)
        P.release(mk)

    def hgrn_phase(self, li, jc, s, first, lbi):
        P, c = self.P, self.cfg
        T, CTX, NCH = c.T, c.CTX, c.NCH
        NCC = CTX // 64
        Win, Wout = self.hgrn_w_in[jc], self.hgrn_w_out[jc]
        depth = c.lbdepth
        mk = P.mark()
        self.alloc_wring(2)
        nwb = P.alloc([64, D], F32)
        P.dma('sp', nwb[:, :], self.hgrn_norm_w[jc:jc + 1, :].broadcast_to([64, D]), w=[nwb])
        uT = P.alloc([128, KC, T], BF16)
        self.prenorm_seq(s, first, uT)
        lg_tm = P.alloc([64, NCH, 128], F32)
        k_tm = P.alloc([64, NCH, 128], BF16)
        v_tm = P.alloc([64, NCH, 128], BF16)
        gs_tm = P.alloc([64, NCH, 128], BF16)
        qT = P.alloc([128, T], BF16)
        oTh = P.alloc([128, T], BF16)
        lbraw = P.alloc([64, 2, depth, 128], F32)
        lbt = P.alloc([64, 2, 128], F32)
        omlt = P.alloc([64, 2, 128], F32)
        lden = P.alloc([64, 2, 128], F32)
        state = P.alloc([128, 128], F32)
        state_bf = P.alloc([128, 128], BF16)
        R2 = lambda shape, dt: [P.alloc(shape, dt) for _ in range(2)]
        sig, gg = R2([64, 128], F32), R2([64, 128], F32)
        egT, engT, qgT, kgT = R2([128, 64], F32), R2([128, 64], F32), R2([128, 64], BF16), R2([128, 64], BF16)
        ket, kend, sT = R2([64, 128], F32), R2([64, 128], BF16), R2([64, 64], BF16)
        osb, obl, on, on2 = R2([64, 128], F32), R2([64, 128], F32), R2([64, 128], F32), R2([64, 128], BF16)
        ss, junk = R2([64, 1], F32), R2([64, 128], F32)
        trif = self.trif
        srcW = Win.rearrange('(k p) m -> p k m', p=128)
        sgq = segs(0, T, 0, 512)
        it = 0
        npb = 0
        for h in range(16):
            for d in range(2):
                P.dma('sp', lbraw[:, d, :, :], self.hgrn_lb[d:d + 1, :, h * 128:(h + 1) * 128].broadcast_to([64, depth, 128]), w=[lbraw])
            P.act(lbraw[:, :, :, :], lbraw[:, :, :, :], AF.Exp, r=[lbraw], w=[lbraw])
            P.copy('dve', lden[:, :, :], lbraw[:, :, 0, :], r=[lbraw], w=[lden])
            for j in range(1, depth):
                P.tt('dve', lden[:, :, :], lden[:, :, :], lbraw[:, :, j, :], ALU.add, r=[lden, lbraw], w=[lden])
            P.op('dve', lambda e: e.reciprocal(omlt[:, :, :], lden[:, :, :]), r=[lden], w=[omlt])
            P.memset('dve', lbt[:, :, :], 0.0, w=[lbt])
            for j in range(1, lbi + 1):
                P.tt('dve', lbt[:, :, :], lbt[:, :, :], lbraw[:, :, j, :], ALU.add, r=[lbt, lbraw], w=[lbt])
            P.tt('dve', lbt[:, :, :], lbt[:, :, :], omlt[:, :, :], ALU.mult, r=[lbt, omlt], w=[lbt])
            P.ts('dve', omlt[:, :, :], lbt[:, :, :], -1.0, 1.0, ALU.mult, ALU.add, r=[lbt], w=[omlt])
            wt = self.wtile()
            v = wt[:, 0:KC * 512].rearrange('p (k m) -> p k m', m=512)
            for qi, c0 in enumerate((2048 + h * 128, 4096 + h * 128, 8192 + h * 128, h * 128)):
                for k0 in range(0, KC, 8):
                    P.dma('pool', v[:, k0:k0 + 8, qi * 128:(qi + 1) * 128], srcW[:, k0:k0 + 8, c0:c0 + 128], w=[wt])
            wq, vq = self.load_w(Win, KC, 6144 + h * 128, 128)
            for (p, n, _) in sgq:
                bk = P.banks[6 + npb % 2]
                npb += 1
                for kc in range(KC):
                    P.mm(bk[:, 0:n], vq[:, kc, :], uT[:, kc, p:p + n], kc == 0, kc == KC - 1, r=[wq, uT], w=[bk])
                P.act(qT[:, p:p + n], bk[:, 0:n], AF.Silu, r=[bk], w=[qT])
            for d in (1, 0):
                c_lo, c_hi = (0, 384) if d == 1 else (384, 512)
                for cc in range(NCH):
                    tok = slice(cc * 64, (cc + 1) * 64)
                    bk = P.banks[6 + npb % 2]
                    k2 = npb % 2
                    npb += 1
                    ncol = c_hi - c_lo
                    for kc in range(KC):
                        P.mm(bk[0:64, 0:ncol], uT[:, kc, tok], v[:, kc, c_lo:c_hi], kc == 0, kc == KC - 1, r=[wt, uT], w=[bk])
                    P.act(sig[k2][:, :], bk[0:64, 0:128], AF.Sigmoid, r=[bk], w=[sig[k2]])
                    P.tt('dve', gg[k2][:, :], sig[k2][:, :], omlt[:, d, :], ALU.mult, r=[sig[k2], omlt], w=[gg[k2]])
                    P.tt('dve', gg[k2][:, :], gg[k2][:, :], lbt[:, d, :], ALU.add, r=[gg[k2], lbt], w=[gg[k2]])
                    P.act(lg_tm[:, cc, :], gg[k2][:, :], AF.Ln, r=[gg[k2]], w=[lg_tm])
                    P.ts('dve', k_tm[:, cc, :], gg[k2][:, :], -1.0, 1.0, ALU.mult, ALU.add, r=[gg[k2]], w=[k_tm])
                    if d == 1:
                        P.copy('dve', v_tm[:, cc, :], bk[0:64, 128:256], r=[bk], w=[v_tm])
                        P.act(gs_tm[:, cc, :], bk[0:64, 256:384], AF.Silu, r=[bk], w=[gs_tm])
                order = list(range(NCH)) if d == 0 else (list(range(NCC - 1, -1, -1)) + list(range(NCH - 1, NCC - 1, -1)))
                P.memset('dve', state[:, :], 0.0, w=[state])
                P.memset('pool', state_bf[:, :], 0.0, w=[state_bf])
                for cix in order:
                    k2 = it % 2
                    it += 1
                    tok = slice(cix * 64, (cix + 1) * 64)
                    bA, bB, bC, bD, bE, bF = (P.banks[i_] for i_ in range(6))
                    P.mm(bA[:, 0:64], lg_tm[:, cix, :], trif[:, d, :], True, True, r=[lg_tm, trif], w=[bA])
                    P.mm(bA[:, 64:128], k_tm[:, cix, :], self.ident[0:64, 0:64], True, True, r=[k_tm, self.ident], w=[bA])
                    P.mm(bB[0:64, 0:128], trif[:, 2 + d, :], lg_tm[:, cix, :], True, True, r=[trif, lg_tm], w=[bB])
                    P.act(egT[k2][:, :], bA[:, 0:64], AF.Exp, r=[bA], w=[egT[k2]])
                    P.act(engT[k2][:, :], bA[:, 0:64], AF.Exp, r=[bA], w=[engT[k2]], scale=-1.0)
                    P.tt('dve', qgT[k2][:, :], qT[:, tok], egT[k2][:, :], ALU.mult, r=[qT, egT[k2]], w=[qgT[k2]])
                    P.tt('dve', kgT[k2][:, :], bA[:, 64:128], engT[k2][:, :], ALU.mult, r=[bA, engT[k2]], w=[kgT[k2]])
                    P.act(ket[k2][:, :], bB[0:64, 0:128], AF.Exp, r=[bB], w=[ket[k2]])
                    P.tt('pool', kend[k2][:, :], ket[k2][:, :], k_tm[:, cix, :], ALU.mult, r=[ket[k2], k_tm], w=[kend[k2]])
                    P.mm(bC[0:64, 0:64], kgT[k2][:, :], qgT[k2][:, :], True, True, r=[kgT[k2], qgT[k2]], w=[bC])
                    P.tt('dve', sT[k2][:, :], bC[0:64, 0:64], trif[:, d, :], ALU.mult, r=[bC, trif], w=[sT[k2]])
                    P.mm(bD[0:64, 0:128], sT[k2][:, :], v_tm[:, cix, :], True, False, r=[sT[k2], v_tm], w=[bD])
                    P.mm(bD[0:64, 0:128], qgT[k2][:, :], state_bf[:, :], False, True, r=[qgT[k2], state_bf], w=[bD])
                    P.mm(bE[:, 0:128], kend[k2][:, :], v_tm[:, cix, :], True, True, r=[kend[k2], v_tm], w=[bE])
                    ecol = 63 if d == 0 else 0
                    P.stt('dve', state[:, :], state[:, :], egT[k2][:, ecol:ecol + 1], bE[:, 0:128], ALU.mult, ALU.add,
                          r=[state, egT[k2], bE], w=[state])
                    P.copy('act', state_bf[:, :], state[:, :], r=[state], w=[state_bf])
                    ybt = self.tYB.sub(cix)
                    if d == 1:
                        P.copy('act', osb[k2][:, :], bD[0:64, 0:128], r=[bD], w=[osb[k2]])
                        P.dma('sp', self.YB[tok, 0:128], osb[k2][:, :], r=[osb[k2]], w=[ybt])
                    else:
                        P.dma('sp', obl[k2][:, :], self.YB[tok, 0:128], r=[ybt], w=[obl[k2]])
                        P.tt('dve', osb[k2][:, :], bD[0:64, 0:128], obl[k2][:, :], ALU.add, r=[bD, obl[k2]], w=[osb[k2]])
                        P.memset('dve', ss[k2][:, :], 0.0, w=[ss[k2]])
                        P.act(junk[k2][:, :], osb[k2][:, :], AF.Square, r=[osb[k2]], w=[junk[k2], ss[k2]], accum_out=ss[k2][:, 0:1])
                        P.act(ss[k2][:, :], ss[k2][:, :], AF.Ln, r=[ss[k2]], w=[ss[k2]], bias=EPS, scale=1.0 / 128)
                        P.act(ss[k2][:, :], ss[k2][:, :], AF.Exp, r=[ss[k2]], w=[ss[k2]], scale=-0.5)
                        P.stt('dve', on[k2][:, :], osb[k2][:, :], ss[k2][:, 0:1], nwb[:, h * 128:(h + 1) * 128], ALU.mult, ALU.mult,
                              r=[osb[k2], ss[k2], nwb], w=[on[k2]])
                        P.tt('dve', on2[k2][:, :], on[k2][:, :], gs_tm[:, cix, :], ALU.mult, r=[on[k2], gs_tm], w=[on2[k2]])
                        P.mm(bF[:, 0:64], on2[k2][:, :], self.ident[0:64, 0:64], True, True, r=[on2[k2], self.ident], w=[bF])
                        P.copy('act', oTh[:, tok], bF[:, 0:64], r=[bF], w=[oTh])
            P.dma('sp', self.OT[h * 128:(h + 1) * 128, :], oTh[:, :], r=[oTh], w=[self.tOT.sub(h)])
        P.release(mk)
        mk = P.mark()
        self.alloc_wring(3)
        src = P.alloc([128, KC, T], BF16)
        P.dma('sp', src[:, :, :], self.OT[0:D, :].rearrange('(q p) t -> p q t', p=128), r=[self.tOT], w=[src])
        self.outproj(Wout, KC, src, src, 0, 0, T, s)
        P.release(mk)

class _Kind:
    def __init__(self, modT, kind):
        self.node = modT.node
        self.ap = modT.ap[:, kind, :, :]

    def __getitem__(self, k):
        return self.ap[k]


def rope_tables(cfg):
    SEQ, CTX, T = cfg.SEQ, cfg.CTX, cfg.T
    pos = np.arange(SEQ)
    row = (pos // 64).astype(np.float32)
    col = (pos % 64).astype(np.float32)
    inv = (10000.0 ** (-np.arange(16, dtype=np.float32) / 16)).astype(np.float32)
    ang = np.stack([row[:, None] * inv, col[:, None] * inv], axis=1)
    cs, sn = np.cos(ang).astype(np.float32), np.sin(ang).astype(np.float32)
    C = np.ones((128, T), np.float32)
    S = np.zeros((128, T), np.float32)
    for dd in range(64):
        axis, half, pair = dd // 32, (dd % 32) // 16, dd % 16
        for rep in range(2):
            C[rep * 64 + dd, CTX:] = cs[:, axis, pair]
            S[rep * 64 + dd, CTX:] = sn[:, axis, pair] * (-1.0 if half == 0 else 1.0)
    return C, S


def const_tables():
    ident = np.eye(128, dtype=np.float32)
    j = np.arange(128)[:, None]
    q = np.arange(128)[None, :]
    maskp = np.where(j >= q, 0.0, -30000.0).astype(np.float32)
    maskn = np.where(j <= q, 0.0, -30000.0).astype(np.float32)
    t = np.arange(64)[:, None]
    i = np.arange(64)[None, :]
    tri = np.stack([(t <= i), (t >= i), (t > i), (t < i)], axis=1).astype(np.float32)
    return ident, maskp, maskn, tri


def host_inputs(cfg, inp, b0, nseq):
    d = D
    f32 = np.float32
    sl = slice(b0, b0 + nseq)
    xcat = np.concatenate([inp['ctx'][sl], inp['x'][sl]], axis=1)
    xin = np.ascontiguousarray(xcat.transpose(0, 2, 1))
    cv = np.concatenate([inp['c'][sl], inp['c_ctx'][None, :]], axis=0)
    cvec = np.ascontiguousarray(cv.reshape(nseq + 1, KC, 128).transpose(2, 1, 0))
    depth = inp['ada_w'].shape[0]
    ada_bT = np.ascontiguousarray(inp['ada_b'].reshape(depth, 96, 128).transpose(0, 2, 1))
    norm_gT = np.ascontiguousarray(inp['norm_g'].reshape(depth, 4, KC, 128).transpose(0, 1, 3, 2))
    w = inp['attn_w_in']
    perm = np.arange(64).reshape(2, 2, 16)[:, ::-1, :].reshape(64)
    kperm = (np.arange(4)[:, None] * 64 + perm[None, :]).reshape(-1)
    qperm = 512 + (np.arange(32)[:, None] * 64 + perm[None, :]).reshape(-1)
    w_rot = np.ascontiguousarray(w[:, :, np.concatenate([kperm, qperm])])
    C, S = rope_tables(cfg)
    ident, maskp, maskn, tri = const_tables()
    nb = inp['ssd_conv_w'].shape[0]
    conv_wT = np.ascontiguousarray(inp['ssd_conv_w'].reshape(nb, 3, 48, 128).transpose(0, 3, 2, 1))
    conv_bT = np.ascontiguousarray(inp['ssd_conv_b'].reshape(nb, 48, 128).transpose(0, 2, 1))
    m = dict(
        xin=xin, cvec=cvec, ada_w=inp['ada_w'], ada_bT=ada_bT, norm_gT=norm_gT,
        ffn_w_in=inp['ffn_w_in'], ffn_w_out=inp['ffn_w_out'],
        attn_w_in=w, attn_w_rot=w_rot, attn_w_out=inp['attn_w_out'], attn_sink=inp['attn_sink'],
        ssd_w_in=inp['ssd_w_in'], ssd_conv_wT=conv_wT, ssd_conv_bT=conv_bT,
        ssd_dt_bias=np.ascontiguousarray(inp['ssd_dt_bias'].reshape(nb, 128)),
        ssd_a_log=np.ascontiguousarray(inp['ssd_a_log'].reshape(nb, 128)),
        ssd_d=inp['ssd_d'], ssd_norm_w=inp['ssd_norm_w'], ssd_w_out=inp['ssd_w_out'],
        hgrn_w_in=inp['hgrn_w_in'], hgrn_lb=inp['hgrn_lb'], hgrn_norm_w=inp['hgrn_norm_w'], hgrn_w_out=inp['hgrn_w_out'],
        ropeC=C, ropeS=S, c_ident=ident, c_maskp=maskp, c_maskn=maskn, c_tri=tri,
    )
    return {k: np.ascontiguousarray(np.asarray(v, dtype=f32)) for k, v in m.items()}


def build_program(cfg):
    b = Builder(cfg)
    P = b.P
    with b.nc.allow_low_precision("bf16 matmul operands, fp32 accumulation"):
        nl = len(cfg.layers)
        cnt = {0: 0, 1: 0, 2: 0}
        for n_, li in enumerate(cfg.layers):
            kind = li % NMIX
            j = cnt[kind]
            cnt[kind] += 1
            b.adaln(n_)
            b.ffn_prepare(n_)
            for s in range(cfg.NSEQ):
                first = (n_ == 0)
                if kind == 0:
                    b.attn_phase(n_, j, s, first, 0)
                elif kind == 1:
                    b.ssd_phase(n_, j, s, first, 0)
                else:
                    b.hgrn_phase(n_, j, s, first, li)
                b.ffn_phase(n_, s, n_ == nl - 1)
        stats = P.finish()
    return b, stats


_CACHE = {}
GROUPS = ((0,), (1,), (2,), (3,))


def _slice_inputs(inp, layers):
    ls = list(layers)
    out = dict(inp)
    for k in ('ada_w', 'ada_b', 'norm_g', 'ffn_w_in', 'ffn_w_out'):
        out[k] = inp[k][ls]
    for kind, keys in ((0, ('attn_w_in', 'attn_w_out', 'attn_sink')),
                       (1, ('ssd_w_in', 'ssd_conv_w', 'ssd_conv_b', 'ssd_dt_bias', 'ssd_a_log', 'ssd_d', 'ssd_norm_w', 'ssd_w_out')),
                       (2, ('hgrn_w_in', 'hgrn_norm_w', 'hgrn_w_out'))):
        js = [li // NMIX for li in ls if li % NMIX == kind]
        for k in keys:
            out[k] = inp[k][js] if js else inp[k][:1]
    return out


def kernel(**inputs):
    n_cores = 8
    inp = {k: np.asarray(v) for k, v in inputs.items()}
    B, SEQ = inp['x'].shape[0], inp['x'].shape[1]
    CTX = inp['ctx'].shape[1]
    nseq = B // n_cores
    xs = None
    for grp in GROUPS:
        cfg = Cfg(SEQ=SEQ, CTX=CTX, NSEQ=nseq, layers=grp, depth=len(grp), lbdepth=inp['hgrn_lb'].shape[1])
        key = tuple(li % NMIX if (li % NMIX) != 2 else ('h', li) for li in grp)
        if key not in _CACHE:
            _CACHE[key] = build_program(cfg)
        b, stats = _CACHE[key]
        sl = _slice_inputs(inp, grp)
        in_maps = []
        for core in range(n_cores):
            m = host_inputs(cfg, sl, core * nseq, nseq)
            if xs is not None:
                m['xin'] = xs[core]
            in_maps.append(m)
        res = run_bass_kernel_spmd(b.nc, in_maps, core_ids=list(range(n_cores)))
        xs = [np.ascontiguousarray(res.results[core]["xout"]) for core in range(n_cores)]
    outs = [np.ascontiguousarray(xs[core][:, :, CTX:].transpose(0, 2, 1)) for core in range(n_cores)]
    return np.concatenate(outs, axis=0).astype(np.float32)
```

```python
import numpy as np
import ml_dtypes
from concourse.bass_utils import run_bass_kernel_spmd
import concourse.bass as bass
import concourse.mybir as mybir

F32 = mybir.dt.float32
BF16 = mybir.dt.bfloat16
ALU = mybir.AluOpType
AF = mybir.ActivationFunctionType
AX = mybir.AxisListType
ENGS = ['pe', 'act', 'dve', 'pool', 'sp']
EPOCH = 30000
DMA_R = 8
DMA_EPOCH = 1800
CUT_EVERY = 10 ** 9


class Node:
    __slots__ = ('lw', 'rd', 'parent', 'kids')

    def __init__(self, parent=None):
        self.lw = None
        self.rd = []
        self.parent = parent
        self.kids = []


class Ins:
    __slots__ = ('eng', 'fn', 'idx', 'waits', 'mile', 'mnum', 'dma', 'dsem', 'dval', 'blk')

    def __init__(self, eng, fn, idx, dma):
        self.eng = eng
        self.fn = fn
        self.idx = idx
        self.waits = []
        self.mile = False
        self.mnum = 0
        self.dma = dma
        self.dsem = None
        self.dval = 0
        self.blk = 0


class Tile:
    def __init__(self, ap, parent_node=None):
        self.ap = ap
        self.node = Node(parent_node)
        if parent_node is not None:
            parent_node.kids.append(self.node)
        self.subs = {}

    def __getitem__(self, k):
        return self.ap[k]

    def sub(self, key):
        s = self.subs.get(key)
        if s is None:
            s = Tile(self.ap, self.node)
            self.subs[key] = s
        return s


class Prog:
    def __init__(self, nc, arena_bytes=204 * 1024):
        self.nc = nc
        self.streams = {e: [] for e in ENGS}
        self.seen = {e: {} for e in ENGS}
        self.dseen = {e: {} for e in ENGS}
        self.ndma = {e: 0 for e in ENGS}
        self.dma_hist = {e: [] for e in ENGS}
        self.arena_bytes = arena_bytes
        self.top = 0
        self.ghosts = []
        self.live = []
        self._arena_cm = nc.sbuf_tensor('arena', [128, arena_bytes // 2], BF16)
        self.arena = self._arena_cm.__enter__()
        self._psum_cms = [nc.psum_tensor(f'psb{i}', [128, 512], F32) for i in range(8)]
        self.banks = [Tile(cm.__enter__()[:, :]) for cm in self._psum_cms]
        self.dsems = {}
        self.final_waits = []
        self.blk = 0
        self.since_cut = 0
        self.cut_every = CUT_EVERY

    def mark(self):
        return (self.top, len(self.live))

    def release(self, mark):
        top, nlive = mark
        for (s, e, t) in self.live[nlive:]:
            deps = []
            self._collect(t.node, deps)
            self.ghosts.append((s, e, deps))
        del self.live[nlive:]
        self.top = top

    def _collect(self, node, deps):
        if node.lw is not None:
            deps.append(node.lw)
        deps.extend(node.rd)
        for k in node.kids:
            self._collect(k, deps)

    def alloc(self, shape, dtype):
        esz = 4 if dtype == F32 else 2
        free = int(np.prod(shape[1:]))
        nbytes = (free * esz + 31) // 32 * 32
        s = self.top
        e = s + nbytes
        assert e <= self.arena_bytes, f"SBUF arena overflow {e} > {self.arena_bytes}"
        self.top = e
        ap = self.arena[0:shape[0], s // 2:(s + free * esz) // 2]
        if dtype != BF16:
            ap = ap.bitcast(dtype)
        if len(shape) == 3:
            ap = ap.rearrange('p (a b) -> p a b', b=shape[2])
        elif len(shape) == 4:
            ap = ap.rearrange('p (a b c) -> p a b c', b=shape[2], c=shape[3])
        t = Tile(ap)
        inherited = []
        ng = []
        for (gs, ge, deps) in self.ghosts:
            if gs < e and s < ge:
                inherited.extend(deps)
                if s <= gs and ge <= e:
                    continue
            ng.append((gs, ge, deps))
        self.ghosts = ng
        t.node.rd = list(dict.fromkeys(inherited))
        self.live.append((s, e, t))
        return t

    def _resolve(self, ins, deps, soft=()):
        e = ins.eng
        best = {}
        for hard, lst in ((True, deps), (False, soft)):
          for d in lst:
            if d is ins:
                continue
            if d.dma:
                key = ('d', d.dsem)
                if key not in best or best[key].dval < d.dval:
                    best[key] = d
            else:
                if d.eng == e and (e == 'pe' or not hard):
                    continue
                key = ('e', d.eng)
                if key not in best or best[key].idx < d.idx:
                    best[key] = d
        for key, d in best.items():
            if d.dma:
                if self.dseen[e].get(d.dsem, 0) >= d.dval:
                    continue
                self.dseen[e][d.dsem] = d.dval
                ins.waits.append(d)
            else:
                if self.seen[e].get(d.eng, -1) >= d.idx:
                    continue
                self.seen[e][d.eng] = d.idx
                d.mile = True
                ins.waits.append(d)

    def op(self, eng, fn, r=(), w=(), dma=False):
        st = self.streams[eng]
        ins = Ins(eng, fn, len(st), dma)
        self.since_cut += 1
        if self.since_cut >= self.cut_every:
            self.since_cut = 0
            self.blk += 1
        ins.blk = self.blk
        deps = []
        if dma:
            i = self.ndma[eng]
            self.ndma[eng] = i + 1
            slot = i % DMA_R
            use = i // DMA_R
            ins.dsem = (eng, slot, use // DMA_EPOCH)
            ins.dval = 16 * (use % DMA_EPOCH + 1)
            hist = self.dma_hist[eng]
            if i >= DMA_R:
                deps.append(hist[i - DMA_R])
            hist.append(ins)
        for t in r:
            n = t.node
            if n.lw is not None:
                deps.append(n.lw)
            if n.parent is not None and n.parent.lw is not None:
                deps.append(n.parent.lw)
            for k in n.kids:
                if k.lw is not None:
                    deps.append(k.lw)
        soft = []
        for t in w:
            n = t.node
            nodes = [n] + n.kids + ([n.parent] if n.parent is not None else [])
            for m in nodes:
                if m.lw is not None:
                    deps.append(m.lw)
                soft.extend(m.rd)
        self._resolve(ins, deps, soft)
        for t in r:
            rd = t.node.rd
            if not dma:
                rd[:] = [x for x in rd if x.dma or x.eng != eng]
            rd.append(ins)
        for t in w:
            n = t.node
            n.lw = ins
            n.rd = []
            for k in n.kids:
                k.lw = ins
                k.rd = []
        st.append(ins)
        return ins

    def dma(self, eng, out, in_, r=(), w=(), **kw):
        return self.op(eng, lambda e: e.dma_start(out=out, in_=in_, **kw), r=r, w=w, dma=True)


    def act(self, out, in_, func, r=(), w=(), **kw):
        return self.op('act', lambda e: e.activation(out, in_, func, **kw), r=r, w=w)

    def tt(self, eng, out, a, b, op, r=(), w=()):
        return self.op(eng, lambda e: e.tensor_tensor(out, a, b, op), r=r, w=w)

    def ts(self, eng, out, a, s1, s2, op0, op1=None, r=(), w=()):
        if op1 is None:
            return self.op(eng, lambda e: e.tensor_scalar(out, a, s1, None, op0), r=r, w=w)
        return self.op(eng, lambda e: e.tensor_scalar(out, a, s1, s2, op0, op1), r=r, w=w)

    def stt(self, eng, out, in0, scalar, in1, op0, op1, r=(), w=()):
        return self.op('dve', lambda e: e.scalar_tensor_tensor(out, in0, scalar, in1, op0, op1), r=r, w=w)

    def copy(self, eng, out, in_, r=(), w=()):
        if eng == 'act':
            return self.op('act', lambda e: e.activation(out, in_, AF.Copy), r=r, w=w)
        return self.op(eng, lambda e: e.tensor_copy(out, in_), r=r, w=w)

    def mm(self, out, lhsT, rhs, start, stop, r=(), w=()):
        return self.op('pe', lambda e: e.matmul(out, lhsT, rhs, start=start, stop=stop), r=r, w=w)

    def memset(self, eng, ap, val, w=()):
        return self.op(eng, lambda e: e.memset(ap, val), w=w)

    def finish(self, final_tiles=()):
        nc = self.nc
        fin = Ins('sp', None, len(self.streams['sp']), False)
        fdeps = []
        for e in ENGS:
            fdeps.extend(self.dma_hist[e][-DMA_R:])
            if e != 'sp' and self.streams[e]:
                fdeps.append(self.streams[e][-1])
        self._resolve(fin, fdeps)
        self.streams['sp'].append(fin)
        nmile = {}
        for e in ENGS:
            m = 0
            for ins in self.streams[e]:
                if ins.mile:
                    m += 1
                    ins.mnum = m
            nmile[e] = m
        sem_cms = []

        def newsem(name):
            cm = nc.semaphore(name)
            sem_cms.append(cm)
            return cm.__enter__()

        esems = {e: [newsem(f's_{e}_{k}') for k in range((nmile[e] + EPOCH - 1) // EPOCH)] for e in ENGS}
        dkeys = set()
        for e in ENGS:
            for d in self.dma_hist[e]:
                dkeys.add(d.dsem)
        dsems = {k: newsem(f'd_{k[0]}_{k[1]}_{k[2]}') for k in sorted(dkeys)}
        self.n_sems = len(sem_cms)

        fin.blk = self.blk
        pos = {e: 0 for e in ENGS}

        def emit(e, eh, blk):
            st = self.streams[e]
            i = pos[e]
            while i < len(st) and st[i].blk == blk:
                ins = st[i]
                i += 1
                for d in ins.waits:
                    if d.dma:
                        eh.wait_ge(dsems[d.dsem], d.dval)
                    else:
                        m = d.mnum - 1
                        eh.wait_ge(esems[d.eng][m // EPOCH], m % EPOCH + 1)
                if ins.fn is None:
                    continue
                bi = ins.fn(eh)
                if ins.dma:
                    bi.then_inc(dsems[ins.dsem], 16)
                elif ins.mile:
                    m = ins.mnum - 1
                    bi.then_inc(esems[e][m // EPOCH], 1)
            pos[e] = i

        for blk in range(self.blk + 1):
            with nc.Block() as block:
                @block.tensor
                def _(eh):
                    emit('pe', eh, blk)

                @block.scalar
                def _(eh):
                    emit('act', eh, blk)

                @block.vector
                def _(eh):
                    emit('dve', eh, blk)

                @block.gpsimd
                def _(eh):
                    emit('pool', eh, blk)

                @block.sync
                def _(eh):
                    emit('sp', eh, blk)
        for cm in reversed(sem_cms):
            cm.__exit__(None, None, None)
        for cm in reversed(self._psum_cms):
            cm.__exit__(None, None, None)
        self._arena_cm.__exit__(None, None, None)
        return {e: len(self.streams[e]) for e in ENGS}, nmile


D = 2048
KC = 16
FF = 5632
FC = 44
EPS = 1e-6
NMIX = 3


class Cfg:
    def __init__(self, SEQ=2048, CTX=256, NSEQ=2, layers=(0, 1, 2, 3), depth=4, fblk=512, lbdepth=4):
        self.lbdepth = lbdepth
        self.SEQ, self.CTX, self.NSEQ = SEQ, CTX, NSEQ
        self.T = SEQ + CTX
        self.NJ = NSEQ + 1
        self.layers = tuple(layers)
        self.depth = depth
        self.fblk = fblk
        self.NCH = self.T // 64
        self.NCB = CTX // 128
        self.NLB = SEQ // 128


def segs(t0, t1, CTX, maxn=512):
    out = []
    pieces = []
    if t0 < CTX:
        pieces.append((t0, min(t1, CTX), True))
    if t1 > CTX:
        pieces.append((max(t0, CTX), t1, False))
    for a, b, isctx in pieces:
        n = b - a
        k = -(-n // maxn)
        sz = -(-n // k)
        p = a
        while p < b:
            e = min(b, p + sz)
            out.append((p, e - p, isctx))
            p = e
    return out


class Builder:
    def __init__(self, cfg):
        self.cfg = cfg
        nc = bass.Bass("TRN2", target_bir_lowering=False)
        self.nc = nc
        self.P = Prog(nc)
        c = cfg
        T, NJ = c.T, c.NJ

        def inp(name, shape, dt=F32):
            return nc.dram_tensor(name, list(shape), dt, kind="ExternalInput").ap()

        self.xin = inp("xin", [c.NSEQ, D, T])
        self.cvec = inp("cvec", [128, KC, NJ])
        self.ada_w = inp("ada_w", [c.depth, D, 6 * D])
        self.ada_bT = inp("ada_bT", [c.depth, 128, 96])
        self.norm_gT = inp("norm_gT", [c.depth, 4, 128, KC])
        self.ffn_w_in = inp("ffn_w_in", [c.depth, D, 2 * FF])
        self.ffn_w_out = inp("ffn_w_out", [c.depth, FF, D])
        na = len(range(0, c.depth, NMIX))
        nb = len(range(1, c.depth, NMIX))
        ncx = len(range(2, c.depth, NMIX))
        self.attn_w_in = inp("attn_w_in", [na, D, 2560])
        self.attn_w_rot = inp("attn_w_rot", [na, D, 2304])
        self.attn_w_out = inp("attn_w_out", [na, D, D])
        self.attn_sink = inp("attn_sink", [na, 32])
        self.ssd_w_in = inp("ssd_w_in", [max(nb, 1), D, 10368])
        self.ssd_conv_wT = inp("ssd_conv_wT", [max(nb, 1), 128, 48, 3])
        self.ssd_conv_bT = inp("ssd_conv_bT", [max(nb, 1), 128, 48])
        self.ssd_dt_bias = inp("ssd_dt_bias", [max(nb, 1), 128])
        self.ssd_a_log = inp("ssd_a_log", [max(nb, 1), 128])
        self.ssd_d = inp("ssd_d", [max(nb, 1), 64])
        self.ssd_norm_w = inp("ssd_norm_w", [max(nb, 1), 4096])
        self.ssd_w_out = inp("ssd_w_out", [max(nb, 1), 4096, D])
        self.hgrn_w_in = inp("hgrn_w_in", [max(ncx, 1), D, 10240])
        self.hgrn_lb = inp("hgrn_lb", [2, c.lbdepth, D])
        self.hgrn_norm_w = inp("hgrn_norm_w", [max(ncx, 1), D])
        self.hgrn_w_out = inp("hgrn_w_out", [max(ncx, 1), D, D])
        self.ropeC = inp("ropeC", [128, T])
        self.ropeS = inp("ropeS", [128, T])
        self.c_ident = inp("c_ident", [128, 128])
        self.c_maskp = inp("c_maskp", [128, 128])
        self.c_maskn = inp("c_maskn", [128, 128])
        self.c_tri = inp("c_tri", [64, 4, 64])
        self.xout = nc.dram_tensor("xout", [c.NSEQ, D, T], F32, kind="ExternalOutput").ap()
        self.X = nc.dram_tensor("Xs", [c.NSEQ, D, T], F32).ap()
        self.Y = nc.dram_tensor("Ys", [c.NSEQ, D, T], F32).ap()
        self.QT = nc.dram_tensor("QTs", [D, T], BF16).ap()
        self.OT = nc.dram_tensor("OTs", [4096, T], BF16).ap()
        self.XC = nc.dram_tensor("XCs", [6144, T], BF16).ap()
        self.DT = nc.dram_tensor("DTs", [T, 128], F32).ap()
        self.ZS = nc.dram_tensor("ZSs", [T, 4096], BF16).ap()
        self.YB = nc.dram_tensor("YBs", [T, 512], F32).ap()
        self.tX = [Tile(self.X[s]) for s in range(c.NSEQ)]
        self.tY = [Tile(self.Y[s]) for s in range(c.NSEQ)]
        self.tXin = Tile(self.xin)
        self.tXout = Tile(self.xout)
        self.tQT, self.tOT, self.tXC = Tile(self.QT), Tile(self.OT), Tile(self.XC)
        self.tDT, self.tZS, self.tYB = Tile(self.DT), Tile(self.ZS), Tile(self.YB)
        self.Wbin = nc.dram_tensor("Wbin_s", [D, 2 * FF], BF16).ap()
        self.Wbout = nc.dram_tensor("Wbout_s", [FF, D], BF16).ap()
        self.tWbin, self.tWbout = Tile(self.Wbin), Tile(self.Wbout)
        self.wcount = 0
        self.dbg = {}
        self.dbg_on = False
        self.setup_consts()

    def setup_consts(self):
        P, c = self.P, self.cfg
        self.ident = P.alloc([128, 128], BF16)
        P.dma('pool', self.ident[:, :], self.c_ident, w=[self.ident])
        self.ones = P.alloc([128, 128], BF16)
        P.memset('dve', self.ones[:, :], 1.0, w=[self.ones])
        self.identf = P.alloc([128, 128], F32)
        P.dma('sp', self.identf[:, :], self.c_ident, w=[self.identf])
        self.onesf = P.alloc([128, 128], F32)
        P.memset('dve', self.onesf[:, :], 1.0, w=[self.onesf])
        self.trif = P.alloc([64, 4, 64], F32)
        P.dma('sp', self.trif[:, :, :], self.c_tri, w=[self.trif])
        self.trib = P.alloc([64, 4, 64], BF16)
        P.dma('pool', self.trib[:, :, :], self.c_tri, w=[self.trib])
        cv = P.alloc([128, KC, c.NJ], F32)
        P.dma('sp', cv[:, :, :], self.cvec, w=[cv])
        self.csil = P.alloc([128, KC, c.NJ], BF16)
        P.act(self.csil[:, :, :], cv[:, :, :], AF.Silu, r=[cv], w=[self.csil])
        self.modT = P.alloc([128, 6, KC, c.NJ], F32)
        self.gT = P.alloc([128, 4, KC], F32)
        self.A1 = P.alloc([128, KC, c.NJ], F32)
        self.G1 = P.alloc([128, KC, c.NJ], F32)
        self.A2 = P.alloc([128, KC, c.NJ], F32)
        self.G2 = P.alloc([128, KC, c.NJ], F32)
        self.adab = P.alloc([128, 96], F32)
        self.wring = []

    def dump(self, name, tile, ap, shape, dt=F32):
        if not self.dbg_on:
            return
        d = self.nc.dram_tensor("dbg_" + name, list(shape), dt, kind="ExternalOutput").ap()
        t = Tile(d)
        self.dbg[name] = t
        self.P.dma('pool' if dt != tile.ap.dtype else 'sp', d, ap, r=[tile], w=[t])

    def alloc_wring(self, n=3):
        self.wring = [self.P.alloc([128, 8192], BF16) for _ in range(n)]

    def wtile(self):
        t = self.wring[self.wcount % len(self.wring)]
        self.wcount += 1
        return t

    def ffn_prepare(self, li):
        P = self.P
        src = self.ffn_w_in[li].rearrange('(a p) m -> p a m', p=128)
        dst = self.Wbin.rearrange('(a p) m -> p a m', p=128)
        for a in range(KC):
            for hh in range(2):
                P.dma('pool', dst[:, a, hh * FF:(hh + 1) * FF], src[:, a, hh * FF:(hh + 1) * FF], w=[self.tWbin.sub(a * 2 + hh)])
        src = self.ffn_w_out[li].rearrange('(a p) m -> p a m', p=128)
        dst = self.Wbout.rearrange('(a p) m -> p a m', p=128)
        for a in range(FC):
            P.dma('pool', dst[:, a, :], src[:, a, :], w=[self.tWbout.sub(a)])

    def load_w(self, Wd, kchunks, c0, ncols, wt=None, eng='pool', rt=()):
        P = self.P
        if wt is None:
            wt = self.wtile()
        assert kchunks * ncols <= 8192
        v = wt[:, 0:kchunks * ncols].rearrange('p (k m) -> p k m', m=ncols)
        src = Wd.rearrange('(k p) m -> p k m', p=128)
        step = max(1, 2048 // ncols)
        for k0 in range(0, kchunks, step):
            k1 = min(kchunks, k0 + step)
            P.dma(eng, v[:, k0:k1, :], src[:, k0:k1, c0:c0 + ncols], r=list(rt), w=[wt])
        return wt, v

    def adaln(self, li):
        P, c = self.P, self.cfg
        NJ = c.NJ
        mk = P.mark()
        self.alloc_wring(3)
        P.dma('sp', self.adab[:, :], self.ada_bT[li], w=[self.adab])
        P.dma('sp', self.gT[:, :, :], self.norm_gT[li].rearrange('j p k -> p j k'), w=[self.gT])
        W = self.ada_w[li]
        n = 0
        for g in range(6 * D // 512):
            wt, v = self.load_w(W, KC, g * 512, 512)
            bank = P.banks[4 + (n % 2)]
            n += 1
            for mc in range(4):
                for kc in range(KC):
                    P.mm(bank[:, mc * NJ:(mc + 1) * NJ], v[:, kc, mc * 128:(mc + 1) * 128], self.csil[:, kc, :],
                         kc == 0, kc == KC - 1, r=[wt, self.csil], w=[bank])
            ch0 = g * 4
            kind, chunk = ch0 // KC, ch0 % KC
            P.tt('dve', self.modT[:, kind, chunk:chunk + 4, :],
                 bank[:, 0:4 * NJ].rearrange('p (m j) -> p m j', j=NJ),
                 self.adab[:, ch0:ch0 + 4].unsqueeze(2).broadcast_to([128, 4, NJ]), ALU.add,
                 r=[bank, self.adab], w=[self.modT])
        m = self.modT

        def gb(j):
            return self.gT[:, j, :].unsqueeze(2).broadcast_to([128, KC, NJ])
        P.stt('dve', self.A1[:, :, :], m[:, 1, :, :], 1.0, gb(0), ALU.add, ALU.mult, r=[m, self.gT], w=[self.A1])
        P.tt('dve', self.G1[:, :, :], m[:, 2, :, :], gb(1), ALU.mult, r=[m, self.gT], w=[self.G1])
        P.stt('dve', self.A2[:, :, :], m[:, 4, :, :], 1.0, gb(2), ALU.add, ALU.mult, r=[m, self.gT], w=[self.A2])
        P.tt('dve', self.G2[:, :, :], m[:, 5, :, :], gb(3), ALU.mult, r=[m, self.gT], w=[self.G2])
        P.release(mk)

    def rstd_of(self, src, srct, n, out_rstd, sqring, bank, nfeat=D, kchunks=KC):
        P = self.P
        for kc in range(kchunks):
            sq = sqring[kc % len(sqring)]
            P.act(sq[:, 0:n], src[:, kc, :], AF.Square, r=[srct], w=[sq])
            P.mm(bank[:, 0:n], self.ones[:, :], sq[:, 0:n], kc == 0, kc == kchunks - 1, r=[sq, self.ones], w=[bank])
        P.act(out_rstd[:, 0:n], bank[:, 0:n], AF.Ln, r=[bank], w=[out_rstd], bias=EPS, scale=1.0 / nfeat)
        P.act(out_rstd[:, 0:n], out_rstd[:, 0:n], AF.Exp, r=[out_rstd], w=[out_rstd], scale=-0.5)

    def modulate(self, xt, xv, n, rstd, A, Bm, j, uout, uoutt, tmp):
        P = self.P
        P.tt('dve', tmp[:, :, 0:n], xv, rstd[:, 0:n].unsqueeze(1).broadcast_to([128, KC, n]), ALU.mult,
             r=[xt, rstd], w=[tmp])
        for kc in range(KC):
            if kc % 2 == 0:
                P.act(uout[:, kc, :], tmp[:, kc, 0:n], AF.Identity, r=[tmp, A, Bm], w=[uoutt],
                      bias=Bm[:, kc, j:j + 1], scale=A[:, kc, j:j + 1])
            else:
                P.ts('pool', uout[:, kc, :], tmp[:, kc, 0:n], A[:, kc, j:j + 1], Bm[:, kc, j:j + 1], ALU.mult, ALU.add,
                     r=[tmp, A, Bm], w=[uoutt])

    def prenorm_seq(self, s, first, uT):
        P, c = self.P, self.cfg
        mk = P.mark()
        xb = [P.alloc([128, KC, 256], F32) for _ in range(2)]
        sqr = [P.alloc([128, 512], BF16) for _ in range(3)]
        rstd = [P.alloc([128, 256], F32) for _ in range(2)]
        src = self.xin[s] if first else self.X[s]
        srct = self.tXin if first else self.tX[s]
        for i, (t0, n, isctx) in enumerate(segs(0, c.T, c.CTX, 256)):
            x = xb[i % 2]
            j = c.NSEQ if isctx else s
            P.dma('sp', x[:, :, 0:n], src.rearrange('(k p) t -> p k t', p=128)[:, :, t0:t0 + n], r=[srct], w=[x])
            if first:
                P.dma('sp', self.X[s].rearrange('(k p) t -> p k t', p=128)[:, :, t0:t0 + n], x[:, :, 0:n], r=[x], w=[self.tX[s]])
            r = rstd[i % 2]
            self.rstd_of(x[:, :, 0:n], x, n, r, sqr, P.banks[6 + (i % 2)])
            self.modulate(x, x[:, :, 0:n], n, r, self.A1, _Kind(self.modT, 0), j, uT[:, :, t0:t0 + n], uT, x)
        P.release(mk)

    def ffn_phase(self, li, s, last):
        P, c = self.P, self.cfg
        Xd = self.X[s].rearrange('(k p) t -> p k t', p=128)
        Yd = self.Y[s].rearrange('(k p) t -> p k t', p=128)
        Od = self.xout[s].rearrange('(k p) t -> p k t', p=128)
        Win = self.ffn_w_in[li]
        Wout = self.ffn_w_out[li]
        B2 = _Kind(self.modT, 3)
        mk0 = P.mark()
        self.alloc_wring(3)
        for t0 in range(0, c.T, c.fblk):
            t1 = min(c.T, t0 + c.fblk)
            nb = t1 - t0
            sg = segs(t0, t1, c.CTX, 512)
            sgm = segs(t0, t1, 0, 512)
            mk = P.mark()
            x = P.alloc([128, KC, nb], F32)
            y = P.alloc([128, KC, nb], BF16)
            u = P.alloc([128, KC, nb], BF16)
            sqr = [P.alloc([128, 512], BF16) for _ in range(3)]
            rs = [P.alloc([128, 512], F32) for _ in range(2)]
            mk2 = P.mark()
            ym = P.alloc([128, KC, nb], F32)
            for i, (p, n, isctx) in enumerate(sg):
                j = c.NSEQ if isctx else s
                o = p - t0
                P.dma('sp', x[:, :, o:o + n], Xd[:, :, p:p + n], r=[self.tX[s]], w=[x])
                P.dma('sp', ym[:, :, o:o + n], Yd[:, :, p:p + n], r=[self.tY[s]], w=[ym])
                r0 = rs[0]
                self.rstd_of(ym[:, :, o:o + n], ym, n, r0, sqr, P.banks[6])
                P.tt('dve', ym[:, :, o:o + n], ym[:, :, o:o + n], r0[:, 0:n].unsqueeze(1).broadcast_to([128, KC, n]), ALU.mult,
                     r=[ym, r0], w=[ym])
                for kc in range(KC):
                    P.stt('dve' if kc % 2 else 'pool', x[:, kc, o:o + n], ym[:, kc, o:o + n], self.G1[:, kc, j:j + 1], x[:, kc, o:o + n],
                          ALU.mult, ALU.add, r=[ym, self.G1, x], w=[x])
                P.dma('sp', Xd[:, :, p:p + n], x[:, :, o:o + n], r=[x], w=[self.tX[s]])
                r1 = rs[1]
                self.rstd_of(x[:, :, o:o + n], x, n, r1, sqr, P.banks[7])
                P.tt('dve', ym[:, :, o:o + n], x[:, :, o:o + n], r1[:, 0:n].unsqueeze(1).broadcast_to([128, KC, n]), ALU.mult,
                     r=[x, r1], w=[ym])
                for kc in range(KC):
                    if kc % 2 == 0:
                        P.act(u[:, kc, o:o + n], ym[:, kc, o:o + n], AF.Identity, r=[ym, self.A2, B2], w=[u],
                              bias=B2[:, kc, j:j + 1], scale=self.A2[:, kc, j:j + 1])
                    else:
                        P.ts('pool', u[:, kc, o:o + n], ym[:, kc, o:o + n], self.A2[:, kc, j:j + 1], B2[:, kc, j:j + 1],
                             ALU.mult, ALU.add, r=[ym, self.A2, B2], w=[u])
            P.release(mk2)
            h = P.alloc([128, FC, nb], BF16)
            sgt = [P.alloc([128, 512], F32) for _ in range(2)]
            nbk = 0
            for m0 in range(0, FC, 2):
                wt = self.wtile()
                v = wt[:, 0:KC * 512].rearrange('p (k m) -> p k m', m=512)
                srcw = self.Wbin.rearrange('(k p) m -> p k m', p=128)
                for k0 in range(0, KC, 8):
                    P.dma('sp', v[:, k0:k0 + 8, 0:256], srcw[:, k0:k0 + 8, m0 * 128:m0 * 128 + 256], r=[self.tWbin], w=[wt])
                    P.dma('sp', v[:, k0:k0 + 8, 256:512], srcw[:, k0:k0 + 8, FF + m0 * 128:FF + m0 * 128 + 256], r=[self.tWbin], w=[wt])
                for mm_ in range(2):
                    m = m0 + mm_
                    for (p, n, isctx) in sgm:
                        o = p - t0
                        bg = P.banks[(nbk % 2) * 2]
                        bu = P.banks[(nbk % 2) * 2 + 1]
                        nbk += 1
                        for kc in range(KC):
                            P.mm(bg[:, 0:n], v[:, kc, mm_ * 128:(mm_ + 1) * 128], u[:, kc, o:o + n], kc == 0, kc == KC - 1,
                                 r=[wt, u], w=[bg])
                        for kc in range(KC):
                            P.mm(bu[:, 0:n], v[:, kc, 256 + mm_ * 128:256 + (mm_ + 1) * 128], u[:, kc, o:o + n], kc == 0, kc == KC - 1,
                                 r=[wt, u], w=[bu])
                        st_ = sgt[nbk % 2]
                        P.act(st_[:, 0:n], bg[:, 0:n], AF.Silu, r=[bg], w=[st_])
                        P.tt('dve', h[:, m, o:o + n], st_[:, 0:n], bu[:, 0:n], ALU.mult, r=[st_, bu], w=[h])
            for mo in range(KC):
                wt, v = self.load_w(self.Wbout, FC, mo * 128, 128, eng='sp', rt=[self.tWbout])
                for (p, n, isctx) in sgm:
                    o = p - t0
                    bk = P.banks[nbk % 4]
                    nbk += 1
                    for kc in range(FC):
                        P.mm(bk[:, 0:n], v[:, kc, :], h[:, kc, o:o + n], kc == 0, kc == FC - 1, r=[wt, h], w=[bk])
                    P.copy('act', y[:, mo, o:o + n], bk[:, 0:n], r=[bk], w=[y])
            P.release(mk2)
            tmp = P.alloc([128, KC, 512], F32)
            for i, (p, n, isctx) in enumerate(sg):
                j = c.NSEQ if isctx else s
                o = p - t0
                r0 = rs[0]
                self.rstd_of(y[:, :, o:o + n], y, n, r0, sqr, P.banks[6])
                P.tt('dve', tmp[:, :, 0:n], y[:, :, o:o + n], r0[:, 0:n].unsqueeze(1).broadcast_to([128, KC, n]), ALU.mult,
                     r=[y, r0], w=[tmp])
                for kc in range(KC):
                    P.stt('dve' if kc % 2 else 'pool', x[:, kc, o:o + n], tmp[:, kc, 0:n], self.G2[:, kc, j:j + 1], x[:, kc, o:o + n],
                          ALU.mult, ALU.add, r=[tmp, self.G2, x], w=[x])
                if last:
                    P.dma('sp', Od[:, :, p:p + n], x[:, :, o:o + n], r=[x], w=[self.tXout])
                else:
                    P.dma('sp', Xd[:, :, p:p + n], x[:, :, o:o + n], r=[x], w=[self.tX[s]])
            P.release(mk)
        P.release(mk0)


    def outproj(self, Wd, kch, src, srct, col0, tok0, ntok, s):
        P = self.P
        gcols = min(512, (8192 // kch) // 128 * 128)
        mk = P.mark()
        ys = [P.alloc([128, 512], F32) for _ in range(3)]
        nb = 0
        sg = segs(0, ntok, 0, 512)
        for m0 in range(0, D, gcols):
            wt, v = self.load_w(Wd, kch, m0, gcols)
            for mc in range(gcols // 128):
                for (p, n, _) in sg:
                    bk = P.banks[nb % 4]
                    y = ys[nb % 3]
                    nb += 1
                    for kc in range(kch):
                        P.mm(bk[:, 0:n], v[:, kc, mc * 128:(mc + 1) * 128], src[:, kc, col0 + p:col0 + p + n], kc == 0, kc == kch - 1,
                             r=[wt, srct], w=[bk])
                    P.copy('act' if nb % 2 else 'dve', y[:, 0:n], bk[:, 0:n], r=[bk], w=[y])
                    P.dma('sp', self.Y[s][m0 + mc * 128:m0 + (mc + 1) * 128, tok0 + p:tok0 + p + n], y[:, 0:n], r=[y], w=[self.tY[s]])
        P.release(mk)

    def attn_phase(self, li, ja, s, first, _x):
        P, c = self.P, self.cfg
        T, NCB = c.T, c.NCB
        NTT = T // 128
        Win, Wrot, Wout = self.attn_w_in[ja], self.attn_w_rot[ja], self.attn_w_out[ja]
        mk = P.mark()
        kT = P.alloc([128, 4, T], BF16)
        vt = P.alloc([128, NTT, 4 * 65], BF16)
        es = P.alloc([128, 32], F32)
        P.dma('sp', es[:, :], self.attn_sink[ja:ja + 1, :].broadcast_to([128, 32]), w=[es])
        P.act(es[:, :], es[:, :], AF.Exp, r=[es], w=[es])
        P.memset('dve', vt[:, :, :], 1.0, w=[vt])
        mk1 = P.mark()
        self.alloc_wring(3)
        uT = P.alloc([128, KC, T], BF16)
        self.prenorm_seq(s, first, uT)
        rC = P.alloc([128, T], F32)
        rS = P.alloc([128, T], F32)
        P.dma('sp', rC[:, :], self.ropeC, w=[rC])
        P.dma('sp', rS[:, :], self.ropeS, w=[rS])
        sg = segs(0, T, 0, 512)
        t1r = [P.alloc([128, 512], F32) for _ in range(2)]
        t2r = [P.alloc([128, 512], F32) for _ in range(2)]
        qs = [P.alloc([128, T], BF16) for _ in range(2)]
        nb = 0
        srcW = Win.rearrange('(k p) m -> p k m', p=128)
        srcR = Wrot.rearrange('(k p) m -> p k m', p=128)

        def rope_proj(v, wt, dst_ap_fn, dstt):
            nonlocal nb
            for (p, n, _) in sg:
                ba = P.banks[(nb % 2) * 2]
                bb = P.banks[(nb % 2) * 2 + 1]
                t1 = t1r[nb % 2]
                t2 = t2r[nb % 2]
                nb += 1
                for kc in range(KC):
                    P.mm(ba[:, 0:n], v[:, kc, 0:128], uT[:, kc, p:p + n], kc == 0, kc == KC - 1, r=[wt, uT], w=[ba])
                for kc in range(KC):
                    P.mm(bb[:, 0:n], v[:, kc, 128:256], uT[:, kc, p:p + n], kc == 0, kc == KC - 1, r=[wt, uT], w=[bb])
                P.tt('dve', t1[:, 0:n], ba[:, 0:n], rC[:, p:p + n], ALU.mult, r=[ba, rC], w=[t1])
                P.tt('dve', t2[:, 0:n], bb[:, 0:n], rS[:, p:p + n], ALU.mult, r=[bb, rS], w=[t2])
                P.tt('pool', dst_ap_fn(p, n), t1[:, 0:n], t2[:, 0:n], ALU.add, r=[t1, t2], w=[dstt])

        for kv in range(4):
            wt = self.wtile()
            v = wt[:, 0:KC * 256].rearrange('p (k m) -> p k m', m=256)
            for k0 in range(0, KC, 8):
                for hh in range(2):
                    P.dma('pool', v[:, k0:k0 + 8, hh * 64:(hh + 1) * 64], srcW[:, k0:k0 + 8, kv * 64:(kv + 1) * 64], w=[wt])
                    P.dma('pool', v[:, k0:k0 + 8, 128 + hh * 64:128 + (hh + 1) * 64], srcR[:, k0:k0 + 8, kv * 64:(kv + 1) * 64], w=[wt])
            rope_proj(v, wt, lambda p, n, kv=kv: kT[:, kv, p:p + n], kT)
        for ch in range(KC):
            wt = self.wtile()
            v = wt[:, 0:KC * 256].rearrange('p (k m) -> p k m', m=256)
            for k0 in range(0, KC, 8):
                P.dma('pool', v[:, k0:k0 + 8, 0:128], srcW[:, k0:k0 + 8, 512 + ch * 128:512 + (ch + 1) * 128], w=[wt])
                P.dma('pool', v[:, k0:k0 + 8, 128:256], srcR[:, k0:k0 + 8, 256 + ch * 128:256 + (ch + 1) * 128], w=[wt])
            q = qs[ch % 2]
            rope_proj(v, wt, lambda p, n, q=q: q[:, p:p + n], q)
            P.dma('sp', self.QT[ch * 128:(ch + 1) * 128, :], q[:, :], r=[q], w=[self.tQT])
        wt, v = self.load_w(Win, KC, 256, 256)
        for tt in range(NTT):
            bk = P.banks[tt % 4]
            for kc in range(KC):
                P.mm(bk[:, 0:256], uT[:, kc, tt * 128:(tt + 1) * 128], v[:, kc, :], kc == 0, kc == KC - 1, r=[wt, uT], w=[bk])
            P.copy('act' if tt % 2 else 'dve', vt[:, tt, :].rearrange('p (a b) -> p a b', b=65)[:, :, 0:64],
                   bk[:, 0:256].rearrange('p (a b) -> p a b', b=64), r=[bk], w=[vt])
        self.dump('kT', kT, kT[:, :, :], [128, 4, T])
        self.dump('vt', vt, vt[:, :, :], [128, NTT, 260])
        self.dump('uT', uT, uT[:, :, :], [128, KC, T])
        P.release(mk1)
        oT = P.alloc([128, KC, T], BF16)
        mkb = P.mark()
        qg = [P.alloc([128, 4, T], BF16) for _ in range(2)]
        pring = [P.alloc([128, 512], BF16) for _ in range(12)]
        otm = [P.alloc([128, 4, 128], BF16) for _ in range(2)]
        den = [P.alloc([128, 4], F32) for _ in range(2)]
        mp = P.alloc([128, 128], BF16)
        mn = P.alloc([128, 128], BF16)
        P.dma('pool', mp[:, :], self.c_maskp, w=[mp])
        P.dma('pool', mn[:, :], self.c_maskn, w=[mn])
        npi = 0
        nsb = 0
        nob = 0
        for g in range(4):
            q = qg[g % 2]
            P.dma('sp', q[:, :, :], self.QT[g * 512:(g + 1) * 512, :].rearrange('(c p) t -> p c t', p=128), r=[self.tQT], w=[q])
            for i in range(NTT):
                if i < NCB:
                    keys = [(jt, None) for jt in range(NCB)]
                else:
                    keys = [(jt, None) for jt in range(NCB)]
                    if i - 1 >= NCB:
                        keys.append((i - 1, mp))
                    keys.append((i, None))
                    if i + 1 < NTT:
                        keys.append((i + 1, mn))
                ot = otm[nob % 2]
                for half in range(2):
                    hs = slice(half * 64, (half + 1) * 64)
                    pts = []
                    for (jt, msk) in keys:
                        bk = P.banks[nsb % 4]
                        nsb += 1
                        P.mm(bk[:, :], kT[hs, g, jt * 128:(jt + 1) * 128], q[hs, :, i * 128:(i + 1) * 128], True, msk is None,
                             r=[kT, q], w=[bk])
                        if msk is not None:
                            P.mm(bk[:, :], self.ident[:, :], msk[:, :].unsqueeze(1).broadcast_to([128, 4, 128]), False, True,
                                 r=[self.ident, msk], w=[bk])
                        pt = pring[npi % 12]
                        npi += 1
                        P.act(pt[:, :], bk[:, :], AF.Exp, r=[bk], w=[pt], scale=0.125)
                        pts.append((pt, jt))
                    bo = P.banks[4 + (nob * 2 + half) % 2]
                    for cc in range(4):
                        for k_i, (pt, jt) in enumerate(pts):
                            P.mm(bo[:, cc * 65:(cc + 1) * 65], pt[:, cc * 128:(cc + 1) * 128], vt[:, jt, g * 65:(g + 1) * 65],
                                 k_i == 0, k_i == len(pts) - 1, r=[pt, vt], w=[bo])
                    dn = den[half]
                    bov = bo[:, 0:260].rearrange('p (a b) -> p a b', b=65)
                    esv = es[:, g * 8:(g + 1) * 8].rearrange('p (a two) -> p a two', two=2)[:, :, half]
                    P.tt('dve', dn[:, :], bov[:, :, 64], esv, ALU.add, r=[bo, es], w=[dn])
                    P.op('dve', lambda e, dn=dn: e.reciprocal(dn[:, :], dn[:, :]), r=[dn], w=[dn])
                    P.tt('dve', ot[:, :, half * 64:(half + 1) * 64], bov[:, :, 0:64], dn[:, :].unsqueeze(2).broadcast_to([128, 4, 64]),
                         ALU.mult, r=[bo, dn], w=[ot])
                bt = P.banks[6 + nob % 2]
                nob += 1
                for cc in range(4):
                    P.mm(bt[:, cc * 128:(cc + 1) * 128], ot[:, cc, :], self.ident[:, :], True, True, r=[ot, self.ident], w=[bt])
                P.copy('act', oT[:, g * 4:(g + 1) * 4, i * 128:(i + 1) * 128], bt[:, :].rearrange('p (a b) -> p a b', b=128),
                       r=[bt], w=[oT])
        self.dump('oT', oT, oT[:, :, :], [128, KC, T])
        P.release(mkb)
        self.alloc_wring(3)
        self.outproj(Wout, KC, oT, oT, 0, 0, T, s)
        P.release(mk)

    def ssd_phase(self, li, jb, s, first, _x):
        P, c = self.P, self.cfg
        T, CTX, SEQ, NCH = c.T, c.CTX, c.SEQ, c.NCH
        NCC = CTX // 64
        Win, Wout = self.ssd_w_in[jb], self.ssd_w_out[jb]
        mk = P.mark()
        self.alloc_wring(3)
        dt_all = P.alloc([64, NCH, 128], F32)
        Abc = P.alloc([64, 128], F32)
        Dbc = P.alloc([64, 64], F32)
        dtb = P.alloc([64, 128], F32)
        P.dma('sp', Abc[:, :], self.ssd_a_log[jb:jb + 1, :].broadcast_to([64, 128]), w=[Abc])
        P.act(Abc[:, :], Abc[:, :], AF.Exp, r=[Abc], w=[Abc])
        P.ts('dve', Abc[:, :], Abc[:, :], -1.0, None, ALU.mult, r=[Abc], w=[Abc])
        P.dma('sp', Dbc[:, :], self.ssd_d[jb:jb + 1, :].broadcast_to([64, 64]), w=[Dbc])
        P.dma('sp', dtb[:, :], self.ssd_dt_bias[jb:jb + 1, :].broadcast_to([64, 128]), w=[dtb])
        mk1 = P.mark()
        uT = P.alloc([128, KC, T], BF16)
        self.prenorm_seq(s, first, uT)
        cw = P.alloc([128, 48, 3], F32)
        cb = P.alloc([128, 48], F32)
        P.dma('sp', cw[:, :, :], self.ssd_conv_wT[jb], w=[cw])
        P.dma('sp', cb[:, :], self.ssd_conv_bT[jb], w=[cb])
        pre = [P.alloc([128, T + 4], F32) for _ in range(2)]
        acc = P.alloc([128, T], F32)
        xcs = [P.alloc([128, T], BF16) for _ in range(2)]
        for pz in pre:
            P.memset('dve', pz[:, :], 0.0, w=[pz])
        sgc = segs(0, T, CTX, 512)
        nbk = 0
        for g4 in range(12):
            wt, v = self.load_w(Win, KC, g4 * 512, 512)
            for mc in range(4):
                ch = g4 * 4 + mc
                pz = pre[ch % 2]
                xc = xcs[ch % 2]
                for (p, n, isctx) in sgc:
                    bk = P.banks[nbk % 4]
                    nbk += 1
                    for kc in range(KC):
                        P.mm(bk[:, 0:n], v[:, kc, mc * 128:(mc + 1) * 128], uT[:, kc, p:p + n], kc == 0, kc == KC - 1, r=[wt, uT], w=[bk])
                    off = 1 + p if isctx else 3 + p
                    P.copy('act', pz[:, off:off + n], bk[:, 0:n], r=[bk], w=[pz])
                for (off, t0, n) in ((1, 0, CTX), (CTX + 3, CTX, SEQ)):
                    P.ts('dve', acc[:, t0:t0 + n], pz[:, off:off + n], cw[:, ch, 1:2], cb[:, ch:ch + 1], ALU.mult, ALU.add, r=[pz, cw, cb], w=[acc])
                    P.stt('dve', acc[:, t0:t0 + n], pz[:, off - 1:off - 1 + n], cw[:, ch, 0:1], acc[:, t0:t0 + n], ALU.mult, ALU.add, r=[pz, cw, acc], w=[acc])
                    P.stt('dve', acc[:, t0:t0 + n], pz[:, off + 1:off + 1 + n], cw[:, ch, 2:3], acc[:, t0:t0 + n], ALU.mult, ALU.add, r=[pz, cw, acc], w=[acc])
                P.act(xc[:, :], acc[:, :], AF.Silu, r=[acc], w=[xc])
                P.dma('sp', self.XC[ch * 128:(ch + 1) * 128, :], xc[:, :], r=[xc], w=[self.tXC.sub(ch)])
        wt, v = self.load_w(Win, KC, 6144, 128)
        et = P.alloc([64, 512], F32)
        for c0 in range(0, NCH, 4):
            nn = min(4, NCH - c0)
            bk = P.banks[nbk % 4]
            nbk += 1
            for q in range(nn):
                cc = c0 + q
                for kc in range(KC):
                    P.mm(bk[0:64, q * 128:(q + 1) * 128], uT[:, kc, cc * 64:(cc + 1) * 64], v[:, kc, :], kc == 0, kc == KC - 1, r=[wt, uT], w=[bk])
            P.tt('dve', et[:, 0:nn * 128].rearrange('p (a b) -> p a b', b=128), bk[0:64, 0:nn * 128].rearrange('p (a b) -> p a b', b=128),
                 dtb[:, :].unsqueeze(1).broadcast_to([64, nn, 128]), ALU.add, r=[bk, dtb], w=[et])
            P.act(et[:, 0:nn * 128], et[:, 0:nn * 128], AF.Exp, r=[et], w=[et])
            P.act(dt_all[:, c0:c0 + nn, :], et[:, 0:nn * 128].rearrange('p (a b) -> p a b', b=128), AF.Ln, r=[et], w=[dt_all], bias=1.0, scale=1.0)
        zst = [P.alloc([128, 512], BF16) for _ in range(3)]
        nz = 0
        for g8 in range(8):
            wt, v = self.load_w(Win, KC, 6272 + g8 * 512, 512)
            for tt in range(T // 128):
                bk = P.banks[nbk % 4]
                nbk += 1
                for kc in range(KC):
                    P.mm(bk[:, :], uT[:, kc, tt * 128:(tt + 1) * 128], v[:, kc, :], kc == 0, kc == KC - 1, r=[wt, uT], w=[bk])
                z = zst[nz % 3]
                nz += 1
                P.act(z[:, :], bk[:, :], AF.Silu, r=[bk], w=[z])
                P.dma('sp', self.ZS[tt * 128:(tt + 1) * 128, g8 * 512:(g8 + 1) * 512], z[:, :], r=[z], w=[self.tZS.sub(g8)])
        P.release(mk1)
        nw = P.alloc([64, 4096], F32)
        P.dma('sp', nw[:, :], self.ssd_norm_w[jb:jb + 1, :].broadcast_to([64, 4096]), w=[nw])
        xcg = P.alloc([128, 6, T], BF16)
        ygT = P.alloc([128, 4, T], BF16)
        a_g = P.alloc([64, NCH, 16], F32)
        dtg = P.alloc([64, NCH, 16], F32)
        eacum = P.alloc([64, NCH, 16], F32)
        dend = P.alloc([64, NCH, 16], F32)
        cdg = P.alloc([128, NCH, 16], F32)
        S = P.alloc([128, 512], F32)
        Sb = P.alloc([128, 512], BF16)
        R2 = lambda shape, dt: [P.alloc(shape, dt) for _ in range(2)]
        x_tm, B_tm, cbm = R2([64, 512], BF16), R2([64, 128], BF16), R2([64, 64], BF16)
        segr, LT, GT = R2([64, 8, 64], F32), R2([64, 8, 64], BF16), R2([64, 8, 64], BF16)
        xdt, xdtd = R2([64, 8, 64], BF16), R2([64, 8, 64], BF16)
        yo, yb, zs, tn = R2([64, 8, 64], F32), R2([64, 512], F32), R2([64, 512], BF16), R2([64, 512], BF16)
        ss, junk = R2([64, 1], F32), R2([64, 512], F32)
        trif = self.trif
        XCv = self.XC
        it = 0
        for g in range(8):
            P.dma('sp', xcg[:, 0:4, :], XCv[g * 512:(g + 1) * 512, :].rearrange('(q p) t -> p q t', p=128),
                  r=[self.tXC.sub(g * 4 + q) for q in range(4)], w=[xcg])
            P.dma('sp', xcg[:, 4, :], XCv[4096 + g * 128:4096 + (g + 1) * 128, :], r=[self.tXC.sub(32 + g)], w=[xcg])
            P.dma('sp', xcg[:, 5, :], XCv[5120 + g * 128:5120 + (g + 1) * 128, :], r=[self.tXC.sub(40 + g)], w=[xcg])
            for d in range(2):
                hsl = slice(d * 64 + g * 8, d * 64 + g * 8 + 8)
                P.tt('dve', a_g[:, :, d * 8:(d + 1) * 8], dt_all[:, :, hsl], Abc[:, hsl].unsqueeze(1).broadcast_to([64, NCH, 8]), ALU.mult,
                     r=[dt_all, Abc], w=[a_g])
                P.copy('dve', dtg[:, :, d * 8:(d + 1) * 8], dt_all[:, :, hsl], r=[dt_all], w=[dtg])
            for d in range(2):
                bk = P.banks[7]
                P.mm(bk[0:64, 0:NCH * 8], trif[:, d, :], a_g[:, :, d * 8:(d + 1) * 8], True, True, r=[trif, a_g], w=[bk])
                P.act(eacum[:, :, d * 8:(d + 1) * 8], bk[0:64, 0:NCH * 8].rearrange('p (a b) -> p a b', b=8), AF.Exp, r=[bk], w=[eacum])
                P.mm(bk[0:64, 0:NCH * 8], trif[:, 2 + d, :], a_g[:, :, d * 8:(d + 1) * 8], True, True, r=[trif, a_g], w=[bk])
                P.act(dend[:, :, d * 8:(d + 1) * 8], bk[0:64, 0:NCH * 8].rearrange('p (a b) -> p a b', b=8), AF.Exp, r=[bk], w=[dend])
            hc = (NCH + 1) // 2
            for c0 in (0, hc):
                nn = min(hc, NCH - c0)
                bk = P.banks[7]
                P.mm(bk[:, 0:nn * 16], self.onesf[0:64, :], a_g[:, c0:c0 + nn, :], True, True, r=[self.onesf, a_g], w=[bk])
                P.act(cdg[:, c0:c0 + nn, :], bk[:, 0:nn * 16].rearrange('p (a b) -> p a b', b=16), AF.Exp, r=[bk], w=[cdg])
            for d in (1, 0):
                order = list(range(NCH)) if d == 0 else (list(range(NCC - 1, -1, -1)) + list(range(NCH - 1, NCC - 1, -1)))
                P.memset('dve', S[:, :], 0.0, w=[S])
                P.memset('pool', Sb[:, :], 0.0, w=[Sb])
                dsl = slice(d * 8, (d + 1) * 8)
                for cix in order:
                    k2 = it % 2
                    it += 1
                    tok = slice(cix * 64, (cix + 1) * 64)
                    bx, bB, bseg, by, boff, bst, btr = (P.banks[i_] for i_ in range(7))
                    for q in range(4):
                        P.mm(bx[0:64, q * 128:(q + 1) * 128], xcg[:, q, tok], self.ident[:, :], True, True, r=[xcg, self.ident], w=[bx])
                    P.mm(bB[0:64, 0:128], xcg[:, 4, tok], self.ident[:, :], True, True, r=[xcg, self.ident], w=[bB])
                    P.mm(bB[0:64, 128:192], xcg[:, 4, tok], xcg[:, 5, tok], True, True, r=[xcg], w=[bB])
                    P.copy('act', x_tm[k2][:, :], bx[0:64, :], r=[bx], w=[x_tm[k2]])
                    P.copy('act', B_tm[k2][:, :], bB[0:64, 0:128], r=[bB], w=[B_tm[k2]])
                    P.tt('dve', cbm[k2][:, :], bB[0:64, 128:192], trif[:, d, :], ALU.mult, r=[bB, trif], w=[cbm[k2]])
                    P.tt('pool', segr[k2][:, :, :], a_g[:, cix, dsl].unsqueeze(2).broadcast_to([64, 8, 64]),
                         trif[:, d, :].unsqueeze(1).broadcast_to([64, 8, 64]), ALU.mult, r=[a_g, trif], w=[segr[k2]])
                    P.mm(bseg[0:64, :], trif[:, 2 + d, :], segr[k2][:, :, :], True, True, r=[trif, segr[k2]], w=[bseg])
                    P.act(LT[k2][:, :, :], bseg[0:64, :].rearrange('p (a b) -> p a b', b=64), AF.Exp, r=[bseg], w=[LT[k2]])
                    P.tt('dve', GT[k2][:, :, :], LT[k2][:, :, :], cbm[k2][:, :].unsqueeze(1).broadcast_to([64, 8, 64]), ALU.mult,
                         r=[LT[k2], cbm[k2]], w=[GT[k2]])
                    xv = x_tm[k2][:, :].rearrange('p (a b) -> p a b', b=64)
                    P.tt('pool', xdt[k2][:, :, :], xv, dtg[:, cix, dsl].unsqueeze(2).broadcast_to([64, 8, 64]), ALU.mult,
                         r=[x_tm[k2], dtg], w=[xdt[k2]])
                    P.tt('pool', xdtd[k2][:, :, :], xdt[k2][:, :, :], dend[:, cix, dsl].unsqueeze(2).broadcast_to([64, 8, 64]), ALU.mult,
                         r=[xdt[k2], dend], w=[xdtd[k2]])
                    for h in range(8):
                        P.mm(by[0:64, h * 64:(h + 1) * 64], GT[k2][:, h, :], xdt[k2][:, h, :], True, True, r=[GT[k2], xdt[k2]], w=[by])
                    P.mm(boff[0:64, :], xcg[:, 5, tok], Sb[:, :], True, True, r=[xcg, Sb], w=[boff])
                    P.tt('dve', yo[k2][:, :, :], boff[0:64, :].rearrange('p (a b) -> p a b', b=64),
                         eacum[:, cix, dsl].unsqueeze(2).broadcast_to([64, 8, 64]), ALU.mult, r=[boff, eacum], w=[yo[k2]])
                    yv = yo[k2][:, :, :].rearrange('p a b -> p (a b)')
                    P.tt('dve', yv, yv, by[0:64, :], ALU.add, r=[yo[k2], by], w=[yo[k2]])
                    P.mm(bst[:, :], B_tm[k2][:, :], xdtd[k2][:, :, :], True, True, r=[B_tm[k2], xdtd[k2]], w=[bst])
                    Sv = S[:, :].rearrange('p (a b) -> p a b', b=64)
                    P.tt('dve', Sv, Sv, cdg[:, cix, dsl].unsqueeze(2).broadcast_to([128, 8, 64]), ALU.mult, r=[S, cdg], w=[S])
                    P.tt('dve', S[:, :], S[:, :], bst[:, :], ALU.add, r=[S, bst], w=[S])
                    P.copy('act', Sb[:, :], S[:, :], r=[S], w=[Sb])
                    ybt = self.tYB.sub(cix)
                    if d == 1:
                        P.dma('sp', self.YB[tok, :], yv, r=[yo[k2]], w=[ybt])
                    else:
                        P.dma('sp', yb[k2][:, :], self.YB[tok, :], r=[ybt], w=[yb[k2]])
                        P.dma('sp', zs[k2][:, :], self.ZS[tok, g * 512:(g + 1) * 512], r=[self.tZS.sub(g)], w=[zs[k2]])
                        P.tt('dve', yv, yv, yb[k2][:, :], ALU.add, r=[yo[k2], yb[k2]], w=[yo[k2]])
                        xd = junk[k2]
                        P.tt('pool', xd[:, :].rearrange('p (a b) -> p a b', b=64), xv,
                             Dbc[:, g * 8:(g + 1) * 8].unsqueeze(2).broadcast_to([64, 8, 64]), ALU.mult, r=[x_tm[k2], Dbc], w=[xd])
                        P.tt('dve', yv, yv, xd[:, :], ALU.add, r=[yo[k2], xd], w=[yo[k2]])
                        P.tt('dve', yv, yv, zs[k2][:, :], ALU.mult, r=[yo[k2], zs[k2]], w=[yo[k2]])
                        P.memset('dve', ss[k2][:, :], 0.0, w=[ss[k2]])
                        P.act(xd[:, :], yv, AF.Square, r=[yo[k2]], w=[xd, ss[k2]], accum_out=ss[k2][:, 0:1])
                        P.act(ss[k2][:, :], ss[k2][:, :], AF.Ln, r=[ss[k2]], w=[ss[k2]], bias=EPS, scale=1.0 / 512)
                        P.act(ss[k2][:, :], ss[k2][:, :], AF.Exp, r=[ss[k2]], w=[ss[k2]], scale=-0.5)
                        P.stt('dve', tn[k2][:, :], yv, ss[k2][:, 0:1], nw[:, g * 512:(g + 1) * 512], ALU.mult, ALU.mult,
                              r=[yo[k2], ss[k2], nw], w=[tn[k2]])
                        for q in range(4):
                            P.mm(btr[:, q * 64:(q + 1) * 64], tn[k2][:, q * 128:(q + 1) * 128], self.ident[0:64, 0:64], True, True,
                                 r=[tn[k2], self.ident], w=[btr])
                        P.copy('act', ygT[:, :, tok], btr[:, 0:256].rearrange('p (a b) -> p a b', b=64), r=[btr], w=[ygT])
            P.dma('sp', self.OT[g * 512:(g + 1) * 512, :].rearrange('(q p) t -> p q t', p=128), ygT[:, :, :], r=[ygT], w=[self.tOT.sub(g)])
        P.release(mk)
        mk = P.mark()
        self.alloc_wring(3)
        nblk = 1152 if T > 1152 else T
        src = P.alloc([128, 32, nblk], BF16)
        for t0 in range(0, T, nblk):
            n = min(nblk, T - t0)
            P.dma('sp', src[:, :, 0:n], self.OT[:, t0:t0 + n].rearrange('(q p) t -> p q t', p=128), r=[self.tOT], w=[src])
            self.outproj(Wout, 32, src, src, 0, t0, n, s)
        P.release(mk)

    def hgrn_phase(self, li, jc, s, first, lbi):
        P, c = self.P, self.cfg
        T, CTX, NCH = c.T, c.CTX, c.NCH
        NCC = CTX // 64
        Win, Wout = self.hgrn_w_in[jc], self.hgrn_w_out[jc]
        depth = c.lbdepth
        mk = P.mark()
        self.alloc_wring(2)
        nwb = P.alloc([64, D], F32)
        P.dma('sp', nwb[:, :], self.hgrn_norm_w[jc:jc + 1, :].broadcast_to([64, D]), w=[nwb])
        uT = P.alloc([128, KC, T], BF16)
        self.prenorm_seq(s, first, uT)
        lg_tm = P.alloc([64, NCH, 128], F32)
        k_tm = P.alloc([64, NCH, 128], BF16)
        v_tm = P.alloc([64, NCH, 128], BF16)
        gs_tm = P.alloc([64, NCH, 128], BF16)
        qT = P.alloc([128, T], BF16)
        oTh = P.alloc([128, T], BF16)
        lbraw = P.alloc([64, 2, depth, 128], F32)
        lbt = P.alloc([64, 2, 128], F32)
        omlt = P.alloc([64, 2, 128], F32)
        lden = P.alloc([64, 2, 128], F32)
        state = P.alloc([128, 128], F32)
        state_bf = P.alloc([128, 128], BF16)
        R2 = lambda shape, dt: [P.alloc(shape, dt) for _ in range(2)]
        sig, gg = R2([64, 128], F32), R2([64, 128], F32)
        egT, engT, qgT, kgT = R2([128, 64], F32), R2([128, 64], F32), R2([128, 64], BF16), R2([128, 64], BF16)
        ket, kend, sT = R2([64, 128], F32), R2([64, 128], BF16), R2([64, 64], BF16)
        osb, obl, on, on2 = R2([64, 128], F32), R2([64, 128], F32), R2([64, 128], F32), R2([64, 128], BF16)
        ss, junk = R2([64, 1], F32), R2([64, 128], F32)
        trif = self.trif
        srcW = Win.rearrange('(k p) m -> p k m', p=128)
        sgq = segs(0, T, 0, 512)
        it = 0
        npb = 0
        for h in range(16):
            for d in range(2):
                P.dma('sp', lbraw[:, d, :, :], self.hgrn_lb[d:d + 1, :, h * 128:(h + 1) * 128].broadcast_to([64, depth, 128]), w=[lbraw])
            P.act(lbraw[:, :, :, :], lbraw[:, :, :, :], AF.Exp, r=[lbraw], w=[lbraw])
            P.copy('dve', lden[:, :, :], lbraw[:, :, 0, :], r=[lbraw], w=[lden])
            for j in range(1, depth):
                P.tt('dve', lden[:, :, :], lden[:, :, :], lbraw[:, :, j, :], ALU.add, r=[lden, lbraw], w=[lden])
            P.op('dve', lambda e: e.reciprocal(omlt[:, :, :], lden[:, :, :]), r=[lden], w=[omlt])
            P.memset('dve', lbt[:, :, :], 0.0, w=[lbt])
            for j in range(1, lbi + 1):
                P.tt('dve', lbt[:, :, :], lbt[:, :, :], lbraw[:, :, j, :], ALU.add, r=[lbt, lbraw], w=[lbt])
            P.tt('dve', lbt[:, :, :], lbt[:, :, :], omlt[:, :, :], ALU.mult, r=[lbt, omlt], w=[lbt])
            P.ts('dve', omlt[:, :, :], lbt[:, :, :], -1.0, 1.0, ALU.mult, ALU.add, r=[lbt], w=[omlt])
            wt = self.wtile()
            v = wt[:, 0:KC * 512].rearrange('p (k m) -> p k m', m=512)
            for qi, c0 in enumerate((2048 + h * 128, 4096 + h * 128, 8192 + h * 128, h * 128)):
                for k0 in range(0, KC, 8):
                    P.dma('pool', v[:, k0:k0 + 8, qi * 128:(qi + 1) * 128], srcW[:, k0:k0 + 8, c0:c0 + 128], w=[wt])
            wq, vq = self.load_w(Win, KC, 6144 + h * 128, 128)
            for (p, n, _) in sgq:
                bk = P.banks[6 + npb % 2]
                npb += 1
                for kc in range(KC):
                    P.mm(bk[:, 0:n], vq[:, kc, :], uT[:, kc, p:p + n], kc == 0, kc == KC - 1, r=[wq, uT], w=[bk])
                P.act(qT[:, p:p + n], bk[:, 0:n], AF.Silu, r=[bk], w=[qT])
            for d in (1, 0):
                c_lo, c_hi = (0, 384) if d == 1 else (384, 512)
                for cc in range(NCH):
                    tok = slice(cc * 64, (cc + 1) * 64)
                    bk = P.banks[6 + npb % 2]
                    k2 = npb % 2
                    npb += 1
                    ncol = c_hi - c_lo
                    for kc in range(KC):
                        P.mm(bk[0:64, 0:ncol], uT[:, kc, tok], v[:, kc, c_lo:c_hi], kc == 0, kc == KC - 1, r=[wt, uT], w=[bk])
                    P.act(sig[k2][:, :], bk[0:64, 0:128], AF.Sigmoid, r=[bk], w=[sig[k2]])
                    P.tt('dve', gg[k2][:, :], sig[k2][:, :], omlt[:, d, :], ALU.mult, r=[sig[k2], omlt], w=[gg[k2]])
                    P.tt('dve', gg[k2][:, :], gg[k2][:, :], lbt[:, d, :], ALU.add, r=[gg[k2], lbt], w=[gg[k2]])
                    P.act(lg_tm[:, cc, :], gg[k2][:, :], AF.Ln, r=[gg[k2]], w=[lg_tm])
                    P.ts('dve', k_tm[:, cc, :], gg[k2][:, :], -1.0, 1.0, ALU.mult, ALU.add, r=[gg[k2]], w=[k_tm])
                    if d == 1:
                        P.copy('dve', v_tm[:, cc, :], bk[0:64, 128:256], r=[bk], w=[v_tm])
                        P.act(gs_tm[:, cc, :], bk[0:64, 256:384], AF.Silu, r=[bk], w=[gs_tm])
                order = list(range(NCH)) if d == 0 else (list(range(NCC - 1, -1, -1)) + list(range(NCH - 1, NCC - 1, -1)))
                P.memset('dve', state[:, :], 0.0, w=[state])
                P.memset('pool', state_bf[:, :], 0.0, w=[state_bf])
                for cix in order:
                    k2 = it % 2
                    it += 1
                    tok = slice(cix * 64, (cix + 1) * 64)
                    bA, bB, bC, bD, bE, bF = (P.banks[i_] for i_ in range(6))
                    P.mm(bA[:, 0:64], lg_tm[:, cix, :], trif[:, d, :], True, True, r=[lg_tm, trif], w=[bA])
                    P.mm(bA[:, 64:128], k_tm[:, cix, :], self.ident[0:64, 0:64], True, True, r=[k_tm, self.ident], w=[bA])
                    P.mm(bB[0:64, 0:128], trif[:, 2 + d, :], lg_tm[:, cix, :], True, True, r=[trif, lg_tm], w=[bB])
                    P.act(egT[k2][:, :], bA[:, 0:64], AF.Exp, r=[bA], w=[egT[k2]])
                    P.act(engT[k2][:, :], bA[:, 0:64], AF.Exp, r=[bA], w=[engT[k2]], scale=-1.0)
                    P.tt('dve', qgT[k2][:, :], qT[:, tok], egT[k2][:, :], ALU.mult, r=[qT, egT[k2]], w=[qgT[k2]])
                    P.tt('dve', kgT[k2][:, :], bA[:, 64:128], engT[k2][:, :], ALU.mult, r=[bA, engT[k2]], w=[kgT[k2]])
                    P.act(ket[k2][:, :], bB[0:64, 0:128], AF.Exp, r=[bB], w=[ket[k2]])
                    P.tt('pool', kend[k2][:, :], ket[k2][:, :], k_tm[:, cix, :], ALU.mult, r=[ket[k2], k_tm], w=[kend[k2]])
                    P.mm(bC[0:64, 0:64], kgT[k2][:, :], qgT[k2][:, :], True, True, r=[kgT[k2], qgT[k2]], w=[bC])
                    P.tt('dve', sT[k2][:, :], bC[0:64, 0:64], trif[:, d, :], ALU.mult, r=[bC, trif], w=[sT[k2]])
                    P.mm(bD[0:64, 0:128], sT[k2][:, :], v_tm[:, cix, :], True, False, r=[sT[k2], v_tm], w=[bD])
                    P.mm(bD[0:64, 0:128], qgT[k2][:, :], state_bf[:, :], False, True, r=[qgT[k2], state_bf], w=[bD])
                    P.mm(bE[:, 0:128], kend[k2][:, :], v_tm[:, cix, :], True, True, r=[kend[k2], v_tm], w=[bE])
                    ecol = 63 if d == 0 else 0
                    P.stt('dve', state[:, :], state[:, :], egT[k2][:, ecol:ecol + 1], bE[:, 0:128], ALU.mult, ALU.add,
                          r=[state, egT[k2], bE], w=[state])
                    P.copy('act', state_bf[:, :], state[:, :], r=[state], w=[state_bf])
                    ybt = self.tYB.sub(cix)
                    if d == 1:
                        P.copy('act', osb[k2][:, :], bD[0:64, 0:128], r=[bD], w=[osb[k2]])
                        P.dma('sp', self.YB[tok, 0:128], osb[k2][:, :], r=[osb[k2]], w=[ybt])
                    else:
                        P.dma('sp', obl[k2][:, :], self.YB[tok, 0:128], r=[ybt], w=[obl[k2]])
                        P.tt('dve', osb[k2][:, :], bD[0:64, 0:128], obl[k2][:, :], ALU.add, r=[bD, obl[k2]], w=[osb[k2]])
                        P.memset('dve', ss[k2][:, :], 0.0, w=[ss[k2]])
                        P.act(junk[k2][:, :], osb[k2][:, :], AF.Square, r=[osb[k2]], w=[junk[k2], ss[k2]], accum_out=ss[k2][:, 0:1])
                        P.act(ss[k2][:, :], ss[k2][:, :], AF.Ln, r=[ss[k2]], w=[ss[k2]], bias=EPS, scale=1.0 / 128)
                        P.act(ss[k2][:, :], ss[k2][:, :], AF.Exp, r=[ss[k2]], w=[ss[k2]], scale=-0.5)
                        P.stt('dve', on[k2][:, :], osb[k2][:, :], ss[k2][:, 0:1], nwb[:, h * 128:(h + 1) * 128], ALU.mult, ALU.mult,
                              r=[osb[k2], ss[k2], nwb], w=[on[k2]])
                        P.tt('dve', on2[k2][:, :], on[k2][:, :], gs_tm[:, cix, :], ALU.mult, r=[on[k2], gs_tm], w=[on2[k2]])
                        P.mm(bF[:, 0:64], on2[k2][:, :], self.ident[0:64, 0:64], True, True, r=[on2[k2], self.ident], w=[bF])
                        P.copy('act', oTh[:, tok], bF[:, 0:64], r=[bF], w=[oTh])
            P.dma('sp', self.OT[h * 128:(h + 1) * 128, :], oTh[:, :], r=[oTh], w=[self.tOT.sub(h)])
        P.release(mk)
        mk = P.mark()
        self.alloc_wring(3)
        src = P.alloc([128, KC, T], BF16)
        P.dma('sp', src[:, :, :], self.OT[0:D, :].rearrange('(q p) t -> p q t', p=128), r=[self.tOT], w=[src])
        self.outproj(Wout, KC, src, src, 0, 0, T, s)
        P.release(mk)

class _Kind:
    def __init__(self, modT, kind):
        self.node = modT.node
        self.ap = modT.ap[:, kind, :, :]

    def __getitem__(self, k):
        return self.ap[k]


def rope_tables(cfg):
    SEQ, CTX, T = cfg.SEQ, cfg.CTX, cfg.T
    pos = np.arange(SEQ)
    row = (pos // 64).astype(np.float32)
    col = (pos % 64).astype(np.float32)
    inv = (10000.0 ** (-np.arange(16, dtype=np.float32) / 16)).astype(np.float32)
    ang = np.stack([row[:, None] * inv, col[:, None] * inv], axis=1)
    cs, sn = np.cos(ang).astype(np.float32), np.sin(ang).astype(np.float32)
    C = np.ones((128, T), np.float32)
    S = np.zeros((128, T), np.float32)
    for dd in range(64):
        axis, half, pair = dd // 32, (dd % 32) // 16, dd % 16
        for rep in range(2):
            C[rep * 64 + dd, CTX:] = cs[:, axis, pair]
            S[rep * 64 + dd, CTX:] = sn[:, axis, pair] * (-1.0 if half == 0 else 1.0)
    return C, S


def const_tables():
    ident = np.eye(128, dtype=np.float32)
    j = np.arange(128)[:, None]
    q = np.arange(128)[None, :]
    maskp = np.where(j >= q, 0.0, -30000.0).astype(np.float32)
    maskn = np.where(j <= q, 0.0, -30000.0).astype(np.float32)
    t = np.arange(64)[:, None]
    i = np.arange(64)[None, :]
    tri = np.stack([(t <= i), (t >= i), (t > i), (t < i)], axis=1).astype(np.float32)
    return ident, maskp, maskn, tri


def host_inputs(cfg, inp, b0, nseq):
    d = D
    f32 = np.float32
    sl = slice(b0, b0 + nseq)
    xcat = np.concatenate([inp['ctx'][sl], inp['x'][sl]], axis=1)
    xin = np.ascontiguousarray(xcat.transpose(0, 2, 1))
    cv = np.concatenate([inp['c'][sl], inp['c_ctx'][None, :]], axis=0)
    cvec = np.ascontiguousarray(cv.reshape(nseq + 1, KC, 128).transpose(2, 1, 0))
    depth = inp['ada_w'].shape[0]
    ada_bT = np.ascontiguousarray(inp['ada_b'].reshape(depth, 96, 128).transpose(0, 2, 1))
    norm_gT = np.ascontiguousarray(inp['norm_g'].reshape(depth, 4, KC, 128).transpose(0, 1, 3, 2))
    w = inp['attn_w_in']
    perm = np.arange(64).reshape(2, 2, 16)[:, ::-1, :].reshape(64)
    kperm = (np.arange(4)[:, None] * 64 + perm[None, :]).reshape(-1)
    qperm = 512 + (np.arange(32)[:, None] * 64 + perm[None, :]).reshape(-1)
    w_rot = np.ascontiguousarray(w[:, :, np.concatenate([kperm, qperm])])
    C, S = rope_tables(cfg)
    ident, maskp, maskn, tri = const_tables()
    nb = inp['ssd_conv_w'].shape[0]
    conv_wT = np.ascontiguousarray(inp['ssd_conv_w'].reshape(nb, 3, 48, 128).transpose(0, 3, 2, 1))
    conv_bT = np.ascontiguousarray(inp['ssd_conv_b'].reshape(nb, 48, 128).transpose(0, 2, 1))
    m = dict(
        xin=xin, cvec=cvec, ada_w=inp['ada_w'], ada_bT=ada_bT, norm_gT=norm_gT,
        ffn_w_in=inp['ffn_w_in'], ffn_w_out=inp['ffn_w_out'],
        attn_w_in=w, attn_w_rot=w_rot, attn_w_out=inp['attn_w_out'], attn_sink=inp['attn_sink'],
        ssd_w_in=inp['ssd_w_in'], ssd_conv_wT=conv_wT, ssd_conv_bT=conv_bT,
        ssd_dt_bias=np.ascontiguousarray(inp['ssd_dt_bias'].reshape(nb, 128)),
        ssd_a_log=np.ascontiguousarray(inp['ssd_a_log'].reshape(nb, 128)),
        ssd_d=inp['ssd_d'], ssd_norm_w=inp['ssd_norm_w'], ssd_w_out=inp['ssd_w_out'],
        hgrn_w_in=inp['hgrn_w_in'], hgrn_lb=inp['hgrn_lb'], hgrn_norm_w=inp['hgrn_norm_w'], hgrn_w_out=inp['hgrn_w_out'],
        ropeC=C, ropeS=S, c_ident=ident, c_maskp=maskp, c_maskn=maskn, c_tri=tri,
    )
    return {k: np.ascontiguousarray(np.asarray(v, dtype=f32)) for k, v in m.items()}


def build_program(cfg):
    b = Builder(cfg)
    P = b.P
    with b.nc.allow_low_precision("bf16 matmul operands, fp32 accumulation"):
        nl = len(cfg.layers)
        cnt = {0: 0, 1: 0, 2: 0}
        for n_, li in enumerate(cfg.layers):
            kind = li % NMIX
            j = cnt[kind]
            cnt[kind] += 1
            b.adaln(n_)
            b.ffn_prepare(n_)
            for s in range(cfg.NSEQ):
                first = (n_ == 0)
                if kind == 0:
                    b.attn_phase(n_, j, s, first, 0)
                elif kind == 1:
                    b.ssd_phase(n_, j, s, first, 0)
                else:
                    b.hgrn_phase(n_, j, s, first, li)
                b.ffn_phase(n_, s, n_ == nl - 1)
        stats = P.finish()
    return b, stats


_CACHE = {}
GROUPS = ((0, 1), (2, 3))


def _slice_inputs(inp, layers):
    ls = list(layers)
    out = dict(inp)
    for k in ('ada_w', 'ada_b', 'norm_g', 'ffn_w_in', 'ffn_w_out'):
        out[k] = inp[k][ls]
    for kind, keys in ((0, ('attn_w_in', 'attn_w_out', 'attn_sink')),
                       (1, ('ssd_w_in', 'ssd_conv_w', 'ssd_conv_b', 'ssd_dt_bias', 'ssd_a_log', 'ssd_d', 'ssd_norm_w', 'ssd_w_out')),
                       (2, ('hgrn_w_in', 'hgrn_norm_w', 'hgrn_w_out'))):
        js = [li // NMIX for li in ls if li % NMIX == kind]
        for k in keys:
            out[k] = inp[k][js] if js else inp[k][:1]
    return out


def kernel(**inputs):
    n_cores = 8
    inp = {k: np.asarray(v) for k, v in inputs.items()}
    B, SEQ = inp['x'].shape[0], inp['x'].shape[1]
    CTX = inp['ctx'].shape[1]
    nseq = B // n_cores
    xs = None
    for grp in GROUPS:
        cfg = Cfg(SEQ=SEQ, CTX=CTX, NSEQ=nseq, layers=grp, depth=len(grp), lbdepth=inp['hgrn_lb'].shape[1])
        key = tuple(li % NMIX if (li % NMIX) != 2 else ('h', li) for li in grp)
        if key not in _CACHE:
            _CACHE[key] = build_program(cfg)
        b, stats = _CACHE[key]
        sl = _slice_inputs(inp, grp)
        in_maps = []
        for core in range(n_cores):
            m = host_inputs(cfg, sl, core * nseq, nseq)
            if xs is not None:
                m['xin'] = xs[core]
            in_maps.append(m)
        res = run_bass_kernel_spmd(b.nc, in_maps, core_ids=list(range(n_cores)))
        xs = [np.ascontiguousarray(res.results[core]["xout"]) for core in range(n_cores)]
    outs = [np.ascontiguousarray(xs[core][:, :, CTX:].transpose(0, 2, 1)) for core in range(n_cores)]
    return np.concatenate(outs, axis=0).astype(np.float32)
```
